# Optimizing a Trainium2 kernel written in Bass

```python
import jax, jax.numpy as jnp
from jax import lax
import numpy as np

D_MODEL = 1024
BATCH = 2
SEQ = 8192
DEPTH = 1

GLA_HEADS = 4
GLA_DK = 128
GLA_DV = 256
GLA_KEY = GLA_HEADS * GLA_DK
GLA_VAL = GLA_HEADS * GLA_DV
GLA_GATE_RANK = 16
GLA_GATE_NORM = 16.0
GLA_CHUNK = 64
SSM_INNER = 2 * D_MODEL
SSM_HEADDIM = 64
SSM_HEADS = SSM_INNER // SSM_HEADDIM
SSM_GROUPS = 4
SSM_HPG = SSM_HEADS // SSM_GROUPS
SSM_STATE = 128
SSM_CONV = 4
SSM_CHUNK = 128
SSM_BC = SSM_GROUPS * SSM_STATE
SSM_XBC = SSM_INNER + 2 * SSM_BC
FFN_HIDDEN = 2816
FFN_CONV = 3
PLE_DIM = 256
EPS = 1e-6

IN_SPLITS = (GLA_KEY, GLA_KEY, GLA_VAL, GLA_VAL, GLA_GATE_RANK,
             SSM_INNER, SSM_XBC, SSM_HEADS, D_MODEL, D_MODEL)
IN_WIDTH = sum(IN_SPLITS)

kernel_name = "hybrid_gla_ssd_gated_merge_block"


def rms_norm(x, gain):
    xf = x.astype(jnp.float32)
    y = xf * lax.rsqrt(jnp.mean(xf * xf, axis=-1, keepdims=True) + EPS)
    return (y * gain.astype(jnp.float32)).astype(x.dtype)


def split_cols(a, sizes):
    idx, s = [], 0
    for n in sizes[:-1]:
        s += n
        idx.append(s)
    return jnp.split(a, idx, axis=-1)


def causal_dwconv(x, w):
    k_width, ch = w.shape
    return lax.conv_general_dilated(
        x, w.astype(x.dtype)[:, None, :], window_strides=(1,),
        padding=[(k_width - 1, 0)], dimension_numbers=('NWC', 'WIO', 'NWC'),
        feature_group_count=ch)


def gla_mixer(q, k, v, g_out, a_lr, w_gate, b_gate, head_norm):
    f32 = jnp.float32
    bsz, seq, _ = q.shape
    H, dk, dv, C = GLA_HEADS, GLA_DK, GLA_DV, GLA_CHUNK
    n = seq // C
    g_log = jax.nn.log_sigmoid((a_lr @ w_gate + b_gate).astype(f32)) / GLA_GATE_NORM
    qf = q.astype(f32).reshape(bsz, n, C, H, dk) * (dk ** -0.5)
    kf = k.astype(f32).reshape(bsz, n, C, H, dk)
    vf = v.astype(f32).reshape(bsz, n, C, H, dv)
    b = jnp.cumsum(g_log.reshape(bsz, n, C, H, dk), axis=2)
    b_last = b[:, :, -1]
    q_t = qf * jnp.exp(b)
    k_t = kf * jnp.exp(-b)
    causal = jnp.tril(jnp.ones((C, C), dtype=bool))
    att = jnp.where(causal, jnp.einsum('bnihd,bnjhd->bnhij', q_t, k_t), 0.0)
    o_intra = jnp.einsum('bnhij,bnjhv->bnihv', att, vf)
    u = jnp.einsum('bnchd,bnchv->nbhdv', kf * jnp.exp(b_last[:, :, None] - b), vf)
    decay = jnp.exp(b_last).transpose(1, 0, 2, 3)

    def step(state, inp):
        dec, u_n = inp
        return dec[..., None] * state + u_n, state

    _, s_prev = lax.scan(step, jnp.zeros((bsz, H, dk, dv), f32), (decay, u))
    o_inter = jnp.einsum('bnihd,nbhdv->bnihv', q_t, s_prev)
    o = (o_intra + o_inter).reshape(bsz, seq, H, dv)
    o = o * lax.rsqrt(jnp.mean(o * o, axis=-1, keepdims=True) + EPS) * head_norm.astype(f32)
    o = o.reshape(bsz, seq, GLA_VAL) * jax.nn.silu(g_out.astype(f32))
    return o.astype(q.dtype)


def ssd_mixer(z, xbc, dt_raw, w_conv, b_conv, dt_bias, a_log, d_skip, out_norm):
    f32 = jnp.float32
    bsz, seq, _ = xbc.shape
    G, J, P, N, L = SSM_GROUPS, SSM_HPG, SSM_HEADDIM, SSM_STATE, SSM_CHUNK
    nc = seq // L
    xbc = jax.nn.silu(causal_dwconv(xbc, w_conv) + b_conv)
    xs, bm, cm = jnp.split(xbc, [SSM_INNER, SSM_INNER + SSM_BC], axis=-1)
    dt = jax.nn.softplus(dt_raw.astype(f32) + dt_bias.astype(f32))
    a_head = -jnp.exp(a_log.astype(f32)).reshape(G, J)
    X = xs.astype(f32).reshape(bsz, nc, L, G, J, P)
    dtc = dt.reshape(bsz, nc, L, G, J)
    Bc = bm.astype(f32).reshape(bsz, nc, L, G, N)
    Cc = cm.astype(f32).reshape(bsz, nc, L, G, N)
    a_cs = jnp.cumsum(dtc * a_head, axis=2)
    Xdt = X * dtc[..., None]
    seg = a_cs[:, :, :, None] - a_cs[:, :, None]
    causal = jnp.tril(jnp.ones((L, L), dtype=bool))[:, :, None, None]
    l_dec = jnp.exp(jnp.where(causal, seg, -jnp.inf))
    scores = jnp.einsum('bclgn,bcsgn->bclsg', Cc, Bc)
    y_diag = jnp.einsum('bclsgj,bcsgjp->bclgjp', scores[..., None] * l_dec, Xdt)
    decay_states = jnp.exp(a_cs[:, :, -1:] - a_cs)
    states = jnp.einsum('bclgn,bclgjp->cbgjpn', Bc, Xdt * decay_states[..., None])
    chunk_decay = jnp.exp(a_cs[:, :, -1]).transpose(1, 0, 2, 3)

    def step(state, inp):
        dec, s_n = inp
        return dec[..., None, None] * state + s_n, state

    _, s_prev = lax.scan(step, jnp.zeros((bsz, G, J, P, N), f32), (chunk_decay, states))
    y_off = jnp.einsum('bclgn,cbgjpn->bclgjp', Cc, s_prev) * jnp.exp(a_cs)[..., None]
    y = y_diag + y_off + X * d_skip.astype(f32).reshape(G, J)[..., None]
    y = y.reshape(bsz, seq, SSM_INNER) * jax.nn.silu(z.astype(f32))
    yg = y.reshape(bsz, seq, G, SSM_INNER // G)
    yg = yg * lax.rsqrt(jnp.mean(yg * yg, axis=-1, keepdims=True) + EPS)
    y = yg.reshape(bsz, seq, SSM_INNER) * out_norm.astype(f32)
    return y.astype(z.dtype)


def setup_inputs(seed: int = 0) -> dict:
    key = jax.random.key(seed)
    ks = jax.random.split(key, 26)

    def nrm(k, shape, scale):
        return jax.random.normal(k, shape, jnp.float32) * scale

    def gain(k, width):
        return 1.0 + nrm(k, (DEPTH, width), 0.02)

    dt0 = jnp.exp(jax.random.uniform(ks[8], (DEPTH, SSM_HEADS), jnp.float32,
                                     np.log(1e-3), np.log(1e-1)))
    return {
        "x": nrm(ks[0], (BATCH, SEQ, D_MODEL), 1.0),
        "p": nrm(ks[1], (DEPTH, BATCH, SEQ, PLE_DIM), 1.0),
        "mixer_norm": gain(ks[2], D_MODEL),
        "w_in": nrm(ks[3], (DEPTH, D_MODEL, IN_WIDTH), D_MODEL ** -0.5),
        "w_gla_gate": nrm(ks[4], (DEPTH, GLA_GATE_RANK, GLA_KEY), GLA_GATE_RANK ** -0.5),
        "b_gla_gate": nrm(ks[5], (DEPTH, GLA_KEY), 0.1),
        "gla_norm": gain(ks[6], GLA_DV),
        "w_ssm_conv": nrm(ks[7], (DEPTH, SSM_CONV, SSM_XBC), SSM_CONV ** -0.5),
        "b_ssm_conv": nrm(ks[9], (DEPTH, SSM_XBC), 0.02),
        "dt_bias": dt0 + jnp.log(-jnp.expm1(-dt0)),
        "a_log": jnp.log(jax.random.uniform(ks[10], (DEPTH, SSM_HEADS), jnp.float32, 1.0, 16.0)),
        "d_skip": 1.0 + nrm(ks[11], (DEPTH, SSM_HEADS), 0.02),
        "ssm_norm": gain(ks[12], SSM_INNER),
        "w_branch_a": nrm(ks[13], (DEPTH, GLA_VAL, D_MODEL), GLA_VAL ** -0.5),
        "w_branch_b": nrm(ks[14], (DEPTH, SSM_INNER, D_MODEL), SSM_INNER ** -0.5),
        "w_out": nrm(ks[15], (DEPTH, D_MODEL, D_MODEL), D_MODEL ** -0.5),
        "ffn_norm": gain(ks[16], D_MODEL),
        "w_ffn_up": nrm(ks[17], (DEPTH, D_MODEL, 2 * FFN_HIDDEN), D_MODEL ** -0.5),
        "w_ffn_conv": nrm(ks[18], (DEPTH, FFN_CONV, FFN_HIDDEN), FFN_CONV ** -0.5),
        "b_ffn_conv": nrm(ks[19], (DEPTH, FFN_HIDDEN), 0.02),
        "w_ffn_down": nrm(ks[20], (DEPTH, FFN_HIDDEN, D_MODEL), FFN_HIDDEN ** -0.5),
        "ple_norm": gain(ks[21], D_MODEL),
        "w_ple_gate": nrm(ks[22], (DEPTH, D_MODEL, D_MODEL), D_MODEL ** -0.5),
        "w_ple_proj": nrm(ks[23], (DEPTH, PLE_DIM, D_MODEL), PLE_DIM ** -0.5),
        "final_norm": 1.0 + nrm(ks[24], (D_MODEL,), 0.02),
    }


def reference(x, p, mixer_norm, w_in, w_gla_gate, b_gla_gate, gla_norm, w_ssm_conv,
              b_ssm_conv, dt_bias, a_log, d_skip, ssm_norm, w_branch_a, w_branch_b,
              w_out, ffn_norm, w_ffn_up, w_ffn_conv, b_ffn_conv, w_ffn_down,
              ple_norm, w_ple_gate, w_ple_proj, final_norm):
    for i in range(DEPTH):
        h = rms_norm(x, mixer_norm[i])
        proj = h @ w_in[i]
        (q, k, v, g_out, a_lr, z, xbc, dt_raw, gate_a, gate_b) = split_cols(proj, IN_SPLITS)
        o_a = gla_mixer(q, k, v, g_out, a_lr, w_gla_gate[i], b_gla_gate[i], gla_norm[i])
        o_b = ssd_mixer(z, xbc, dt_raw, w_ssm_conv[i], b_ssm_conv[i], dt_bias[i],
                        a_log[i], d_skip[i], ssm_norm[i])
        merged = (jax.nn.sigmoid(gate_a) * (o_a @ w_branch_a[i])
                  + jax.nn.sigmoid(gate_b) * (o_b @ w_branch_b[i]))
        x = x + merged @ w_out[i]
        h = rms_norm(x, ffn_norm[i])
        act, lin = jnp.split(h @ w_ffn_up[i], 2, axis=-1)
        act = causal_dwconv(act, w_ffn_conv[i]) + b_ffn_conv[i]
        x = x + (jax.nn.gelu(act) * lin) @ w_ffn_down[i]
        g = jax.nn.sigmoid(rms_norm(x, ple_norm[i]) @ w_ple_gate[i])
        x = x + g * (p[i] @ w_ple_proj[i])
    return rms_norm(x, final_norm)
```

```python
import numpy as np
from contextlib import ExitStack
import concourse.bass as bass
import concourse.mybir as mybir
from concourse.bass_utils import run_bass_kernel_spmd

F32 = mybir.dt.float32
BF16 = mybir.dt.bfloat16
AF = mybir.ActivationFunctionType
ALU = mybir.AluOpType
AX = mybir.AxisListType

D = 1024
SEQ = 8192
BATCH = 2
NCORES = 8
OWN = 2048
EPS = 1e-6
C_Q, C_K, C_V, C_G, C_ALR, C_Z, C_XBC, C_DT, C_GA, C_GB = 0, 512, 1024, 2048, 3072, 3088, 5136, 8208, 8240, 9264
IN_W = 10288
FFN_H = 2816

K_ID, K_TRI, K_TRIG, K_WG, K_BG, K_FIN = 0, 128, 256, 384, 896, 1408
K_MIX, K_FFN, K_PLE, K_HN, K_SSM = 2432, 2440, 2448, 2456, 2464
K_WC, K_BC, K_WFC, K_BFC = 2480, 2576, 2600, 2666
K_DTB, K_ALOG, K_DSK, K_MASK = 2688, 2720, 2752, 2784


class Tok:
    __slots__ = ("sem", "val")

    def __init__(self, sem, val):
        self.sem = sem
        self.val = val


class Buf:
    def __init__(self, name):
        self.name = name
        self.w = None
        self.r = []


class Eng:
    def __init__(self, name, sem):
        self.name = name
        self.sem = sem
        self.count = 0
        self.known = {}
        self.prog = []


class Sched:
    def __init__(self, nc, es):
        self.nc = nc
        self.es = es
        self.eng = {}
        for n in ("pe", "act", "dve", "pool", "sp"):
            self.eng[n] = Eng(n, es.enter_context(nc.semaphore("sem_" + n)))
        self.nsem = 0
        self.dma_sems = {}

    def _need(self, e, tok, waits, skip_sem=None):
        if tok is None:
            return
        if isinstance(tok, list):
            for t in tok:
                self._need(e, t, waits, skip_sem)
            return
        if skip_sem is not None and tok.sem is skip_sem:
            return
        if tok.sem is e.sem and e.name == "pe":
            return
        k = id(tok.sem)
        if e.known.get(k, 0) < tok.val:
            e.known[k] = tok.val
            waits[k] = tok

    def _deps(self, e, reads, writes, skip_sem=None):
        waits = {}
        for b in reads:
            self._need(e, b.w, waits)
        for b in writes:
            self._need(e, b.w, waits, skip_sem)
            for t in b.r:
                self._need(e, t, waits)
        for t in waits.values():
            sem, val = t.sem, t.val
            e.prog.append(lambda h, sem=sem, val=val: h.wait_ge(sem, val))

    def _commit(self, tok, reads, writes):
        for b in reads:
            b.r.append(tok)
            if len(b.r) > 12:
                best = {}
                for t in b.r:
                    k = id(t.sem)
                    if k not in best or best[k].val < t.val:
                        best[k] = t
                b.r = list(best.values())
        for b in writes:
            b.w = tok
            b.r = []

    def op(self, en, fn, reads=(), writes=(), inc=True):
        e = self.eng[en]
        self._deps(e, reads, writes)
        if inc:
            e.count += 1
            sem = e.sem
            e.prog.append(lambda h, fn=fn, sem=sem: fn(h).then_inc(sem, 1))
            tok = Tok(e.sem, e.count)
        else:
            e.prog.append(lambda h, fn=fn: fn(h))
            tok = Tok(e.sem, e.count + 1)
        self._commit(tok, reads, writes)

    def dma(self, q, out, in_, reads, writes, key, **kw):
        e = self.eng[q]
        if key not in self.dma_sems:
            self.dma_sems[key] = [self.es.enter_context(self.nc.semaphore("dma_" + key)), 0]
        ent = self.dma_sems[key]
        self._deps(e, reads, writes, skip_sem=ent[0])
        ent[1] += 16
        sem = ent[0]
        e.prog.append(lambda h, out=out, in_=in_, sem=sem, kw=kw: h.dma_start(out=out, in_=in_, **kw).then_inc(sem, 16))
        tok = Tok(sem, ent[1])
        self._commit(tok, reads, writes)
        return tok

    def wait_tok(self, en, tok):
        e = self.eng[en]
        waits = {}
        self._need(e, tok, waits)
        for t in waits.values():
            sem, val = t.sem, t.val
            e.prog.append(lambda h, sem=sem, val=val: h.wait_ge(sem, val))

    def emit(self, block):
        def run(prog):
            def f(h):
                for p in prog:
                    p(h)
            return f
        block.sync(run(self.eng["sp"].prog))
        block.tensor(run(self.eng["pe"].prog))
        block.scalar(run(self.eng["act"].prog))
        block.vector(run(self.eng["dve"].prog))
        block.gpsimd(run(self.eng["pool"].prog))


def build_program(n_sblk, n_fblk, n_out_tiles, NT=2, debug=None):
    TB = NT * 128
    NBLK = n_sblk + n_fblk
    NTILES = NBLK * NT
    WT = NTILES * 128
    FT = n_fblk * TB
    OT = n_out_tiles * 128
    assert NTILES <= 64

    nc = bass.Bass("TRN2", target_bir_lowering=False)

    def din(name, shape, dt=F32):
        return nc.dram_tensor(name, list(shape), dt, kind="ExternalInput").ap()

    xw = din("xw", [WT, D])
    pw = din("pw", [FT, 256])
    consts_d = din("consts", [128, 2848])
    w_in = din("w_in", [D, IN_W])
    w_a = din("w_branch_a", [1024, D])
    w_b = din("w_branch_b", [2048, D])
    w_out = din("w_out", [D, D])
    w_up = din("w_ffn_up", [D, 2 * FFN_H])
    w_down = din("w_ffn_down", [FFN_H, D])
    w_pg = din("w_ple_gate", [D, D])
    w_pp = din("w_ple_proj", [256, D])
    out_d = nc.dram_tensor("out", [OT, D], F32, kind="ExternalOutput").ap()
    dbg_d = None
    if debug is not None:
        dbg_d = nc.dram_tensor("dbg", list(debug), F32, kind="ExternalOutput").ap()

    def dscr(name, shape):
        return nc.dram_tensor(name, list(shape), BF16, kind="Internal").ap()

    wsrc = {"in": w_in, "a": w_a, "b": w_b, "out": w_out, "up": w_up, "down": w_down, "pg": w_pg, "pp": w_pp}
    wbf = {k: dscr("wbf_" + k, v.shape) for k, v in wsrc.items()}

    es = ExitStack()
    with es:
        S = Sched(nc, es)
        bufs = {}

        def sb(name, shape, dt=F32):
            t = es.enter_context(nc.sbuf_tensor(name, list(shape), dt))
            bufs[name] = Buf(name)
            return t

        cst = sb("cst", [128, 2848])
        identB = sb("identB", [128, 128], BF16)
        onesF = sb("onesF", [128, 128])
        negc = sb("negc", [128, 1])
        A_bc = sb("A_bc", [128, 32])
        xs = [sb("xs%d" % i, [128, NT, D]) for i in range(2)]
        hT = sb("hT", [128, 8, TB], BF16)
        b_sb = sb("b_sb", [128, NT, 512])
        ebl = sb("ebl", [128, NT, 4])
        qt = sb("qt", [128, NT, 512], BF16)
        kt = sb("kt", [128, NT, 512], BF16)
        v_sb = sb("v_sb", [128, NT, 1024], BF16)
        sg = sb("sg", [128, NT, 1024], BF16)
        sz = sb("sz", [128, NT, 2048], BF16)
        xbcT = sb("xbcT", [128, 24, TB], BF16)
        sgab = sb("sgab", [128, 16, TB], BF16)
        oaT = sb("oaT", [128, 8, TB], BF16)
        ynT = sb("ynT", [128, 16, TB], BF16)
        mT = sb("mT", [128, 8, TB], BF16)
        pT = sb("pT", [128, 2, TB], BF16)
        p_in = sb("p_in", [128, 256])
        dtv = sb("dtv", [128, NT, 32])
        dav = sb("dav", [128, NT, 32])
        nacs = sb("nacs", [128, NT, 32])
        eacs = sb("eacs", [128, NT, 32])
        dtd = sb("dtd", [128, NT, 32])
        cdbc = sb("cdbc", [128, NT, 32])
        acs_sb = sb("acs_sb", [128, 32])
        sm_a = sb("sm_a", [128, 48])
        sm_b = sb("sm_b", [128, 32])
        alr = sb("alr", [128, 16])
        alrT = sb("alrT", [16, 128])
        tmpA = sb("tmpA", [128, 1024])
        tmpB = sb("tmpB", [128, 1024])
        tmpC = sb("tmpC", [128, 512])
        l1 = tmpC
        bufs["l1"] = bufs["tmpC"]
        tmpD = sb("tmpD", [128, 512])
        junk = sb("junk", [128, 1024], BF16)
        st4 = sb("st4", [128, 8])
        rs4 = sb("rs4", [128, 8])
        qT = sb("qT", [128, 4, 128], BF16)
        kT = sb("kT", [128, 4, 128], BF16)
        attm = sb("attm", [128, 4, 128], BF16)
        Sg = sb("Sg", [128, 4, 256])
        Sgb = sb("Sgb", [128, 4, 256], BF16)
        Xtm = sb("Xtm", [128, 2048], BF16)
        Xdt = sb("Xdt", [128, 2048], BF16)
        Xd = sb("Xd", [128, 2048], BF16)
        Btm = sb("Btm", [128, 512], BF16)
        scm = sb("scm", [128, 4, 128], BF16)
        Zg = [sb("Zg%d" % i, [128, 4, 128]) for i in range(2)]
        Eg = [sb("Eg%d" % i, [128, 4, 128]) for i in range(2)]
        MTg = [sb("MTg%d" % i, [128, 8, 128], BF16) for i in range(2)]
        yz = sb("yz", [128, 2048])
        Hs = sb("Hs", [128, 2048])
        Hsb = sb("Hsb", [128, 2048], BF16)
        pre = [sb("pre%d" % i, [128, TB + 3]) for i in range(2)]
        acc = [sb("acc0", [128, TB])] * 2
        halo_x = sb("halo_x", [128, 24, 3])
        halo_f = sb("halo_f", [128, 22, 2])
        out_sb = [sb("out_sb0", [128, D])] * 2
        NW = 4
        wpool = [sb("wp%d" % i, [128, 4096], BF16) for i in range(NW)]
        wsm = sb("wsm", [128, 8, 48], BF16)

        B = bufs
        psF = []
        for i in range(6):
            t = es.enter_context(nc.psum_tensor("psF%d" % i, [128, 512], F32))
            psF.append((t, Buf("psF%d" % i)))
        psB = []
        for i in range(2):
            t = es.enter_context(nc.psum_tensor("psB%d" % i, [128, 1024], BF16))
            psB.append((t, Buf("psB%d" % i)))
        freeF = list(range(6))
        freeB = list(range(2))

        def psum_f():
            i = freeF.pop(0)
            return i

        def free_f(i):
            freeF.append(i)

        def psum_b():
            return freeB.pop(0)

        def free_b(i):
            freeB.append(i)

        wbuf = {k: Buf("wbf_" + k) for k in wsrc}

        def mm(out, lhsT, rhs, start, stop, reads, writes, inc):
            S.op("pe", lambda h: h.matmul(out, lhsT, rhs, start=start, stop=stop), reads, writes, inc)

        def tr(out, in_, ident, reads, writes, inc):
            S.op("pe", lambda h: h.transpose(out, in_, ident), reads, writes, inc)

        def act(out, in_, func, reads, writes, bias=None, scale=None):
            kw = {}
            if bias is not None:
                kw["bias"] = bias
            if scale is not None:
                kw["scale"] = scale
            S.op("act", lambda h: h.activation(out=out, in_=in_, func=func, **kw), reads, writes)

        def tt(en, out, in0, in1, op, reads, writes):
            S.op(en, lambda h: h.tensor_tensor(out=out, in0=in0, in1=in1, op=op), reads, writes)

        def ts(en, out, in0, s1, s2, op0, op1, reads, writes):
            if op1 is None:
                S.op(en, lambda h: h.tensor_scalar(out=out, in0=in0, scalar1=s1, scalar2=None, op0=op0), reads, writes)
            else:
                S.op(en, lambda h: h.tensor_scalar(out=out, in0=in0, scalar1=s1, scalar2=s2, op0=op0, op1=op1), reads, writes)

        def stt(out, in0, scalar, in1, op0, op1, reads, writes):
            S.op("dve", lambda h: h.scalar_tensor_tensor(out=out, in0=in0, scalar=scalar, in1=in1, op0=op0, op1=op1), reads, writes)

        def cp(en, out, in_, reads, writes):
            if en == "act":
                S.op("act", lambda h: h.activation(out=out, in_=in_, func=AF.Copy), reads, writes)
            else:
                S.op(en, lambda h: h.tensor_copy(out=out, in_=in_), reads, writes)

        def ttr(out, in0, in1, accum, reads, writes):
            S.op("act", lambda h: h.activation(out=out, in_=in0, func=AF.Square, accum_out=accum), reads, writes)

        def rstd_from_ss(ss_ap, out_ap, n, width, reads_buf, out_buf):
            ts("dve", out_ap, ss_ap, 1.0 / n, EPS, ALU.mult, ALU.add, [reads_buf], [out_buf])
            act(out_ap, out_ap, AF.Sqrt, [out_buf], [out_buf])
            S.op("dve", lambda h: h.reciprocal(out=out_ap, in_=out_ap), [out_buf], [out_buf])

        cF = lambda a, n: cst[:, a:a + n]
        identF = cF(K_ID, 128)
        triT = cF(K_TRI, 128)
        triG = cF(K_TRIG, 128)
        CST = B["cst"]

        wstate = {"next": 0}

        def wload(name, kc0, nkc, c0, ncols):
            i = wstate["next"]
            wstate["next"] = (i + 1) % NW
            t = wpool[i]
            bw = B["wp%d" % i]
            src = wbf[name].rearrange("(kc p) n -> p kc n", p=128)[:, kc0:kc0 + nkc, c0:c0 + ncols]
            dst = t[:, 0:nkc * ncols].rearrange("p (kc n) -> p kc n", n=ncols)
            S.dma("sp", dst, src, [wbuf[name]], [bw], "wp%d" % i)
            return dst, bw

        S.dma("sp", cst[:], consts_d, [], [CST], "cst")
        S.op("pool", lambda h: h.memset(onesF[:], 1.0), [], [B["onesF"]])
        S.op("pool", lambda h: h.memset(negc[:], -1.0 / 16.0), [], [B["negc"]])
        S.op("pool", lambda h: h.memset(Sg[:], 0.0), [], [B["Sg"]])
        S.op("pool", lambda h: h.memset(Sgb[:], 0.0), [], [B["Sgb"]])
        S.op("pool", lambda h: h.memset(Hs[:], 0.0), [], [B["Hs"]])
        S.op("pool", lambda h: h.memset(Hsb[:], 0.0), [], [B["Hsb"]])
        S.op("pool", lambda h: h.memset(halo_x[:], 0.0), [], [B["halo_x"]])
        S.op("pool", lambda h: h.memset(halo_f[:], 0.0), [], [B["halo_f"]])
        cp("dve", identB[:], identF, [CST], [B["identB"]])
        act(A_bc[:], cF(K_ALOG, 32), AF.Exp, [CST], [B["A_bc"]])
        ts("dve", A_bc[:], A_bc[:], -1.0, None, ALU.mult, None, [B["A_bc"]], [B["A_bc"]])

        cvt_engs = ["dve", "act", "pool"]
        cvt_in = [xs[i][:].rearrange("p a b -> p (a b)") for i in range(2)]
        cvt_in_b = [B["xs0"], B["xs1"]]
        cvt_out = [Xtm, Xdt]
        cvt_out_b = [B["Xtm"], B["Xdt"]]
        ci = 0
        for name, wd in wsrc.items():
            K, N = wd.shape
            for r0 in range(0, K, 128):
                for c0 in range(0, N, 2048):
                    ncol = min(2048, N - c0)
                    j = ci % 2
                    S.dma("sp", cvt_in[j][:, 0:ncol], wd[r0:r0 + 128, c0:c0 + ncol], [], [cvt_in_b[j]], "cvi%d" % j)
                    cp(cvt_engs[ci % 3], cvt_out[j][:, 0:ncol], cvt_in[j][:, 0:ncol], [cvt_in_b[j]], [cvt_out_b[j]])
                    S.dma("pool", wbf[name][r0:r0 + 128, c0:c0 + ncol], cvt_out[j][:, 0:ncol], [cvt_out_b[j]], [], "cvo%d" % j)
                    ci += 1
        cv_done = [Tok(S.dma_sems["cvo%d" % j][0], S.dma_sems["cvo%d" % j][1]) for j in range(2)]
        for name in wsrc:
            wbuf[name].w = cv_done

        def norm_transpose(xsrc, xbuf, gain_col0, ti, mask_col=None):
            ttr(junk[:], xsrc, xsrc, st4[:, 0:1], [xbuf], [B["junk"], B["st4"]])
            rstd_from_ss(st4[:, 0:1], rs4[:, 0:1], D, 1, B["st4"], B["rs4"])
            if mask_col is not None:
                tt("dve", rs4[:, 0:1], rs4[:, 0:1], mask_col, ALU.mult, [B["rs4"], CST], [B["rs4"]])
            act(tmpA[:], xsrc, AF.Copy, [xbuf, B["rs4"]], [B["tmpA"]], scale=rs4[:, 0:1])
            for hb in range(2):
                pi = psum_f()
                pt, pb = psF[pi]
                for c in range(4):
                    cc = hb * 4 + c
                    tr(pt[:, c * 128:(c + 1) * 128], tmpA[:, cc * 128:(cc + 1) * 128], identF, [B["tmpA"], CST], [pb], c == 3)
                g = cst[:, gain_col0 + hb * 4:gain_col0 + hb * 4 + 4].unsqueeze(2).to_broadcast([128, 4, 128])
                tt("dve", hT[:, hb * 4:hb * 4 + 4, ti * 128:(ti + 1) * 128], pt[:].rearrange("p (c t) -> p c t", c=4), g, ALU.mult,
                   [pb, CST], [B["hT"]])
                free_f(pi)

        def tm_group(wname, c0, ncols, evac):
            wt, bw = wload(wname, 0, 8, c0, ncols)
            for ti in range(NT):
                pi = psum_f()
                pt, pb = psF[pi]
                for k in range(8):
                    mm(pt[:, 0:ncols], hT[:, k, ti * 128:(ti + 1) * 128], wt[:, k, :], k == 0, k == 7, [B["hT"], bw], [pb], k == 7)
                evac(ti, pt, pb)
                free_f(pi)

        def fm_group(wname, c0, nch, evac, src=None, srcbuf=None, nk=8):
            src = hT if src is None else src
            srcbuf = B["hT"] if srcbuf is None else srcbuf
            wt, bw = wload(wname, 0, nk, c0, nch * 128)
            for j in range(nch):
                pi = psum_f()
                pt, pb = psF[pi]
                for k in range(nk):
                    mm(pt[:, 0:TB], wt[:, k, j * 128:(j + 1) * 128], src[:, k, :], k == 0, k == nk - 1, [srcbuf, bw], [pb], k == nk - 1)
                evac(j, pt, pb)
                free_f(pi)

        conv_i = {"i": 0}

        def conv_chunk(pt, pb, TBn, ntap, halo, halo_buf, cidx, wcol0, bcol, func, outap, outbuf, scale_after=None):
            j = conv_i["i"] % 2
            conv_i["i"] += 1
            hl = ntap - 1
            pr, pbuf = pre[j], B["pre%d" % j]
            ac, abuf = acc[0], B["acc0"]
            cp("pool", pr[:, 0:hl], halo[:, cidx, :], [halo_buf], [pbuf])
            cp("act", pr[:, hl:hl + TB], pt[:, 0:TB], [pb], [pbuf])
            cp("pool", halo[:, cidx, :], pr[:, TB:TB + hl], [pbuf], [halo_buf])
            for k in range(ntap):
                wk = cst[:, wcol0 + cidx * ntap + k:wcol0 + cidx * ntap + k + 1]
                if k == 0:
                    ts("dve", ac[:], pr[:, 0:TB], wk, None, ALU.mult, None, [pbuf, CST], [abuf])
                else:
                    stt(ac[:], pr[:, k:k + TB], wk, ac[:], ALU.mult, ALU.add, [pbuf, CST, abuf], [abuf])
            act(outap, ac[:], func, [abuf, CST], [outbuf], bias=cst[:, bcol + cidx:bcol + cidx + 1])

        def do_block(bi, full):
            xsb = xs[bi % 2]
            xbuf = B["xs%d" % (bi % 2)]
            t0 = bi * NT
            for ti in range(NT):
                S.dma("sp", xsb[:, ti, :], xw[(t0 + ti) * 128:(t0 + ti + 1) * 128, :], [], [xbuf], "xs%d" % (bi % 2))
            for ti in range(NT):
                norm_transpose(xsb[:, ti, :], xbuf, K_MIX, ti)

            for (dc0, sc0, n) in ((0, C_ALR, 16), (16, C_DT, 32)):
                S.dma("sp", wsm[:, :, dc0:dc0 + n], wbf["in"].rearrange("(kc p) n -> p kc n", p=128)[:, :, sc0:sc0 + n],
                      [wbuf["in"]], [B["wsm"]], "wsm")
            for ti in range(NT):
                tg = t0 + ti
                pi = psum_f()
                pt, pb = psF[pi]
                for (dc0, n) in ((0, 16), (16, 32)):
                    for k in range(8):
                        mm(pt[:, dc0:dc0 + n], hT[:, k, ti * 128:(ti + 1) * 128], wsm[:, k, dc0:dc0 + n], k == 0, k == 7,
                           [B["hT"], B["wsm"]], [pb], k == 7)
                cp("act", sm_a[:], pt[:, 0:48], [pb], [B["sm_a"]])
                free_f(pi)
                pi = psum_f()
                pt, pb = psF[pi]
                tr(pt[0:16, 0:128], sm_a[:, 0:16], identF, [B["sm_a"], CST], [pb], True)
                cp("act", alrT[:], pt[0:16, 0:128], [pb], [B["alrT"]])
                free_f(pi)
                pi = psum_f()
                pt, pb = psF[pi]
                mm(pt[:, 0:512], alrT[:], cst[0:16, K_WG:K_WG + 512], True, False, [B["alrT"], CST], [pb], False)
                mm(pt[:, 0:512], onesF[0:1, :], cst[0:1, K_BG:K_BG + 512], False, True, [B["onesF"], CST], [pb], True)
                act(tmpA[:, 0:512], pt[:, 0:512], AF.Exp, [pb], [B["tmpA"]], scale=-1.0)
                free_f(pi)
                act(l1[:], tmpA[:, 0:512], AF.Ln, [B["tmpA"]], [B["l1"]], bias=1.0)
                pi = psum_f()
                pt, pb = psF[pi]
                mm(pt[:, 0:512], triG, l1[:], True, True, [CST, B["l1"]], [pb], True)
                cp("act", b_sb[:, ti, :], pt[:, 0:512], [pb], [B["b_sb"]])
                free_f(pi)
                pi = psum_f()
                pt, pb = psF[pi]
                for h in range(4):
                    mm(pt[:, h:h + 1], l1[:, h * 128:(h + 1) * 128], negc[:, 0:1], True, True, [B["l1"], B["negc"]], [pb], h == 3)
                act(ebl[:, ti, :], pt[:, 0:4], AF.Exp, [pb], [B["ebl"]])
                free_f(pi)
                tt("dve", sm_b[:], sm_a[:, 16:48], cF(K_DTB, 32), ALU.add, [B["sm_a"], CST], [B["sm_b"]])
                act(sm_b[:], sm_b[:], AF.Exp, [B["sm_b"]], [B["sm_b"]])
                act(dtv[:, ti, :], sm_b[:], AF.Ln, [B["sm_b"]], [B["dtv"]], bias=1.0)
                ts("dve", dtv[:, ti, :], dtv[:, ti, :], cst[:, K_MASK + tg:K_MASK + tg + 1], None, ALU.mult, None, [B["dtv"], CST], [B["dtv"]])
                tt("dve", dav[:, ti, :], dtv[:, ti, :], A_bc[:], ALU.mult, [B["dtv"], B["A_bc"]], [B["dav"]])
                pi = psum_f()
                pt, pb = psF[pi]
                mm(pt[:, 0:32], triT, dav[:, ti, :], True, True, [CST, B["dav"]], [pb], False)
                mm(pt[:, 32:64], onesF[:], dav[:, ti, :], True, True, [B["onesF"], B["dav"]], [pb], True)
                cp("act", acs_sb[:], pt[:, 0:32], [pb], [B["acs_sb"]])
                act(eacs[:, ti, :], pt[:, 0:32], AF.Exp, [pb], [B["eacs"]])
                act(cdbc[:, ti, :], pt[:, 32:64], AF.Exp, [pb], [B["cdbc"]])
                ts("dve", nacs[:, ti, :], acs_sb[:], -1.0, None, ALU.mult, None, [B["acs_sb"]], [B["nacs"]])
                tt("dve", sm_b[:], pt[:, 32:64], acs_sb[:], ALU.subtract, [pb, B["acs_sb"]], [B["sm_b"]])
                free_f(pi)
                act(sm_b[:], sm_b[:], AF.Exp, [B["sm_b"]], [B["sm_b"]])
                tt("dve", dtd[:, ti, :], dtv[:, ti, :], sm_b[:], ALU.mult, [B["dtv"], B["sm_b"]], [B["dtd"]])

            def ev_q(ti, pt, pb):
                act(tmpA[:, 0:512], b_sb[:, ti, :], AF.Exp, [B["b_sb"]], [B["tmpA"]])
                stt(qt[:, ti, :], pt[:, 0:512], 128.0 ** -0.5, tmpA[:, 0:512], ALU.mult, ALU.mult, [pb, B["tmpA"]], [B["qt"]])

            def ev_k(ti, pt, pb):
                act(tmpA[:, 0:512], b_sb[:, ti, :], AF.Exp, [B["b_sb"]], [B["tmpA"]], scale=-1.0)
                tt("dve", kt[:, ti, :], pt[:, 0:512], tmpA[:, 0:512], ALU.mult, [pb, B["tmpA"]], [B["kt"]])

            if full:
                tm_group("in", C_Q, 512, ev_q)
            tm_group("in", C_K, 512, ev_k)
            for hf in range(2):
                tm_group("in", C_V + hf * 512, 512,
                         lambda ti, pt, pb, hf=hf: cp("act", v_sb[:, ti, hf * 512:(hf + 1) * 512], pt[:, 0:512], [pb], [B["v_sb"]]))
            if full:
                for hf in range(2):
                    tm_group("in", C_G + hf * 512, 512,
                             lambda ti, pt, pb, hf=hf: act(sg[:, ti, hf * 512:(hf + 1) * 512], pt[:, 0:512], AF.Silu, [pb], [B["sg"]]))
                for qd in range(4):
                    tm_group("in", C_Z + qd * 512, 512,
                             lambda ti, pt, pb, qd=qd: act(sz[:, ti, qd * 512:(qd + 1) * 512], pt[:, 0:512], AF.Silu, [pb], [B["sz"]]))

            nxt = 6 if full else 5
            for g in range(nxt):
                def ev_x(j, pt, pb, g=g):
                    c = g * 4 + j
                    conv_chunk(pt, pb, TB, 4, halo_x, B["halo_x"], c, K_WC, K_BC, AF.Silu, xbcT[:, c, :], B["xbcT"])
                fm_group("in", C_XBC + g * 512, 4, ev_x)
            if full:
                for g in range(4):
                    def ev_g(j, pt, pb, g=g):
                        c = g * 4 + j
                        act(sgab[:, c, :], pt[:, 0:TB], AF.Sigmoid, [pb], [B["sgab"]])
                    fm_group("in", C_GA + g * 512, 4, ev_g)

            for ti in range(NT):
                tsl = slice(ti * 128, (ti + 1) * 128)
                bi2 = psum_b()
                ptb, pbb = psB[bi2]
                for h in range(4):
                    tr(ptb[:, h * 128:(h + 1) * 128], kt[:, ti, h * 128:(h + 1) * 128], identB[:], [B["kt"], B["identB"]], [pbb], h == 3)
                cp("act", kT[:], ptb[:, 0:512].rearrange("p (h t) -> p h t", h=4), [pbb], [B["kT"]])
                free_b(bi2)
                if full:
                    bi2 = psum_b()
                    ptb, pbb = psB[bi2]
                    for h in range(4):
                        tr(ptb[:, h * 128:(h + 1) * 128], qt[:, ti, h * 128:(h + 1) * 128], identB[:], [B["qt"], B["identB"]], [pbb], h == 3)
                    cp("act", qT[:], ptb[:, 0:512].rearrange("p (h t) -> p h t", h=4), [pbb], [B["qT"]])
                    free_b(bi2)
                    pi = psum_f()
                    pt, pb = psF[pi]
                    for h in range(4):
                        mm(pt[:, h * 128:(h + 1) * 128], kT[:, h, :], qT[:, h, :], True, True, [B["kT"], B["qT"]], [pb], h == 3)
                    tt("dve", attm[:], pt[:].rearrange("p (h t) -> p h t", h=4), triT.unsqueeze(1).to_broadcast([128, 4, 128]), ALU.mult,
                       [pb, CST], [B["attm"]])
                    free_f(pi)
                    pos = [psum_f(), psum_f()]
                    for h in range(4):
                        pt, pb = psF[pos[h // 2]]
                        o_ap = pt[:, (h % 2) * 256:(h % 2) * 256 + 256]
                        mm(o_ap, attm[:, h, :], v_sb[:, ti, h * 256:(h + 1) * 256], True, False, [B["attm"], B["v_sb"]], [pb], False)
                        mm(o_ap, qT[:, h, :], Sgb[:, h, :], False, True, [B["qT"], B["Sgb"]], [pb], True)
                    for b2 in range(2):
                        pt, pb = psF[pos[b2]]
                        cp("act", tmpA[:, b2 * 512:(b2 + 1) * 512], pt[:, 0:512], [pb], [B["tmpA"]])
                        free_f(pos[b2])
                    for h in range(4):
                        o_ap = tmpA[:, h * 256:(h + 1) * 256]
                        ttr(junk[:, 0:256], o_ap, o_ap, st4[:, h:h + 1], [B["tmpA"]], [B["junk"], B["st4"]])
                    rstd_from_ss(st4[:, 0:4], rs4[:, 0:4], 256, 4, B["st4"], B["rs4"])
                    for h in range(4):
                        o_ap = tmpA[:, h * 256:(h + 1) * 256]
                        stt(tmpB[:, h * 256:(h + 1) * 256], o_ap, rs4[:, h:h + 1], sg[:, ti, h * 256:(h + 1) * 256], ALU.mult, ALU.mult,
                            [B["tmpA"], B["rs4"], B["sg"]], [B["tmpB"]])
                    for hb in range(2):
                        pi = psum_f()
                        pt, pb = psF[pi]
                        for c in range(4):
                            cc = hb * 4 + c
                            tr(pt[:, c * 128:(c + 1) * 128], tmpB[:, cc * 128:(cc + 1) * 128], identF, [B["tmpB"], CST], [pb], c == 3)
                        g = cst[:, K_HN + hb * 4:K_HN + hb * 4 + 4].unsqueeze(2).to_broadcast([128, 4, 128])
                        tt("dve", oaT[:, hb * 4:hb * 4 + 4, tsl], pt[:].rearrange("p (c t) -> p c t", c=4), g, ALU.mult, [pb, CST], [B["oaT"]])
                        free_f(pi)
                pds = [psum_f(), psum_f()]
                for h in range(4):
                    pt, pb = psF[pds[h // 2]]
                    mm(pt[:, (h % 2) * 256:(h % 2) * 256 + 256], kt[:, ti, h * 128:(h + 1) * 128], v_sb[:, ti, h * 256:(h + 1) * 256], True, True,
                       [B["kt"], B["v_sb"]], [pb], h % 2 == 1)
                for b2 in range(2):
                    pt, pb = psF[pds[b2]]
                    sl = Sg[:, 2 * b2:2 * b2 + 2, :]
                    tt("dve", sl, pt[:].rearrange("p (h v) -> p h v", h=2), sl, ALU.add, [pb, B["Sg"]], [B["Sg"]])
                    tt("dve", sl, sl, ebl[:, ti, 2 * b2:2 * b2 + 2].unsqueeze(2).to_broadcast([128, 2, 256]), ALU.mult, [B["Sg"], B["ebl"]], [B["Sg"]])
                    free_f(pds[b2])
                cp("pool", Sgb[:], Sg[:], [B["Sg"]], [B["Sgb"]])

                for hb in range(2):
                    bi2 = psum_b()
                    ptb, pbb = psB[bi2]
                    for c in range(8):
                        cc = hb * 8 + c
                        tr(ptb[:, c * 128:(c + 1) * 128], xbcT[:, cc, tsl], identB[:], [B["xbcT"], B["identB"]], [pbb], c == 7)
                    cp("act", Xtm[:, hb * 1024:(hb + 1) * 1024], ptb[:, 0:1024], [pbb], [B["Xtm"]])
                    free_b(bi2)
                bi2 = psum_b()
                ptb, pbb = psB[bi2]
                for c in range(4):
                    tr(ptb[:, c * 128:(c + 1) * 128], xbcT[:, 16 + c, tsl], identB[:], [B["xbcT"], B["identB"]], [pbb], c == 3)
                cp("act", Btm[:], ptb[:, 0:512], [pbb], [B["Btm"]])
                free_b(bi2)
                X3 = Xtm[:].rearrange("p (h d) -> p h d", d=64)
                tt("pool", Xd[:].rearrange("p (h d) -> p h d", d=64), X3, dtd[:, ti, :].unsqueeze(2).to_broadcast([128, 32, 64]), ALU.mult,
                   [B["Xtm"], B["dtd"]], [B["Xd"]])
                if full:
                    tt("pool", Xdt[:].rearrange("p (h d) -> p h d", d=64), X3, dtv[:, ti, :].unsqueeze(2).to_broadcast([128, 32, 64]), ALU.mult,
                       [B["Xtm"], B["dtv"]], [B["Xdt"]])
                    pi = psum_f()
                    pt, pb = psF[pi]
                    for g in range(4):
                        mm(pt[:, g * 128:(g + 1) * 128], xbcT[:, 16 + g, tsl], xbcT[:, 20 + g, tsl], True, True, [B["xbcT"]], [pb], g == 3)
                    tt("dve", scm[:], pt[:].rearrange("p (g t) -> p g t", g=4), triT.unsqueeze(1).to_broadcast([128, 4, 128]), ALU.mult,
                       [pb, CST], [B["scm"]])
                    free_f(pi)
                    for g in range(4):
                        j = g % 2
                        for hh2 in range(2):
                            jz = (2 * g + hh2) % 2
                            h0 = 8 * g + 4 * hh2
                            tt("pool", Zg[jz][:], triT.unsqueeze(1).to_broadcast([128, 4, 128]),
                               dav[:, ti, h0:h0 + 4].unsqueeze(2).to_broadcast([128, 4, 128]), ALU.mult, [CST, B["dav"]], [B["Zg%d" % jz]])
                            pi = psum_f()
                            pt, pb = psF[pi]
                            mm(pt[:, 0:512], onesF[:], Zg[jz][:].rearrange("p h t -> p (h t)"), True, True,
                               [B["onesF"], B["Zg%d" % jz]], [pb], True)
                            for h4 in range(4):
                                hd = h0 + h4
                                ts("dve", Eg[jz][:, h4, :], pt[:, h4 * 128:(h4 + 1) * 128], nacs[:, ti, hd:hd + 1], 0.0, ALU.add, ALU.min,
                                   [pb, B["nacs"]], [B["Eg%d" % jz]])
                            free_f(pi)
                            act(Eg[jz][:], Eg[jz][:], AF.Exp, [B["Eg%d" % jz]], [B["Eg%d" % jz]])
                            tt("dve", MTg[j][:, 4 * hh2:4 * hh2 + 4, :], Eg[jz][:], scm[:, g, :].unsqueeze(1).to_broadcast([128, 4, 128]), ALU.mult,
                               [B["Eg%d" % jz], B["scm"]], [B["MTg%d" % j]])
                        pyd = psum_f()
                        pt, pb = psF[pyd]
                        for hh in range(8):
                            hd = 8 * g + hh
                            mm(pt[:, hh * 64:(hh + 1) * 64], MTg[j][:, hh, :], Xdt[:, hd * 64:(hd + 1) * 64], True, True,
                               [B["MTg%d" % j], B["Xdt"]], [pb], hh == 7)
                        pyo = psum_f()
                        pt2, pb2 = psF[pyo]
                        mm(pt2[:, 0:512], xbcT[:, 20 + g, tsl], Hsb[:, g * 512:(g + 1) * 512], True, True, [B["xbcT"], B["Hsb"]], [pb2], True)
                        e_bc = eacs[:, ti, 8 * g:8 * g + 8].unsqueeze(2).to_broadcast([128, 8, 64])
                        tt("dve", tmpC[:].rearrange("p (h d) -> p h d", d=64), pt2[:].rearrange("p (h d) -> p h d", d=64), e_bc, ALU.mult,
                           [pb2, B["eacs"]], [B["tmpC"]])
                        tt("dve", tmpC[:], pt[:, 0:512], tmpC[:], ALU.add, [pb, B["tmpC"]], [B["tmpC"]])
                        free_f(pyd)
                        free_f(pyo)
                        d_bc = cst[:, K_DSK + 8 * g:K_DSK + 8 * g + 8].unsqueeze(2).to_broadcast([128, 8, 64])
                        tt("pool", tmpD[:].rearrange("p (h d) -> p h d", d=64), Xtm[:, g * 512:(g + 1) * 512].rearrange("p (h d) -> p h d", d=64),
                           d_bc, ALU.mult, [B["Xtm"], CST], [B["tmpD"]])
                        tt("dve", tmpC[:], tmpC[:], tmpD[:], ALU.add, [B["tmpC"], B["tmpD"]], [B["tmpC"]])
                        tt("dve", yz[:, g * 512:(g + 1) * 512], tmpC[:], sz[:, ti, g * 512:(g + 1) * 512], ALU.mult, [B["tmpC"], B["sz"]], [B["yz"]])
                        ttr(junk[:, 0:512], yz[:, g * 512:(g + 1) * 512], yz[:, g * 512:(g + 1) * 512], st4[:, 4 + g:5 + g],
                            [B["yz"]], [B["junk"], B["st4"]])
                    rstd_from_ss(st4[:, 4:8], rs4[:, 4:8], 512, 4, B["st4"], B["rs4"])
                    tt("dve", yz[:].rearrange("p (g c) -> p g c", g=4), yz[:].rearrange("p (g c) -> p g c", g=4),
                       rs4[:, 4:8].unsqueeze(2).to_broadcast([128, 4, 512]), ALU.mult, [B["yz"], B["rs4"]], [B["yz"]])
                    for hb in range(4):
                        pi = psum_f()
                        pt, pb = psF[pi]
                        for c in range(4):
                            cc = hb * 4 + c
                            tr(pt[:, c * 128:(c + 1) * 128], yz[:, cc * 128:(cc + 1) * 128], identF, [B["yz"], CST], [pb], c == 3)
                        g = cst[:, K_SSM + hb * 4:K_SSM + hb * 4 + 4].unsqueeze(2).to_broadcast([128, 4, 128])
                        tt("dve", ynT[:, hb * 4:hb * 4 + 4, tsl], pt[:].rearrange("p (c t) -> p c t", c=4), g, ALU.mult, [pb, CST], [B["ynT"]])
                        free_f(pi)
                for g in range(4):
                    pi = psum_f()
                    pt, pb = psF[pi]
                    mm(pt[:, 0:512], Btm[:, g * 128:(g + 1) * 128], Xd[:, g * 512:(g + 1) * 512], True, True, [B["Btm"], B["Xd"]], [pb], True)
                    hsl = Hs[:, g * 512:(g + 1) * 512]
                    c_bc = cdbc[:, ti, 8 * g:8 * g + 8].unsqueeze(2).to_broadcast([128, 8, 64])
                    tt("dve", hsl.rearrange("p (h d) -> p h d", d=64), hsl.rearrange("p (h d) -> p h d", d=64), c_bc, ALU.mult,
                       [B["Hs"], B["cdbc"]], [B["Hs"]])
                    tt("dve", hsl, hsl, pt[:, 0:512], ALU.add, [B["Hs"], pb], [B["Hs"]])
                    free_f(pi)
                cp("pool", Hsb[:], Hs[:], [B["Hs"]], [B["Hsb"]])

            if not full:
                return

            for nh in range(2):
                wa_t, wa_b = wload("a", 0, 8, nh * 512, 512)
                wb0_t, wb0_b = wload("b", 0, 8, nh * 512, 512)
                wb1_t, wb1_b = wload("b", 8, 8, nh * 512, 512)
                for j in range(4):
                    n = nh * 4 + j
                    pa = psum_f()
                    pta, pba = psF[pa]
                    for k in range(8):
                        mm(pta[:, 0:TB], wa_t[:, k, j * 128:(j + 1) * 128], oaT[:, k, :], k == 0, k == 7, [wa_b, B["oaT"]], [pba], k == 7)
                    pbk = psum_f()
                    ptb_, pbb_ = psF[pbk]
                    for k in range(16):
                        wt_, wb_ = (wb0_t, wb0_b) if k < 8 else (wb1_t, wb1_b)
                        mm(ptb_[:, 0:TB], wt_[:, k % 8, j * 128:(j + 1) * 128], ynT[:, k, :], k == 0, k == 15, [wb_, B["ynT"]], [pbb_], k % 8 == 7)
                    tt("dve", tmpC[:, 0:TB], pta[:, 0:TB], sgab[:, n, :], ALU.mult, [pba, B["sgab"]], [B["tmpC"]])
                    tt("dve", tmpD[:, 0:TB], ptb_[:, 0:TB], sgab[:, 8 + n, :], ALU.mult, [pbb_, B["sgab"]], [B["tmpD"]])
                    tt("pool", mT[:, n, :], tmpC[:, 0:TB], tmpD[:, 0:TB], ALU.add, [B["tmpC"], B["tmpD"]], [B["mT"]])
                    free_f(pa)
                    free_f(pbk)
            for hf in range(2):
                wt, bw = wload("out", 0, 8, hf * 512, 512)
                for ti in range(NT):
                    pi = psum_f()
                    pt, pb = psF[pi]
                    for k in range(8):
                        mm(pt[:, 0:512], mT[:, k, ti * 128:(ti + 1) * 128], wt[:, k, :], k == 0, k == 7, [B["mT"], bw], [pb], k == 7)
                    xsl = xsb[:, ti, hf * 512:(hf + 1) * 512]
                    tt("dve", xsl, xsl, pt[:, 0:512], ALU.add, [xbuf, pb], [xbuf])
                    free_f(pi)

            for ti in range(NT):
                tg = t0 + ti
                norm_transpose(xsb[:, ti, :], xbuf, K_FFN, ti, mask_col=cst[:, K_MASK + tg:K_MASK + tg + 1])
            gT = xbcT
            GT = B["xbcT"]
            fchunks = [(0, 4), (4, 4), (8, 4), (12, 4), (16, 4), (20, 2)]
            for (f0, nf) in fchunks:
                wa_t, wa_b = wload("up", 0, 8, f0 * 128, nf * 128)
                wl_t, wl_b = wload("up", 0, 8, FFN_H + f0 * 128, nf * 128)
                for j in range(nf):
                    f = f0 + j
                    pa = psum_f()
                    pta, pba = psF[pa]
                    for k in range(8):
                        mm(pta[:, 0:TB], wa_t[:, k, j * 128:(j + 1) * 128], hT[:, k, :], k == 0, k == 7, [wa_b, B["hT"]], [pba], k == 7)
                    pl = psum_f()
                    ptl, pbl = psF[pl]
                    for k in range(8):
                        mm(ptl[:, 0:TB], wl_t[:, k, j * 128:(j + 1) * 128], hT[:, k, :], k == 0, k == 7, [wl_b, B["hT"]], [pbl], k == 7)
                    conv_chunk(pta, pba, TB, 3, halo_f, B["halo_f"], f, K_WFC, K_BFC, AF.Gelu_apprx_tanh, tmpC[:, 0:TB], B["tmpC"])
                    tt("dve", gT[:, f, :], tmpC[:, 0:TB], ptl[:, 0:TB], ALU.mult, [B["tmpC"], pbl], [GT])
                    free_f(pa)
                    free_f(pl)
            kparts = [(0, 8), (8, 8), (16, 6)]
            for hf in range(2):
                pis = [psum_f() for _ in range(NT)]
                for kp, (k0, nk) in enumerate(kparts):
                    wt, bw = wload("down", k0, nk, hf * 512, 512)
                    for ti in range(NT):
                        pt, pb = psF[pis[ti]]
                        for k in range(nk):
                            mm(pt[:, 0:512], gT[:, k0 + k, ti * 128:(ti + 1) * 128], wt[:, k, :], kp == 0 and k == 0, kp == 2 and k == nk - 1,
                               [GT, bw], [pb], k == nk - 1)
                for ti in range(NT):
                    pt, pb = psF[pis[ti]]
                    xsl = xsb[:, ti, hf * 512:(hf + 1) * 512]
                    tt("dve", xsl, xsl, pt[:, 0:512], ALU.add, [xbuf, pb], [xbuf])
                    free_f(pis[ti])

            fb = bi - n_sblk
            for ti in range(NT):
                norm_transpose(xsb[:, ti, :], xbuf, K_PLE, ti)
                S.dma("sp", p_in[:], pw[(fb * NT + ti) * 128:(fb * NT + ti + 1) * 128, :], [], [B["p_in"]], "p_in")
                pi = psum_f()
                pt, pb = psF[pi]
                for c in range(2):
                    tr(pt[:, c * 128:(c + 1) * 128], p_in[:, c * 128:(c + 1) * 128], identF, [B["p_in"], CST], [pb], c == 1)
                cp("act", pT[:, :, ti * 128:(ti + 1) * 128], pt[:, 0:256].rearrange("p (c t) -> p c t", c=2), [pb], [B["pT"]])
                free_f(pi)
            wpp_t, wpp_b = wload("pp", 0, 2, 0, 1024)
            for hf in range(2):
                wt, bw = wload("pg", 0, 8, hf * 512, 512)
                for ti in range(NT):
                    pi = psum_f()
                    pt, pb = psF[pi]
                    for k in range(8):
                        mm(pt[:, 0:512], hT[:, k, ti * 128:(ti + 1) * 128], wt[:, k, :], k == 0, k == 7, [B["hT"], bw], [pb], k == 7)
                    act(tmpA[:, 0:512], pt[:, 0:512], AF.Sigmoid, [pb], [B["tmpA"]])
                    free_f(pi)
                    pi = psum_f()
                    pt, pb = psF[pi]
                    for k in range(2):
                        mm(pt[:, 0:512], pT[:, k, ti * 128:(ti + 1) * 128], wpp_t[:, k, hf * 512:(hf + 1) * 512], k == 0, k == 1,
                           [B["pT"], wpp_b], [pb], k == 1)
                    tt("dve", tmpB[:, 0:512], tmpA[:, 0:512], pt[:, 0:512], ALU.mult, [B["tmpA"], pb], [B["tmpB"]])
                    free_f(pi)
                    xsl = xsb[:, ti, hf * 512:(hf + 1) * 512]
                    tt("dve", xsl, xsl, tmpB[:, 0:512], ALU.add, [xbuf, B["tmpB"]], [xbuf])

            for ti in range(NT):
                tg = t0 + ti
                ot = tg - (NTILES - n_out_tiles)
                if ot < 0:
                    continue
                xsrc = xsb[:, ti, :]
                ttr(junk[:], xsrc, xsrc, st4[:, 0:1], [xbuf], [B["junk"], B["st4"]])
                rstd_from_ss(st4[:, 0:1], rs4[:, 0:1], D, 1, B["st4"], B["rs4"])
                ob = out_sb[0]
                obuf = B["out_sb0"]
                stt(ob[:], xsrc, rs4[:, 0:1], cF(K_FIN, 1024), ALU.mult, ALU.mult, [xbuf, B["rs4"], CST], [obuf])
                S.dma("pool", out_d[ot * 128:(ot + 1) * 128, :], ob[:], [obuf], [], "out_sb0")

        for bi in range(NBLK):
            do_block(bi, bi >= n_sblk)

        if debug is not None:
            debug_fn = build_program.debug_fn
            debug_fn(S, B, locals(), dbg_d)

        for key, ent in S.dma_sems.items():
            if key.startswith("out_sb") or key == "dbg":
                S.wait_tok("pool", Tok(ent[0], ent[1]))

        block = es.enter_context(nc.Block())
        S.emit(block)
    return nc


build_program.debug_fn = None


def make_consts(inp, mask, ntiles_cols=64):
    c = np.zeros((128, 2848), np.float32)
    c[:, K_ID:K_ID + 128] = np.eye(128, dtype=np.float32)
    tri = np.triu(np.ones((128, 128), np.float32))
    c[:, K_TRI:K_TRI + 128] = tri
    c[:, K_TRIG:K_TRIG + 128] = tri * np.float32(-1.0 / 16.0)
    c[0:16, K_WG:K_WG + 512] = inp["w_gla_gate"][0]
    c[0:1, K_BG:K_BG + 512] = inp["b_gla_gate"][0][None, :]
    c[:, K_FIN:K_FIN + 1024] = np.broadcast_to(inp["final_norm"][None, :], (128, 1024))
    colv = lambda v: np.ascontiguousarray(v.reshape(-1, 128).T)
    c[:, K_MIX:K_MIX + 8] = colv(inp["mixer_norm"][0])
    c[:, K_FFN:K_FFN + 8] = colv(inp["ffn_norm"][0])
    c[:, K_PLE:K_PLE + 8] = colv(inp["ple_norm"][0])
    c[:, K_HN:K_HN + 8] = np.tile(colv(inp["gla_norm"][0]), (1, 4))
    c[:, K_SSM:K_SSM + 16] = colv(inp["ssm_norm"][0])
    wc = inp["w_ssm_conv"][0]
    c[:, K_WC:K_WC + 96] = wc.T.reshape(24, 128, 4).transpose(1, 0, 2).reshape(128, 96)
    c[:, K_BC:K_BC + 24] = colv(inp["b_ssm_conv"][0])
    wf = inp["w_ffn_conv"][0]
    c[:, K_WFC:K_WFC + 66] = wf.T.reshape(22, 128, 3).transpose(1, 0, 2).reshape(128, 66)
    c[:, K_BFC:K_BFC + 22] = colv(inp["b_ffn_conv"][0])
    c[:, K_DTB:K_DTB + 32] = np.broadcast_to(inp["dt_bias"][0][None, :], (128, 32))
    c[:, K_ALOG:K_ALOG + 32] = np.broadcast_to(inp["a_log"][0][None, :], (128, 32))
    c[:, K_DSK:K_DSK + 32] = np.broadcast_to(inp["d_skip"][0][None, :], (128, 32))
    nt = mask.shape[0] // 128
    c[:, K_MASK:K_MASK + nt] = mask.reshape(nt, 128).T
    return c


NT_ = 2
N_FBLK = 9
N_SBLK = 23


def kernel(**inp):
    inp = {k: np.asarray(v) for k, v in inp.items()}
    x = inp["x"]
    p = inp["p"][0]
    ntiles = (N_SBLK + N_FBLK) * NT_
    WT = ntiles * 128
    FT = N_FBLK * NT_ * 128
    nc = build_program(N_SBLK, N_FBLK, OWN // 128, NT=NT_)
    shared = {
        "w_in": np.ascontiguousarray(inp["w_in"][0]),
        "w_branch_a": np.ascontiguousarray(inp["w_branch_a"][0]),
        "w_branch_b": np.ascontiguousarray(inp["w_branch_b"][0]),
        "w_out": np.ascontiguousarray(inp["w_out"][0]),
        "w_ffn_up": np.ascontiguousarray(inp["w_ffn_up"][0]),
        "w_ffn_down": np.ascontiguousarray(inp["w_ffn_down"][0]),
        "w_ple_gate": np.ascontiguousarray(inp["w_ple_gate"][0]),
        "w_ple_proj": np.ascontiguousarray(inp["w_ple_proj"][0]),
    }
    in_maps = []
    for core in range(NCORES):
        b, j = core // 4, core % 4
        end = (j + 1) * OWN
        start = end - WT
        xwin = np.zeros((WT, D), np.float32)
        mask = np.zeros((WT,), np.float32)
        s0 = max(start, 0)
        xwin[s0 - start:] = x[b, s0:end]
        mask[s0 - start:] = 1.0
        pwin = np.zeros((FT, 256), np.float32)
        ps = end - FT
        ps0 = max(ps, 0)
        pwin[ps0 - ps:] = p[b, ps0:end]
        m = dict(shared)
        m["xw"] = xwin
        m["pw"] = pwin
        m["consts"] = make_consts(inp, mask)
        in_maps.append(m)
    res = run_bass_kernel_spmd(nc, in_maps, core_ids=list(range(NCORES)))
    out = np.zeros((BATCH, SEQ, D), np.float32)
    for core in range(NCORES):
        b, j = core // 4, core % 4
        out[b, j * OWN:(j + 1) * OWN] = np.asarray(res.results[core]["out"]).reshape(OWN, D)
    return out
```

```python
import numpy as np
from contextlib import ExitStack
import concourse.bass as bass
import concourse.mybir as mybir
from concourse.bass_utils import run_bass_kernel_spmd

F32 = mybir.dt.float32
BF16 = mybir.dt.bfloat16
AF = mybir.ActivationFunctionType
ALU = mybir.AluOpType
AX = mybir.AxisListType

D = 1024
SEQ = 8192
BATCH = 2
NCORES = 8
OWN = 2048
EPS = 1e-6
C_Q, C_K, C_V, C_G, C_ALR, C_Z, C_XBC, C_DT, C_GA, C_GB = 0, 512, 1024, 2048, 3072, 3088, 5136, 8208, 8240, 9264
IN_W = 10288
FFN_H = 2816

K_ID, K_TRI, K_TRIG, K_WG, K_BG, K_FIN = 0, 128, 256, 384, 896, 1408
K_MIX, K_FFN, K_PLE, K_HN, K_SSM = 2432, 2440, 2448, 2456, 2464
K_WC, K_BC, K_WFC, K_BFC = 2480, 2576, 2600, 2666
K_DTB, K_ALOG, K_DSK, K_MASK = 2688, 2720, 2752, 2784


class Tok:
    __slots__ = ("sem", "val")

    def __init__(self, sem, val):
        self.sem = sem
        self.val = val


class Buf:
    def __init__(self, name):
        self.name = name
        self.w = None
        self.r = []


class Eng:
    def __init__(self, name, sem):
        self.name = name
        self.sem = sem
        self.count = 0
        self.known = {}
        self.prog = []


class Sched:
    def __init__(self, nc, es):
        self.nc = nc
        self.es = es
        self.eng = {}
        for n in ("pe", "act", "dve", "pool", "sp"):
            self.eng[n] = Eng(n, es.enter_context(nc.semaphore("sem_" + n)))
        self.nsem = 0
        self.dma_sems = {}

    def _need(self, e, tok, waits, skip_sem=None):
        if tok is None:
            return
        if isinstance(tok, list):
            for t in tok:
                self._need(e, t, waits, skip_sem)
            return
        if skip_sem is not None and tok.sem is skip_sem:
            return
        if tok.sem is e.sem and e.name == "pe":
            return
        k = id(tok.sem)
        if e.known.get(k, 0) < tok.val:
            e.known[k] = tok.val
            waits[k] = tok

    def _deps(self, e, reads, writes, skip_sem=None):
        waits = {}
        for b in reads:
            self._need(e, b.w, waits)
        for b in writes:
            self._need(e, b.w, waits, skip_sem)
            for t in b.r:
                self._need(e, t, waits)
        for t in waits.values():
            sem, val = t.sem, t.val
            e.prog.append(lambda h, sem=sem, val=val: h.wait_ge(sem, val))

    def _commit(self, tok, reads, writes):
        for b in reads:
            b.r.append(tok)
            if len(b.r) > 12:
                best = {}
                for t in b.r:
                    k = id(t.sem)
                    if k not in best or best[k].val < t.val:
                        best[k] = t
                b.r = list(best.values())
        for b in writes:
            b.w = tok
            b.r = []

    def op(self, en, fn, reads=(), writes=(), inc=True):
        e = self.eng[en]
        self._deps(e, reads, writes)
        if inc:
            e.count += 1
            sem = e.sem
            e.prog.append(lambda h, fn=fn, sem=sem: fn(h).then_inc(sem, 1))
            tok = Tok(e.sem, e.count)
        else:
            e.prog.append(lambda h, fn=fn: fn(h))
            tok = Tok(e.sem, e.count + 1)
        self._commit(tok, reads, writes)

    def dma(self, q, out, in_, reads, writes, key, **kw):
        e = self.eng[q]
        if key not in self.dma_sems:
            self.dma_sems[key] = [self.es.enter_context(self.nc.semaphore("dma_" + key)), 0]
        ent = self.dma_sems[key]
        self._deps(e, reads, writes, skip_sem=ent[0])
        ent[1] += 16
        sem = ent[0]
        e.prog.append(lambda h, out=out, in_=in_, sem=sem, kw=kw: h.dma_start(out=out, in_=in_, **kw).then_inc(sem, 16))
        tok = Tok(sem, ent[1])
        self._commit(tok, reads, writes)
        return tok

    def wait_tok(self, en, tok):
        e = self.eng[en]
        waits = {}
        self._need(e, tok, waits)
        for t in waits.values():
            sem, val = t.sem, t.val
            e.prog.append(lambda h, sem=sem, val=val: h.wait_ge(sem, val))

    def emit(self, block):
        def run(prog):
            def f(h):
                for p in prog:
                    p(h)
            return f
        block.sync(run(self.eng["sp"].prog))
        block.tensor(run(self.eng["pe"].prog))
        block.scalar(run(self.eng["act"].prog))
        block.vector(run(self.eng["dve"].prog))
        block.gpsimd(run(self.eng["pool"].prog))


def build_program(n_sblk, n_fblk, n_out_tiles, NT=2, debug=None):
    TB = NT * 128
    NBLK = n_sblk + n_fblk
    NTILES = NBLK * NT
    WT = NTILES * 128
    FT = n_fblk * TB
    OT = n_out_tiles * 128
    assert NTILES <= 64

    nc = bass.Bass("TRN2", target_bir_lowering=False)

    def din(name, shape, dt=F32):
        return nc.dram_tensor(name, list(shape), dt, kind="ExternalInput").ap()

    xw = din("xw", [WT, D])
    pw = din("pw", [FT, 256])
    consts_d = din("consts", [128, 2848])
    w_in = din("w_in", [D, IN_W])
    w_a = din("w_branch_a", [1024, D])
    w_b = din("w_branch_b", [2048, D])
    w_out = din("w_out", [D, D])
    w_up = din("w_ffn_up", [D, 2 * FFN_H])
    w_down = din("w_ffn_down", [FFN_H, D])
    w_pg = din("w_ple_gate", [D, D])
    w_pp = din("w_ple_proj", [256, D])
    out_d = nc.dram_tensor("out", [OT, D], F32, kind="ExternalOutput").ap()
    dbg_d = None
    if debug is not None:
        dbg_d = nc.dram_tensor("dbg", list(debug), F32, kind="ExternalOutput").ap()

    def dscr(name, shape):
        return nc.dram_tensor(name, list(shape), BF16, kind="Internal").ap()

    wsrc = {"in": w_in, "a": w_a, "b": w_b, "out": w_out, "up": w_up, "down": w_down, "pg": w_pg, "pp": w_pp}
    wbf = {k: dscr("wbf_" + k, v.shape) for k, v in wsrc.items()}

    es = ExitStack()
    with es:
        S = Sched(nc, es)
        bufs = {}

        def sb(name, shape, dt=F32):
            t = es.enter_context(nc.sbuf_tensor(name, list(shape), dt))
            bufs[name] = Buf(name)
            return t

        cst = sb("cst", [128, 2848])
        identB = sb("identB", [128, 128], BF16)
        onesF = sb("onesF", [128, 128])
        negc = sb("negc", [128, 1])
        A_bc = sb("A_bc", [128, 32])
        xs = [sb("xs%d" % i, [128, NT, D]) for i in range(2)]
        hT = sb("hT", [128, 8, TB], BF16)
        b_sb = sb("b_sb", [128, NT, 512])
        ebl = sb("ebl", [128, NT, 4])
        qt = sb("qt", [128, NT, 512], BF16)
        kt = sb("kt", [128, NT, 512], BF16)
        v_sb = sb("v_sb", [128, NT, 1024], BF16)
        sg = sb("sg", [128, NT, 1024], BF16)
        sz = sb("sz", [128, NT, 2048], BF16)
        xbcT = sb("xbcT", [128, 24, TB], BF16)
        sgab = sb("sgab", [128, 16, TB], BF16)
        oaT = sb("oaT", [128, 8, TB], BF16)
        ynT = sb("ynT", [128, 16, TB], BF16)
        mT = sb("mT", [128, 8, TB], BF16)
        pT = sb("pT", [128, 2, TB], BF16)
        p_in = sb("p_in", [128, 256])
        dtv = sb("dtv", [128, NT, 32])
        dav = sb("dav", [128, NT, 32])
        nacs = sb("nacs", [128, NT, 32])
        eacs = sb("eacs", [128, NT, 32])
        dtd = sb("dtd", [128, NT, 32])
        cdbc = sb("cdbc", [128, NT, 32])
        acs_t = [sb("acs%d" % i, [128, 32]) for i in range(NT)]
        sm_a_t = [sb("sm_a%d" % i, [128, 48]) for i in range(NT)]
        sm_b_t = [sb("sm_b%d" % i, [128, 32]) for i in range(NT)]
        alrT_t = [sb("alrT%d" % i, [16, 128]) for i in range(NT)]
        tmpA = sb("tmpA", [128, 1024])
        tmpB = sb("tmpB", [128, 1024])
        tmpC = sb("tmpC", [128, 512])
        tmpD = sb("tmpD", [128, 512])
        assert NT == 2
        l1_t = [tmpC, tmpD]
        l1_b = [bufs["tmpC"], bufs["tmpD"]]
        junk = sb("junk", [128, 1024], BF16)
        st4 = sb("st4", [128, 8])
        rs4 = sb("rs4", [128, 8])
        st4s = sb("st4s", [128, 4])
        rs4s = sb("rs4s", [128, 4])
        junk2 = sb("junk2", [128, 512], BF16)
        qT = sb("qT", [128, 4, 128], BF16)
        kT = sb("kT", [128, 4, 128], BF16)
        attm = sb("attm", [128, 4, 128], BF16)
        Sg = sb("Sg", [128, 4, 256])
        Sgb = sb("Sgb", [128, 4, 256], BF16)
        Xtm = sb("Xtm", [128, 2048], BF16)
        Xdt = sb("Xdt", [128, 2048], BF16)
        Xd = sb("Xd", [128, 2048], BF16)
        Btm = sb("Btm", [128, 512], BF16)
        scm = sb("scm", [128, 4, 128], BF16)
        Zg = [sb("Zg%d" % i, [128, 4, 128]) for i in range(2)]
        Eg = [sb("Eg%d" % i, [128, 4, 128]) for i in range(2)]
        MTg = [sb("MTg%d" % i, [128, 8, 128], BF16) for i in range(2)]
        yz = sb("yz", [128, 2048])
        Hs = sb("Hs", [128, 2048])
        Hsb = sb("Hsb", [128, 2048], BF16)
        pre = [sb("pre%d" % i, [128, TB + 3]) for i in range(3)]
        acc = [sb("acc%d" % i, [128, TB]) for i in range(2)]
        halo_x = sb("halo_x", [128, 24, 3])
        halo_f = sb("halo_f", [128, 22, 2])
        out_sb = [tmpB] * 2
        bufs["out_sb0"] = bufs["tmpB"]
        NW = 4
        wpool = [sb("wp%d" % i, [128, 4096], BF16) for i in range(NW)]
        wsm = sb("wsm", [128, 8, 48], BF16)

        B = bufs
        psF = []
        for i in range(6):
            t = es.enter_context(nc.psum_tensor("psF%d" % i, [128, 512], F32))
            psF.append((t, Buf("psF%d" % i)))
        psB = []
        for i in range(2):
            t = es.enter_context(nc.psum_tensor("psB%d" % i, [128, 1024], BF16))
            psB.append((t, Buf("psB%d" % i)))
        freeF = list(range(6))
        freeB = list(range(2))

        def psum_f():
            i = freeF.pop(0)
            return i

        def free_f(i):
            freeF.append(i)

        def psum_b():
            return freeB.pop(0)

        def free_b(i):
            freeB.append(i)

        wbuf = {k: Buf("wbf_" + k) for k in wsrc}

        def mm(out, lhsT, rhs, start, stop, reads, writes, inc):
            S.op("pe", lambda h: h.matmul(out, lhsT, rhs, start=start, stop=stop), reads, writes, inc)

        def tr(out, in_, ident, reads, writes, inc):
            S.op("pe", lambda h: h.transpose(out, in_, ident), reads, writes, inc)

        def act(out, in_, func, reads, writes, bias=None, scale=None):
            kw = {}
            if bias is not None:
                kw["bias"] = bias
            if scale is not None:
                kw["scale"] = scale
            S.op("act", lambda h: h.activation(out=out, in_=in_, func=func, **kw), reads, writes)

        def tt(en, out, in0, in1, op, reads, writes):
            S.op(en, lambda h: h.tensor_tensor(out=out, in0=in0, in1=in1, op=op), reads, writes)

        def ts(en, out, in0, s1, s2, op0, op1, reads, writes):
            if op1 is None:
                S.op(en, lambda h: h.tensor_scalar(out=out, in0=in0, scalar1=s1, scalar2=None, op0=op0), reads, writes)
            else:
                S.op(en, lambda h: h.tensor_scalar(out=out, in0=in0, scalar1=s1, scalar2=s2, op0=op0, op1=op1), reads, writes)

        def stt(out, in0, scalar, in1, op0, op1, reads, writes):
            S.op("dve", lambda h: h.scalar_tensor_tensor(out=out, in0=in0, scalar=scalar, in1=in1, op0=op0, op1=op1), reads, writes)

        def cp(en, out, in_, reads, writes):
            if en == "act":
                S.op("act", lambda h: h.activation(out=out, in_=in_, func=AF.Copy), reads, writes)
            else:
                S.op(en, lambda h: h.tensor_copy(out=out, in_=in_), reads, writes)

        def ttr(out, in0, in1, accum, reads, writes):
            S.op("act", lambda h: h.activation(out=out, in_=in0, func=AF.Square, accum_out=accum), reads, writes)

        def rstd_from_ss(ss_ap, out_ap, n, width, reads_buf, out_buf):
            ts("dve", out_ap, ss_ap, 1.0 / n, EPS, ALU.mult, ALU.add, [reads_buf], [out_buf])
            act(out_ap, out_ap, AF.Sqrt, [out_buf], [out_buf])
            S.op("dve", lambda h: h.reciprocal(out=out_ap, in_=out_ap), [out_buf], [out_buf])

        cF = lambda a, n: cst[:, a:a + n]
        identF = cF(K_ID, 128)
        triT = cF(K_TRI, 128)
        triG = cF(K_TRIG, 128)
        CST = B["cst"]

        wstate = {"next": 0}

        def wload(name, kc0, nkc, c0, ncols):
            i = wstate["next"]
            wstate["next"] = (i + 1) % NW
            t = wpool[i]
            bw = B["wp%d" % i]
            src = wbf[name].rearrange("(kc p) n -> p kc n", p=128)[:, kc0:kc0 + nkc, c0:c0 + ncols]
            dst = t[:, 0:nkc * ncols].rearrange("p (kc n) -> p kc n", n=ncols)
            S.dma("sp", dst, src, [wbuf[name]], [bw], "wp%d" % i)
            return dst, bw

        S.dma("sp", cst[:], consts_d, [], [CST], "cst")
        S.op("pool", lambda h: h.memset(onesF[:], 1.0), [], [B["onesF"]])
        S.op("pool", lambda h: h.memset(negc[:], -1.0 / 16.0), [], [B["negc"]])
        S.op("pool", lambda h: h.memset(Sg[:], 0.0), [], [B["Sg"]])
        S.op("pool", lambda h: h.memset(Sgb[:], 0.0), [], [B["Sgb"]])
        S.op("pool", lambda h: h.memset(Hs[:], 0.0), [], [B["Hs"]])
        S.op("pool", lambda h: h.memset(Hsb[:], 0.0), [], [B["Hsb"]])
        S.op("pool", lambda h: h.memset(halo_x[:], 0.0), [], [B["halo_x"]])
        S.op("pool", lambda h: h.memset(halo_f[:], 0.0), [], [B["halo_f"]])
        cp("dve", identB[:], identF, [CST], [B["identB"]])
        act(A_bc[:], cF(K_ALOG, 32), AF.Exp, [CST], [B["A_bc"]])
        ts("dve", A_bc[:], A_bc[:], -1.0, None, ALU.mult, None, [B["A_bc"]], [B["A_bc"]])

        cvt_engs = ["dve", "act", "pool"]
        cvt_in = [xs[i][:].rearrange("p a b -> p (a b)") for i in range(2)]
        cvt_in_b = [B["xs0"], B["xs1"]]
        cvt_out = [Xtm, Xdt]
        cvt_out_b = [B["Xtm"], B["Xdt"]]
        ci = 0
        for name, wd in wsrc.items():
            K, N = wd.shape
            for r0 in range(0, K, 128):
                for c0 in range(0, N, 2048):
                    ncol = min(2048, N - c0)
                    j = ci % 2
                    S.dma("sp", cvt_in[j][:, 0:ncol], wd[r0:r0 + 128, c0:c0 + ncol], [], [cvt_in_b[j]], "cvi%d" % j)
                    cp(cvt_engs[ci % 3], cvt_out[j][:, 0:ncol], cvt_in[j][:, 0:ncol], [cvt_in_b[j]], [cvt_out_b[j]])
                    S.dma("pool", wbf[name][r0:r0 + 128, c0:c0 + ncol], cvt_out[j][:, 0:ncol], [cvt_out_b[j]], [], "cvo%d" % j)
                    ci += 1
        cv_done = [Tok(S.dma_sems["cvo%d" % j][0], S.dma_sems["cvo%d" % j][1]) for j in range(2)]
        for name in wsrc:
            wbuf[name].w = cv_done

        def norm_transpose(xsrc, xbuf, gain_col0, ti, mask_col=None):
            ttr(junk[:], xsrc, xsrc, st4[:, 0:1], [xbuf], [B["junk"], B["st4"]])
            rstd_from_ss(st4[:, 0:1], rs4[:, 0:1], D, 1, B["st4"], B["rs4"])
            if mask_col is not None:
                tt("dve", rs4[:, 0:1], rs4[:, 0:1], mask_col, ALU.mult, [B["rs4"], CST], [B["rs4"]])
            act(tmpA[:], xsrc, AF.Copy, [xbuf, B["rs4"]], [B["tmpA"]], scale=rs4[:, 0:1])
            for hb in range(2):
                pi = psum_f()
                pt, pb = psF[pi]
                for c in range(4):
                    cc = hb * 4 + c
                    tr(pt[:, c * 128:(c + 1) * 128], tmpA[:, cc * 128:(cc + 1) * 128], identF, [B["tmpA"], CST], [pb], c == 3)
                g = cst[:, gain_col0 + hb * 4:gain_col0 + hb * 4 + 4].unsqueeze(2).to_broadcast([128, 4, 128])
                tt("dve", hT[:, hb * 4:hb * 4 + 4, ti * 128:(ti + 1) * 128], pt[:].rearrange("p (c t) -> p c t", c=4), g, ALU.mult,
                   [pb, CST], [B["hT"]])
                free_f(pi)

        def tm_group(wname, c0, ncols, evac):
            wt, bw = wload(wname, 0, 8, c0, ncols)
            for ti in range(NT):
                pi = psum_f()
                pt, pb = psF[pi]
                for k in range(8):
                    mm(pt[:, 0:ncols], hT[:, k, ti * 128:(ti + 1) * 128], wt[:, k, :], k == 0, k == 7, [B["hT"], bw], [pb], k == 7)
                evac(ti, pt, pb)
                free_f(pi)

        def fm_group(wname, c0, nch, evac, src=None, srcbuf=None, nk=8):
            src = hT if src is None else src
            srcbuf = B["hT"] if srcbuf is None else srcbuf
            wt, bw = wload(wname, 0, nk, c0, nch * 128)
            for j in range(nch):
                pi = psum_f()
                pt, pb = psF[pi]
                for k in range(nk):
                    mm(pt[:, 0:TB], wt[:, k, j * 128:(j + 1) * 128], src[:, k, :], k == 0, k == nk - 1, [srcbuf, bw], [pb], k == nk - 1)
                evac(j, pt, pb)
                free_f(pi)

        conv_i = {"i": 0}

        def conv_chunk(pt, pb, TBn, ntap, halo, halo_buf, cidx, wcol0, bcol, func, outap, outbuf, scale_after=None):
            j = conv_i["i"] % 3
            ja = conv_i["i"] % 2
            conv_i["i"] += 1
            hl = ntap - 1
            pr, pbuf = pre[j], B["pre%d" % j]
            ac, abuf = acc[ja], B["acc%d" % ja]
            cp("pool", pr[:, 0:hl], halo[:, cidx, :], [halo_buf], [pbuf])
            cp("act", pr[:, hl:hl + TB], pt[:, 0:TB], [pb], [pbuf])
            cp("pool", halo[:, cidx, :], pr[:, TB:TB + hl], [pbuf], [halo_buf])
            for k in range(ntap):
                wk = cst[:, wcol0 + cidx * ntap + k:wcol0 + cidx * ntap + k + 1]
                if k == 0:
                    ts("dve", ac[:], pr[:, 0:TB], wk, None, ALU.mult, None, [pbuf, CST], [abuf])
                else:
                    stt(ac[:], pr[:, k:k + TB], wk, ac[:], ALU.mult, ALU.add, [pbuf, CST, abuf], [abuf])
            act(outap, ac[:], func, [abuf, CST], [outbuf], bias=cst[:, bcol + cidx:bcol + cidx + 1])

        def do_block(bi, full):
            xsb = xs[bi % 2]
            xbuf = B["xs%d" % (bi % 2)]
            t0 = bi * NT
            for ti in range(NT):
                S.dma("sp", xsb[:, ti, :], xw[(t0 + ti) * 128:(t0 + ti + 1) * 128, :], [], [xbuf], "xs%d" % (bi % 2))
            for ti in range(NT):
                norm_transpose(xsb[:, ti, :], xbuf, K_MIX, ti)


            for (dc0, sc0, n) in ((0, C_ALR, 16), (16, C_DT, 32)):
                S.dma("sp", wsm[:, :, dc0:dc0 + n], wbf["in"].rearrange("(kc p) n -> p kc n", p=128)[:, :, sc0:sc0 + n],
                      [wbuf["in"]], [B["wsm"]], "wsm")
            for ti in range(NT):
                pi = psum_f()
                pt, pb = psF[pi]
                for (dc0, n) in ((0, 16), (16, 32)):
                    for k in range(8):
                        mm(pt[:, dc0:dc0 + n], hT[:, k, ti * 128:(ti + 1) * 128], wsm[:, k, dc0:dc0 + n], k == 0, k == 7,
                           [B["hT"], B["wsm"]], [pb], k == 7)
                cp("act", sm_a_t[ti][:], pt[:, 0:48], [pb], [B["sm_a%d" % ti]])
                free_f(pi)

            def gates_chain(ti):
                tg = t0 + ti
                sm_a, SMA = sm_a_t[ti], B["sm_a%d" % ti]
                sm_b, SMB = sm_b_t[ti], B["sm_b%d" % ti]
                alrT, ALRT = alrT_t[ti], B["alrT%d" % ti]
                acs_sb, ACS = acs_t[ti], B["acs%d" % ti]
                l1, L1 = l1_t[ti], l1_b[ti]
                tt("dve", sm_b[:], sm_a[:, 16:48], cF(K_DTB, 32), ALU.add, [SMA, CST], [SMB])
                pi = psum_f()
                pt, pb = psF[pi]
                tr(pt[0:16, 0:128], sm_a[:, 0:16], identF, [SMA, CST], [pb], True)
                cp("act", alrT[:], pt[0:16, 0:128], [pb], [ALRT])
                free_f(pi)
                act(sm_b[:], sm_b[:], AF.Exp, [SMB], [SMB])
                yield
                pi = psum_f()
                pt, pb = psF[pi]
                mm(pt[:, 0:512], alrT[:], cst[0:16, K_WG:K_WG + 512], True, False, [ALRT, CST], [pb], False)
                mm(pt[:, 0:512], onesF[0:1, :], cst[0:1, K_BG:K_BG + 512], False, True, [B["onesF"], CST], [pb], True)
                act(l1[:], pt[:, 0:512], AF.Exp, [pb], [L1], scale=-1.0)
                free_f(pi)
                act(dtv[:, ti, :], sm_b[:], AF.Ln, [SMB], [B["dtv"]], bias=1.0)
                yield
                act(l1[:], l1[:], AF.Ln, [L1], [L1], bias=1.0)
                ts("dve", dtv[:, ti, :], dtv[:, ti, :], cst[:, K_MASK + tg:K_MASK + tg + 1], None, ALU.mult, None, [B["dtv"], CST], [B["dtv"]])
                tt("dve", dav[:, ti, :], dtv[:, ti, :], A_bc[:], ALU.mult, [B["dtv"], B["A_bc"]], [B["dav"]])
                yield
                pi = psum_f()
                pt, pb = psF[pi]
                mm(pt[:, 0:512], triG, l1[:], True, True, [CST, L1], [pb], True)
                cp("act", b_sb[:, ti, :], pt[:, 0:512], [pb], [B["b_sb"]])
                free_f(pi)
                yield
                pi = psum_f()
                pt, pb = psF[pi]
                for h in range(4):
                    mm(pt[:, h:h + 1], l1[:, h * 128:(h + 1) * 128], negc[:, 0:1], True, True, [L1, B["negc"]], [pb], h == 3)
                mm(pt[:, 32:64], triT, dav[:, ti, :], True, True, [CST, B["dav"]], [pb], False)
                mm(pt[:, 64:96], onesF[:], dav[:, ti, :], True, True, [B["onesF"], B["dav"]], [pb], True)
                act(ebl[:, ti, :], pt[:, 0:4], AF.Exp, [pb], [B["ebl"]])
                cp("act", acs_sb[:], pt[:, 32:64], [pb], [ACS])
                act(eacs[:, ti, :], pt[:, 32:64], AF.Exp, [pb], [B["eacs"]])
                act(cdbc[:, ti, :], pt[:, 64:96], AF.Exp, [pb], [B["cdbc"]])
                ts("dve", nacs[:, ti, :], acs_sb[:], -1.0, None, ALU.mult, None, [ACS], [B["nacs"]])
                tt("dve", sm_b[:], pt[:, 64:96], acs_sb[:], ALU.subtract, [pb, ACS], [SMB])
                free_f(pi)
                yield
                act(sm_b[:], sm_b[:], AF.Exp, [SMB], [SMB])
                tt("dve", dtd[:, ti, :], dtv[:, ti, :], sm_b[:], ALU.mult, [B["dtv"], SMB], [B["dtd"]])

            fillers = []
            for hf in range(2):
                fillers.append(lambda hf=hf: tm_group("in", C_V + hf * 512, 512,
                               lambda ti, pt, pb: cp("act", v_sb[:, ti, hf * 512:(hf + 1) * 512], pt[:, 0:512], [pb], [B["v_sb"]])))
            nxt = 6 if full else 5
            for g in range(nxt):
                def ev_x(j, pt, pb, g=g):
                    c = g * 4 + j
                    conv_chunk(pt, pb, TB, 4, halo_x, B["halo_x"], c, K_WC, K_BC, AF.Silu, xbcT[:, c, :], B["xbcT"])
                fillers.append(lambda g=g, ev_x=ev_x: fm_group("in", C_XBC + g * 512, 4, ev_x))
            if full:
                for hf in range(2):
                    fillers.append(lambda hf=hf: tm_group("in", C_G + hf * 512, 512,
                                   lambda ti, pt, pb: act(sg[:, ti, hf * 512:(hf + 1) * 512], pt[:, 0:512], AF.Silu, [pb], [B["sg"]])))
                for qd in range(4):
                    fillers.append(lambda qd=qd: tm_group("in", C_Z + qd * 512, 512,
                                   lambda ti, pt, pb: act(sz[:, ti, qd * 512:(qd + 1) * 512], pt[:, 0:512], AF.Silu, [pb], [B["sz"]])))
                for g in range(4):
                    def ev_g(j, pt, pb, g=g):
                        c = g * 4 + j
                        act(sgab[:, c, :], pt[:, 0:TB], AF.Sigmoid, [pb], [B["sgab"]])
                    fillers.append(lambda g=g, ev_g=ev_g: fm_group("in", C_GA + g * 512, 4, ev_g))

            chains = [gates_chain(ti) for ti in range(NT)]
            while chains:
                for gch in list(chains):
                    try:
                        next(gch)
                    except StopIteration:
                        chains.remove(gch)
                if fillers:
                    fillers.pop(0)()
            def ev_q(ti, pt, pb):
                act(tmpA[:, 0:512], b_sb[:, ti, :], AF.Exp, [B["b_sb"]], [B["tmpA"]])
                stt(qt[:, ti, :], pt[:, 0:512], 128.0 ** -0.5, tmpA[:, 0:512], ALU.mult, ALU.mult, [pb, B["tmpA"]], [B["qt"]])

            def ev_k(ti, pt, pb):
                act(tmpB[:, 0:512], b_sb[:, ti, :], AF.Exp, [B["b_sb"]], [B["tmpB"]], scale=-1.0)
                tt("dve", kt[:, ti, :], pt[:, 0:512], tmpB[:, 0:512], ALU.mult, [pb, B["tmpB"]], [B["kt"]])

            tm_group("in", C_K, 512, ev_k)
            if full:
                tm_group("in", C_Q, 512, ev_q)
            for f in fillers:
                f()

            def gla_gen(ti):
                tsl = slice(ti * 128, (ti + 1) * 128)
                bi2 = psum_b()
                ptb, pbb = psB[bi2]
                for h in range(4):
                    tr(ptb[:, h * 128:(h + 1) * 128], kt[:, ti, h * 128:(h + 1) * 128], identB[:], [B["kt"], B["identB"]], [pbb], h == 3)
                cp("act", kT[:], ptb[:, 0:512].rearrange("p (h t) -> p h t", h=4), [pbb], [B["kT"]])
                free_b(bi2)
                yield
                if full:
                    bi2 = psum_b()
                    ptb, pbb = psB[bi2]
                    for h in range(4):
                        tr(ptb[:, h * 128:(h + 1) * 128], qt[:, ti, h * 128:(h + 1) * 128], identB[:], [B["qt"], B["identB"]], [pbb], h == 3)
                    cp("act", qT[:], ptb[:, 0:512].rearrange("p (h t) -> p h t", h=4), [pbb], [B["qT"]])
                    free_b(bi2)
                    yield
                    pi = psum_f()
                    pt, pb = psF[pi]
                    for h in range(4):
                        mm(pt[:, h * 128:(h + 1) * 128], kT[:, h, :], qT[:, h, :], True, True, [B["kT"], B["qT"]], [pb], h == 3)
                    tt("dve", attm[:], pt[:].rearrange("p (h t) -> p h t", h=4), triT.unsqueeze(1).to_broadcast([128, 4, 128]), ALU.mult,
                       [pb, CST], [B["attm"]])
                    free_f(pi)
                    yield
                    pos = [psum_f(), psum_f()]
                    for h in range(4):
                        pt, pb = psF[pos[h // 2]]
                        o_ap = pt[:, (h % 2) * 256:(h % 2) * 256 + 256]
                        mm(o_ap, attm[:, h, :], v_sb[:, ti, h * 256:(h + 1) * 256], True, False, [B["attm"], B["v_sb"]], [pb], False)
                        mm(o_ap, qT[:, h, :], Sgb[:, h, :], False, True, [B["qT"], B["Sgb"]], [pb], True)
                    for b2 in range(2):
                        pt, pb = psF[pos[b2]]
                        cp("act", tmpA[:, b2 * 512:(b2 + 1) * 512], pt[:, 0:512], [pb], [B["tmpA"]])
                        free_f(pos[b2])
                    yield
                    for h in range(4):
                        o_ap = tmpA[:, h * 256:(h + 1) * 256]
                        ttr(junk[:, 0:256], o_ap, o_ap, st4[:, h:h + 1], [B["tmpA"]], [B["junk"], B["st4"]])
                    rstd_from_ss(st4[:, 0:4], rs4[:, 0:4], 256, 4, B["st4"], B["rs4"])
                    yield
                    for h in range(4):
                        o_ap = tmpA[:, h * 256:(h + 1) * 256]
                        stt(tmpB[:, h * 256:(h + 1) * 256], o_ap, rs4[:, h:h + 1], sg[:, ti, h * 256:(h + 1) * 256], ALU.mult, ALU.mult,
                            [B["tmpA"], B["rs4"], B["sg"]], [B["tmpB"]])
                    yield
                    for hb in range(2):
                        pi = psum_f()
                        pt, pb = psF[pi]
                        for c in range(4):
                            cc = hb * 4 + c
                            tr(pt[:, c * 128:(c + 1) * 128], tmpB[:, cc * 128:(cc + 1) * 128], identF, [B["tmpB"], CST], [pb], c == 3)
                        g = cst[:, K_HN + hb * 4:K_HN + hb * 4 + 4].unsqueeze(2).to_broadcast([128, 4, 128])
                        tt("dve", oaT[:, hb * 4:hb * 4 + 4, tsl], pt[:].rearrange("p (c t) -> p c t", c=4), g, ALU.mult, [pb, CST], [B["oaT"]])
                        free_f(pi)
                    yield
                pds = [psum_f(), psum_f()]
                for h in range(4):
                    pt, pb = psF[pds[h // 2]]
                    mm(pt[:, (h % 2) * 256:(h % 2) * 256 + 256], kt[:, ti, h * 128:(h + 1) * 128], v_sb[:, ti, h * 256:(h + 1) * 256], True, True,
                       [B["kt"], B["v_sb"]], [pb], h % 2 == 1)
                for b2 in range(2):
                    pt, pb = psF[pds[b2]]
                    sl = Sg[:, 2 * b2:2 * b2 + 2, :]
                    tt("dve", sl, pt[:].rearrange("p (h v) -> p h v", h=2), sl, ALU.add, [pb, B["Sg"]], [B["Sg"]])
                    tt("dve", sl, sl, ebl[:, ti, 2 * b2:2 * b2 + 2].unsqueeze(2).to_broadcast([128, 2, 256]), ALU.mult, [B["Sg"], B["ebl"]], [B["Sg"]])
                    free_f(pds[b2])
                cp("pool", Sgb[:], Sg[:], [B["Sg"]], [B["Sgb"]])

            def ssd_gen(ti):
                tsl = slice(ti * 128, (ti + 1) * 128)
                for hb in range(2):
                    bi2 = psum_b()
                    ptb, pbb = psB[bi2]
                    for c in range(8):
                        cc = hb * 8 + c
                        tr(ptb[:, c * 128:(c + 1) * 128], xbcT[:, cc, tsl], identB[:], [B["xbcT"], B["identB"]], [pbb], c == 7)
                    cp("act", Xtm[:, hb * 1024:(hb + 1) * 1024], ptb[:, 0:1024], [pbb], [B["Xtm"]])
                    free_b(bi2)
                    yield
                bi2 = psum_b()
                ptb, pbb = psB[bi2]
                for c in range(4):
                    tr(ptb[:, c * 128:(c + 1) * 128], xbcT[:, 16 + c, tsl], identB[:], [B["xbcT"], B["identB"]], [pbb], c == 3)
                cp("act", Btm[:], ptb[:, 0:512], [pbb], [B["Btm"]])
                free_b(bi2)
                X3 = Xtm[:].rearrange("p (h d) -> p h d", d=64)
                tt("pool", Xd[:].rearrange("p (h d) -> p h d", d=64), X3, dtd[:, ti, :].unsqueeze(2).to_broadcast([128, 32, 64]), ALU.mult,
                   [B["Xtm"], B["dtd"]], [B["Xd"]])
                yield
                if full:
                    tt("pool", Xdt[:].rearrange("p (h d) -> p h d", d=64), X3, dtv[:, ti, :].unsqueeze(2).to_broadcast([128, 32, 64]), ALU.mult,
                       [B["Xtm"], B["dtv"]], [B["Xdt"]])
                    pi = psum_f()
                    pt, pb = psF[pi]
                    for g in range(4):
                        mm(pt[:, g * 128:(g + 1) * 128], xbcT[:, 16 + g, tsl], xbcT[:, 20 + g, tsl], True, True, [B["xbcT"]], [pb], g == 3)
                    tt("dve", scm[:], pt[:].rearrange("p (g t) -> p g t", g=4), triT.unsqueeze(1).to_broadcast([128, 4, 128]), ALU.mult,
                       [pb, CST], [B["scm"]])
                    free_f(pi)
                    yield

                    def stage_a(g, hh2):
                        j = g % 2
                        jz = (2 * g + hh2) % 2
                        h0 = 8 * g + 4 * hh2
                        tt("pool", Zg[jz][:], triT.unsqueeze(1).to_broadcast([128, 4, 128]),
                           dav[:, ti, h0:h0 + 4].unsqueeze(2).to_broadcast([128, 4, 128]), ALU.mult, [CST, B["dav"]], [B["Zg%d" % jz]])
                        pi = psum_f()
                        pt, pb = psF[pi]
                        mm(pt[:, 0:512], onesF[:], Zg[jz][:].rearrange("p h t -> p (h t)"), True, True,
                           [B["onesF"], B["Zg%d" % jz]], [pb], True)
                        for h4 in range(4):
                            hd = h0 + h4
                            ts("dve", Eg[jz][:, h4, :], pt[:, h4 * 128:(h4 + 1) * 128], nacs[:, ti, hd:hd + 1], 0.0, ALU.add, ALU.min,
                               [pb, B["nacs"]], [B["Eg%d" % jz]])
                        free_f(pi)
                        act(Eg[jz][:], Eg[jz][:], AF.Exp, [B["Eg%d" % jz]], [B["Eg%d" % jz]])
                        tt("dve", MTg[j][:, 4 * hh2:4 * hh2 + 4, :], Eg[jz][:], scm[:, g, :].unsqueeze(1).to_broadcast([128, 4, 128]), ALU.mult,
                           [B["Eg%d" % jz], B["scm"]], [B["MTg%d" % j]])

                    def stage_b(g):
                        j = g % 2
                        pyd = psum_f()
                        pt, pb = psF[pyd]
                        for hh in range(8):
                            hd = 8 * g + hh
                            mm(pt[:, hh * 64:(hh + 1) * 64], MTg[j][:, hh, :], Xdt[:, hd * 64:(hd + 1) * 64], True, True,
                               [B["MTg%d" % j], B["Xdt"]], [pb], hh == 7)
                        pyo = psum_f()
                        pt2, pb2 = psF[pyo]
                        mm(pt2[:, 0:512], xbcT[:, 20 + g, tsl], Hsb[:, g * 512:(g + 1) * 512], True, True, [B["xbcT"], B["Hsb"]], [pb2], True)
                        e_bc = eacs[:, ti, 8 * g:8 * g + 8].unsqueeze(2).to_broadcast([128, 8, 64])
                        tt("dve", tmpC[:].rearrange("p (h d) -> p h d", d=64), pt2[:].rearrange("p (h d) -> p h d", d=64), e_bc, ALU.mult,
                           [pb2, B["eacs"]], [B["tmpC"]])
                        tt("dve", tmpC[:], pt[:, 0:512], tmpC[:], ALU.add, [pb, B["tmpC"]], [B["tmpC"]])
                        free_f(pyd)
                        free_f(pyo)
                        d_bc = cst[:, K_DSK + 8 * g:K_DSK + 8 * g + 8].unsqueeze(2).to_broadcast([128, 8, 64])
                        tt("pool", tmpD[:].rearrange("p (h d) -> p h d", d=64), Xtm[:, g * 512:(g + 1) * 512].rearrange("p (h d) -> p h d", d=64),
                           d_bc, ALU.mult, [B["Xtm"], CST], [B["tmpD"]])
                        tt("dve", tmpC[:], tmpC[:], tmpD[:], ALU.add, [B["tmpC"], B["tmpD"]], [B["tmpC"]])
                        tt("dve", yz[:, g * 512:(g + 1) * 512], tmpC[:], sz[:, ti, g * 512:(g + 1) * 512], ALU.mult, [B["tmpC"], B["sz"]], [B["yz"]])
                        ttr(junk2[:, 0:512], yz[:, g * 512:(g + 1) * 512], yz[:, g * 512:(g + 1) * 512], st4s[:, g:g + 1],
                            [B["yz"]], [B["junk2"], B["st4s"]])

                    order = [("a", 0), ("a", 1), ("b", 0), ("a", 2), ("b", 1), ("a", 3), ("b", 2), ("b", 3)]
                    for kind, g in order:
                        if kind == "a":
                            stage_a(g, 0)
                            yield
                            stage_a(g, 1)
                            yield
                        else:
                            stage_b(g)
                            yield
                    rstd_from_ss(st4s[:, 0:4], rs4s[:, 0:4], 512, 4, B["st4s"], B["rs4s"])
                    tt("dve", yz[:].rearrange("p (g c) -> p g c", g=4), yz[:].rearrange("p (g c) -> p g c", g=4),
                       rs4s[:, 0:4].unsqueeze(2).to_broadcast([128, 4, 512]), ALU.mult, [B["yz"], B["rs4s"]], [B["yz"]])
                    yield
                    for hb in range(4):
                        pi = psum_f()
                        pt, pb = psF[pi]
                        for c in range(4):
                            cc = hb * 4 + c
                            tr(pt[:, c * 128:(c + 1) * 128], yz[:, cc * 128:(cc + 1) * 128], identF, [B["yz"], CST], [pb], c == 3)
                        g = cst[:, K_SSM + hb * 4:K_SSM + hb * 4 + 4].unsqueeze(2).to_broadcast([128, 4, 128])
                        tt("dve", ynT[:, hb * 4:hb * 4 + 4, tsl], pt[:].rearrange("p (c t) -> p c t", c=4), g, ALU.mult, [pb, CST], [B["ynT"]])
                        free_f(pi)
                        yield
                for g in range(4):
                    pi = psum_f()
                    pt, pb = psF[pi]
                    mm(pt[:, 0:512], Btm[:, g * 128:(g + 1) * 128], Xd[:, g * 512:(g + 1) * 512], True, True, [B["Btm"], B["Xd"]], [pb], True)
                    hsl = Hs[:, g * 512:(g + 1) * 512]
                    c_bc = cdbc[:, ti, 8 * g:8 * g + 8].unsqueeze(2).to_broadcast([128, 8, 64])
                    tt("dve", hsl.rearrange("p (h d) -> p h d", d=64), hsl.rearrange("p (h d) -> p h d", d=64), c_bc, ALU.mult,
                       [B["Hs"], B["cdbc"]], [B["Hs"]])
                    tt("dve", hsl, hsl, pt[:, 0:512], ALU.add, [B["Hs"], pb], [B["Hs"]])
                    free_f(pi)
                    yield
                cp("pool", Hsb[:], Hs[:], [B["Hs"]], [B["Hsb"]])

            for ti in range(NT):
                gens = [ssd_gen(ti), gla_gen(ti)]
                while gens:
                    for gg in list(gens):
                        try:
                            next(gg)
                        except StopIteration:
                            gens.remove(gg)

            if not full:
                return

            for nh in range(2):
                wa_t, wa_b = wload("a", 0, 8, nh * 512, 512)
                wb0_t, wb0_b = wload("b", 0, 8, nh * 512, 512)
                wb1_t, wb1_b = wload("b", 8, 8, nh * 512, 512)
                for j in range(4):
                    n = nh * 4 + j
                    pa = psum_f()
                    pta, pba = psF[pa]
                    for k in range(8):
                        mm(pta[:, 0:TB], wa_t[:, k, j * 128:(j + 1) * 128], oaT[:, k, :], k == 0, k == 7, [wa_b, B["oaT"]], [pba], k == 7)
                    pbk = psum_f()
                    ptb_, pbb_ = psF[pbk]
                    for k in range(16):
                        wt_, wb_ = (wb0_t, wb0_b) if k < 8 else (wb1_t, wb1_b)
                        mm(ptb_[:, 0:TB], wt_[:, k % 8, j * 128:(j + 1) * 128], ynT[:, k, :], k == 0, k == 15, [wb_, B["ynT"]], [pbb_], k % 8 == 7)
                    tt("dve", tmpC[:, 0:TB], pta[:, 0:TB], sgab[:, n, :], ALU.mult, [pba, B["sgab"]], [B["tmpC"]])
                    tt("dve", tmpD[:, 0:TB], ptb_[:, 0:TB], sgab[:, 8 + n, :], ALU.mult, [pbb_, B["sgab"]], [B["tmpD"]])
                    tt("pool", mT[:, n, :], tmpC[:, 0:TB], tmpD[:, 0:TB], ALU.add, [B["tmpC"], B["tmpD"]], [B["mT"]])
                    free_f(pa)
                    free_f(pbk)
            for hf in range(2):
                wt, bw = wload("out", 0, 8, hf * 512, 512)
                for ti in range(NT):
                    pi = psum_f()
                    pt, pb = psF[pi]
                    for k in range(8):
                        mm(pt[:, 0:512], mT[:, k, ti * 128:(ti + 1) * 128], wt[:, k, :], k == 0, k == 7, [B["mT"], bw], [pb], k == 7)
                    xsl = xsb[:, ti, hf * 512:(hf + 1) * 512]
                    tt("dve", xsl, xsl, pt[:, 0:512], ALU.add, [xbuf, pb], [xbuf])
                    free_f(pi)

            for ti in range(NT):
                tg = t0 + ti
                norm_transpose(xsb[:, ti, :], xbuf, K_FFN, ti, mask_col=cst[:, K_MASK + tg:K_MASK + tg + 1])
            gT = xbcT
            GT = B["xbcT"]
            fchunks = [(0, 4), (4, 4), (8, 4), (12, 4), (16, 4), (20, 2)]
            for (f0, nf) in fchunks:
                wa_t, wa_b = wload("up", 0, 8, f0 * 128, nf * 128)
                wl_t, wl_b = wload("up", 0, 8, FFN_H + f0 * 128, nf * 128)
                for j in range(nf):
                    f = f0 + j
                    pa = psum_f()
                    pta, pba = psF[pa]
                    for k in range(8):
                        mm(pta[:, 0:TB], wa_t[:, k, j * 128:(j + 1) * 128], hT[:, k, :], k == 0, k == 7, [wa_b, B["hT"]], [pba], k == 7)
                    pl = psum_f()
                    ptl, pbl = psF[pl]
                    for k in range(8):
                        mm(ptl[:, 0:TB], wl_t[:, k, j * 128:(j + 1) * 128], hT[:, k, :], k == 0, k == 7, [wl_b, B["hT"]], [pbl], k == 7)
                    conv_chunk(pta, pba, TB, 3, halo_f, B["halo_f"], f, K_WFC, K_BFC, AF.Gelu_apprx_tanh, tmpC[:, 0:TB], B["tmpC"])
                    tt("dve", gT[:, f, :], tmpC[:, 0:TB], ptl[:, 0:TB], ALU.mult, [B["tmpC"], pbl], [GT])
                    free_f(pa)
                    free_f(pl)
            kparts = [(0, 8), (8, 8), (16, 6)]
            for hf in range(2):
                pis = [psum_f() for _ in range(NT)]
                for kp, (k0, nk) in enumerate(kparts):
                    wt, bw = wload("down", k0, nk, hf * 512, 512)
                    for ti in range(NT):
                        pt, pb = psF[pis[ti]]
                        for k in range(nk):
                            mm(pt[:, 0:512], gT[:, k0 + k, ti * 128:(ti + 1) * 128], wt[:, k, :], kp == 0 and k == 0, kp == 2 and k == nk - 1,
                               [GT, bw], [pb], k == nk - 1)
                for ti in range(NT):
                    pt, pb = psF[pis[ti]]
                    xsl = xsb[:, ti, hf * 512:(hf + 1) * 512]
                    tt("dve", xsl, xsl, pt[:, 0:512], ALU.add, [xbuf, pb], [xbuf])
                    free_f(pis[ti])

            fb = bi - n_sblk
            for ti in range(NT):
                norm_transpose(xsb[:, ti, :], xbuf, K_PLE, ti)
                S.dma("sp", p_in[:], pw[(fb * NT + ti) * 128:(fb * NT + ti + 1) * 128, :], [], [B["p_in"]], "p_in")
                pi = psum_f()
                pt, pb = psF[pi]
                for c in range(2):
                    tr(pt[:, c * 128:(c + 1) * 128], p_in[:, c * 128:(c + 1) * 128], identF, [B["p_in"], CST], [pb], c == 1)
                cp("act", pT[:, :, ti * 128:(ti + 1) * 128], pt[:, 0:256].rearrange("p (c t) -> p c t", c=2), [pb], [B["pT"]])
                free_f(pi)
            wpp_t, wpp_b = wload("pp", 0, 2, 0, 1024)
            for hf in range(2):
                wt, bw = wload("pg", 0, 8, hf * 512, 512)
                for ti in range(NT):
                    pi = psum_f()
                    pt, pb = psF[pi]
                    for k in range(8):
                        mm(pt[:, 0:512], hT[:, k, ti * 128:(ti + 1) * 128], wt[:, k, :], k == 0, k == 7, [B["hT"], bw], [pb], k == 7)
                    act(tmpA[:, 0:512], pt[:, 0:512], AF.Sigmoid, [pb], [B["tmpA"]])
                    free_f(pi)
                    pi = psum_f()
                    pt, pb = psF[pi]
                    for k in range(2):
                        mm(pt[:, 0:512], pT[:, k, ti * 128:(ti + 1) * 128], wpp_t[:, k, hf * 512:(hf + 1) * 512], k == 0, k == 1,
                           [B["pT"], wpp_b], [pb], k == 1)
                    tt("dve", tmpB[:, 0:512], tmpA[:, 0:512], pt[:, 0:512], ALU.mult, [B["tmpA"], pb], [B["tmpB"]])
                    free_f(pi)
                    xsl = xsb[:, ti, hf * 512:(hf + 1) * 512]
                    tt("dve", xsl, xsl, tmpB[:, 0:512], ALU.add, [xbuf, B["tmpB"]], [xbuf])

            for ti in range(NT):
                tg = t0 + ti
                ot = tg - (NTILES - n_out_tiles)
                if ot < 0:
                    continue
                xsrc = xsb[:, ti, :]
                ttr(junk[:], xsrc, xsrc, st4[:, 0:1], [xbuf], [B["junk"], B["st4"]])
                rstd_from_ss(st4[:, 0:1], rs4[:, 0:1], D, 1, B["st4"], B["rs4"])
                ob = out_sb[0]
                obuf = B["out_sb0"]
                stt(ob[:], xsrc, rs4[:, 0:1], cF(K_FIN, 1024), ALU.mult, ALU.mult, [xbuf, B["rs4"], CST], [obuf])
                S.dma("pool", out_d[ot * 128:(ot + 1) * 128, :], ob[:], [obuf], [], "out_sb0")

        for bi in range(NBLK):
            do_block(bi, bi >= n_sblk)

        if debug is not None:
            debug_fn = build_program.debug_fn
            debug_fn(S, B, locals(), dbg_d)

        for key, ent in S.dma_sems.items():
            if key.startswith("out_sb") or key == "dbg":
                S.wait_tok("pool", Tok(ent[0], ent[1]))

        block = es.enter_context(nc.Block())
        S.emit(block)
    return nc


build_program.debug_fn = None


def make_consts(inp, mask, ntiles_cols=64):
    c = np.zeros((128, 2848), np.float32)
    c[:, K_ID:K_ID + 128] = np.eye(128, dtype=np.float32)
    tri = np.triu(np.ones((128, 128), np.float32))
    c[:, K_TRI:K_TRI + 128] = tri
    c[:, K_TRIG:K_TRIG + 128] = tri * np.float32(-1.0 / 16.0)
    c[0:16, K_WG:K_WG + 512] = inp["w_gla_gate"][0]
    c[0:1, K_BG:K_BG + 512] = inp["b_gla_gate"][0][None, :]
    c[:, K_FIN:K_FIN + 1024] = np.broadcast_to(inp["final_norm"][None, :], (128, 1024))
    colv = lambda v: np.ascontiguousarray(v.reshape(-1, 128).T)
    c[:, K_MIX:K_MIX + 8] = colv(inp["mixer_norm"][0])
    c[:, K_FFN:K_FFN + 8] = colv(inp["ffn_norm"][0])
    c[:, K_PLE:K_PLE + 8] = colv(inp["ple_norm"][0])
    c[:, K_HN:K_HN + 8] = np.tile(colv(inp["gla_norm"][0]), (1, 4))
    c[:, K_SSM:K_SSM + 16] = colv(inp["ssm_norm"][0])
    wc = inp["w_ssm_conv"][0]
    c[:, K_WC:K_WC + 96] = wc.T.reshape(24, 128, 4).transpose(1, 0, 2).reshape(128, 96)
    c[:, K_BC:K_BC + 24] = colv(inp["b_ssm_conv"][0])
    wf = inp["w_ffn_conv"][0]
    c[:, K_WFC:K_WFC + 66] = wf.T.reshape(22, 128, 3).transpose(1, 0, 2).reshape(128, 66)
    c[:, K_BFC:K_BFC + 22] = colv(inp["b_ffn_conv"][0])
    c[:, K_DTB:K_DTB + 32] = np.broadcast_to(inp["dt_bias"][0][None, :], (128, 32))
    c[:, K_ALOG:K_ALOG + 32] = np.broadcast_to(inp["a_log"][0][None, :], (128, 32))
    c[:, K_DSK:K_DSK + 32] = np.broadcast_to(inp["d_skip"][0][None, :], (128, 32))
    nt = mask.shape[0] // 128
    c[:, K_MASK:K_MASK + nt] = mask.reshape(nt, 128).T
    return c


NT_ = 2
N_FBLK = 9
N_SBLK = 23


def kernel(**inp):
    inp = {k: np.asarray(v) for k, v in inp.items()}
    x = inp["x"]
    p = inp["p"][0]
    ntiles = (N_SBLK + N_FBLK) * NT_
    WT = ntiles * 128
    FT = N_FBLK * NT_ * 128
    nc = build_program(N_SBLK, N_FBLK, OWN // 128, NT=NT_)
    shared = {
        "w_in": np.ascontiguousarray(inp["w_in"][0]),
        "w_branch_a": np.ascontiguousarray(inp["w_branch_a"][0]),
        "w_branch_b": np.ascontiguousarray(inp["w_branch_b"][0]),
        "w_out": np.ascontiguousarray(inp["w_out"][0]),
        "w_ffn_up": np.ascontiguousarray(inp["w_ffn_up"][0]),
        "w_ffn_down": np.ascontiguousarray(inp["w_ffn_down"][0]),
        "w_ple_gate": np.ascontiguousarray(inp["w_ple_gate"][0]),
        "w_ple_proj": np.ascontiguousarray(inp["w_ple_proj"][0]),
    }
    in_maps = []
    for core in range(NCORES):
        b, j = core // 4, core % 4
        end = (j + 1) * OWN
        start = end - WT
        xwin = np.zeros((WT, D), np.float32)
        mask = np.zeros((WT,), np.float32)
        s0 = max(start, 0)
        xwin[s0 - start:] = x[b, s0:end]
        mask[s0 - start:] = 1.0
        pwin = np.zeros((FT, 256), np.float32)
        ps = end - FT
        ps0 = max(ps, 0)
        pwin[ps0 - ps:] = p[b, ps0:end]
        m = dict(shared)
        m["xw"] = xwin
        m["pw"] = pwin
        m["consts"] = make_consts(inp, mask)
        in_maps.append(m)
    res = run_bass_kernel_spmd(nc, in_maps, core_ids=list(range(NCORES)))
    out = np.zeros((BATCH, SEQ, D), np.float32)
    for core in range(NCORES):
        b, j = core // 4, core % 4
        out[b, j * OWN:(j + 1) * OWN] = np.asarray(res.results[core]["out"]).reshape(OWN, D)
    return out
```

```python
import numpy as np
from contextlib import ExitStack
import concourse.bass as bass
import concourse.mybir as mybir
from concourse.bass_utils import run_bass_kernel_spmd

F32 = mybir.dt.float32
BF16 = mybir.dt.bfloat16
AF = mybir.ActivationFunctionType
ALU = mybir.AluOpType
AX = mybir.AxisListType

D = 1024
SEQ = 8192
BATCH = 2
NCORES = 8
OWN = 2048
EPS = 1e-6
C_Q, C_K, C_V, C_G, C_ALR, C_Z, C_XBC, C_DT, C_GA, C_GB = 0, 512, 1024, 2048, 3072, 3088, 5136, 8208, 8240, 9264
IN_W = 10288
FFN_H = 2816

K_ID, K_TRI, K_TRIG, K_WG, K_BG, K_FIN = 0, 128, 256, 384, 896, 1408
K_MIX, K_FFN, K_PLE, K_HN, K_SSM = 2432, 2440, 2448, 2456, 2464
K_WC, K_BC, K_WFC, K_BFC = 2480, 2576, 2600, 2666
K_DTB, K_ALOG, K_DSK, K_MASK = 2688, 2720, 2752, 2784


class Tok:
    __slots__ = ("sem", "val")

    def __init__(self, sem, val):
        self.sem = sem
        self.val = val


class Buf:
    def __init__(self, name):
        self.name = name
        self.w = None
        self.r = []


class Eng:
    def __init__(self, name, sem):
        self.name = name
        self.sem = sem
        self.count = 0
        self.known = {}
        self.prog = []


class Sched:
    def __init__(self, nc, es):
        self.nc = nc
        self.es = es
        self.eng = {}
        for n in ("pe", "act", "dve", "pool", "sp"):
            self.eng[n] = Eng(n, es.enter_context(nc.semaphore("sem_" + n)))
        self.nsem = 0
        self.dma_sems = {}

    def _need(self, e, tok, waits, skip_sem=None):
        if tok is None:
            return
        if isinstance(tok, list):
            for t in tok:
                self._need(e, t, waits, skip_sem)
            return
        if skip_sem is not None and tok.sem is skip_sem:
            return
        if tok.sem is e.sem and e.name == "pe":
            return
        k = id(tok.sem)
        if e.known.get(k, 0) < tok.val:
            e.known[k] = tok.val
            waits[k] = tok

    def _deps(self, e, reads, writes, skip_sem=None):
        waits = {}
        for b in reads:
            self._need(e, b.w, waits)
        for b in writes:
            self._need(e, b.w, waits, skip_sem)
            for t in b.r:
                self._need(e, t, waits)
        for t in waits.values():
            sem, val = t.sem, t.val
            e.prog.append(lambda h, sem=sem, val=val: h.wait_ge(sem, val))

    def _commit(self, tok, reads, writes):
        for b in reads:
            b.r.append(tok)
            if len(b.r) > 12:
                best = {}
                for t in b.r:
                    k = id(t.sem)
                    if k not in best or best[k].val < t.val:
                        best[k] = t
                b.r = list(best.values())
        for b in writes:
            b.w = tok
            b.r = []

    def op(self, en, fn, reads=(), writes=(), inc=True):
        e = self.eng[en]
        self._deps(e, reads, writes)
        if inc:
            e.count += 1
            sem = e.sem
            e.prog.append(lambda h, fn=fn, sem=sem: fn(h).then_inc(sem, 1))
            tok = Tok(e.sem, e.count)
        else:
            e.prog.append(lambda h, fn=fn: fn(h))
            tok = Tok(e.sem, e.count + 1)
        self._commit(tok, reads, writes)

    def dma(self, q, out, in_, reads, writes, key, **kw):
        e = self.eng[q]
        if key not in self.dma_sems:
            self.dma_sems[key] = [self.es.enter_context(self.nc.semaphore("dma_" + key)), 0]
        ent = self.dma_sems[key]
        self._deps(e, reads, writes, skip_sem=ent[0])
        ent[1] += 16
        sem = ent[0]
        e.prog.append(lambda h, out=out, in_=in_, sem=sem, kw=kw: h.dma_start(out=out, in_=in_, **kw).then_inc(sem, 16))
        tok = Tok(sem, ent[1])
        self._commit(tok, reads, writes)
        return tok

    def wait_tok(self, en, tok):
        e = self.eng[en]
        waits = {}
        self._need(e, tok, waits)
        for t in waits.values():
            sem, val = t.sem, t.val
            e.prog.append(lambda h, sem=sem, val=val: h.wait_ge(sem, val))

    def emit(self, block):
        def run(prog):
            def f(h):
                for p in prog:
                    p(h)
            return f
        block.sync(run(self.eng["sp"].prog))
        block.tensor(run(self.eng["pe"].prog))
        block.scalar(run(self.eng["act"].prog))
        block.vector(run(self.eng["dve"].prog))
        block.gpsimd(run(self.eng["pool"].prog))


def build_program(n_sblk, n_fblk, n_out_tiles, NT=2, debug=None):
    TB = NT * 128
    NBLK = n_sblk + n_fblk
    NTILES = NBLK * NT
    WT = NTILES * 128
    FT = n_fblk * TB
    OT = n_out_tiles * 128
    assert NTILES <= 64

    nc = bass.Bass("TRN2", target_bir_lowering=False)

    def din(name, shape, dt=F32):
        return nc.dram_tensor(name, list(shape), dt, kind="ExternalInput").ap()

    xw = din("xw", [WT, D])
    pw = din("pw", [FT, 256])
    consts_d = din("consts", [128, 2848])
    w_in = din("w_in", [D, IN_W])
    w_a = din("w_branch_a", [1024, D])
    w_b = din("w_branch_b", [2048, D])
    w_out = din("w_out", [D, D])
    w_up = din("w_ffn_up", [D, 2 * FFN_H])
    w_down = din("w_ffn_down", [FFN_H, D])
    w_pg = din("w_ple_gate", [D, D])
    w_pp = din("w_ple_proj", [256, D])
    out_d = nc.dram_tensor("out", [OT, D], F32, kind="ExternalOutput").ap()
    dbg_d = None
    if debug is not None:
        dbg_d = nc.dram_tensor("dbg", list(debug), F32, kind="ExternalOutput").ap()

    def dscr(name, shape):
        return nc.dram_tensor(name, list(shape), BF16, kind="Internal").ap()

    wsrc = {"in": w_in, "a": w_a, "b": w_b, "out": w_out, "up": w_up, "down": w_down, "pg": w_pg, "pp": w_pp}
    wbf = {k: dscr("wbf_" + k, v.shape) for k, v in wsrc.items()}

    es = ExitStack()
    with es:
        S = Sched(nc, es)
        bufs = {}

        def sb(name, shape, dt=F32):
            t = es.enter_context(nc.sbuf_tensor(name, list(shape), dt))
            bufs[name] = Buf(name)
            return t

        cst = sb("cst", [128, 2848])
        identB = sb("identB", [128, 128], BF16)
        onesF = sb("onesF", [128, 128])
        negc = sb("negc", [128, 1])
        A_bc = sb("A_bc", [128, 32])
        xs = [sb("xs%d" % i, [128, NT, D]) for i in range(2)]
        hT = sb("hT", [128, 8, TB], BF16)
        b_sb = sb("b_sb", [128, NT, 512])
        ebl = sb("ebl", [128, NT, 4])
        qt = sb("qt", [128, NT, 512], BF16)
        kt = sb("kt", [128, NT, 512], BF16)
        v_sb = sb("v_sb", [128, NT, 1024], BF16)
        sg = sb("sg", [128, NT, 1024], BF16)
        sz = sb("sz", [128, NT, 2048], BF16)
        xbcT = sb("xbcT", [128, 24, TB], BF16)
        sgab = sb("sgab", [128, 16, TB], BF16)
        oaT = sb("oaT", [128, 8, TB], BF16)
        ynT = sb("ynT", [128, 16, TB], BF16)
        mT = sb("mT", [128, 8, TB], BF16)
        pT = sb("pT", [128, 2, TB], BF16)
        p_in = sb("p_in", [128, 256])
        dtv = sb("dtv", [128, NT, 32])
        dav = sb("dav", [128, NT, 32])
        nacs = sb("nacs", [128, NT, 32])
        eacs = sb("eacs", [128, NT, 32])
        dtd = sb("dtd", [128, NT, 32])
        cdbc = sb("cdbc", [128, NT, 32])
        acs_t = [sb("acs%d" % i, [128, 32]) for i in range(NT)]
        sm_a_t = [sb("sm_a%d" % i, [128, 48]) for i in range(NT)]
        sm_b_t = [sb("sm_b%d" % i, [128, 32]) for i in range(NT)]
        alrT_t = [sb("alrT%d" % i, [16, 128]) for i in range(NT)]
        tmpA = sb("tmpA", [128, 1024])
        tmpB = sb("tmpB", [128, 1024])
        tmpC = sb("tmpC", [128, 512])
        tmpD = sb("tmpD", [128, 512])
        assert NT == 2
        l1_t = [tmpC, tmpD]
        l1_b = [bufs["tmpC"], bufs["tmpD"]]
        junk = sb("junk", [128, 1024], BF16)
        st4 = sb("st4", [128, 8])
        rs4 = sb("rs4", [128, 8])
        st4s = sb("st4s", [128, 4])
        rs4s = sb("rs4s", [128, 4])
        junk2 = sb("junk2", [128, 512], BF16)
        qT = sb("qT", [128, 4, 128], BF16)
        kT = sb("kT", [128, 4, 128], BF16)
        attm = sb("attm", [128, 4, 128], BF16)
        Sg = sb("Sg", [128, 4, 256])
        Sgb = sb("Sgb", [128, 4, 256], BF16)
        Xtm = sb("Xtm", [128, 2048], BF16)
        Xdt = sb("Xdt", [128, 2048], BF16)
        Xd = sb("Xd", [128, 2048], BF16)
        Btm = sb("Btm", [128, 512], BF16)
        scm = sb("scm", [128, 4, 128], BF16)
        Zg = [sb("Zg%d" % i, [128, 4, 128]) for i in range(2)]
        Eg = [sb("Eg%d" % i, [128, 4, 128]) for i in range(2)]
        MTg = [sb("MTg%d" % i, [128, 8, 128], BF16) for i in range(2)]
        yz = sb("yz", [128, 2048])
        Hs = sb("Hs", [128, 2048])
        Hsb = sb("Hsb", [128, 2048], BF16)
        pre = [sb("pre%d" % i, [128, TB + 3]) for i in range(3)]
        acc = [sb("acc%d" % i, [128, TB]) for i in range(2)]
        halo_x = sb("halo_x", [128, 24, 3])
        halo_f = sb("halo_f", [128, 22, 2])
        out_sb = [tmpB] * 2
        bufs["out_sb0"] = bufs["tmpB"]
        NW = 4
        wpool = [sb("wp%d" % i, [128, 4096], BF16) for i in range(NW)]
        wsm = sb("wsm", [128, 8, 48], BF16)

        B = bufs
        psF = []
        for i in range(6):
            t = es.enter_context(nc.psum_tensor("psF%d" % i, [128, 512], F32))
            psF.append((t, Buf("psF%d" % i)))
        psB = []
        for i in range(2):
            t = es.enter_context(nc.psum_tensor("psB%d" % i, [128, 1024], BF16))
            psB.append((t, Buf("psB%d" % i)))
        freeF = list(range(6))
        freeB = list(range(2))

        def psum_f():
            i = freeF.pop(0)
            return i

        def free_f(i):
            freeF.append(i)

        def psum_b():
            return freeB.pop(0)

        def free_b(i):
            freeB.append(i)

        wbuf = {k: Buf("wbf_" + k) for k in wsrc}

        def mm(out, lhsT, rhs, start, stop, reads, writes, inc):
            S.op("pe", lambda h: h.matmul(out, lhsT, rhs, start=start, stop=stop), reads, writes, inc)

        def tr(out, in_, ident, reads, writes, inc):
            S.op("pe", lambda h: h.transpose(out, in_, ident), reads, writes, inc)

        def act(out, in_, func, reads, writes, bias=None, scale=None):
            kw = {}
            if bias is not None:
                kw["bias"] = bias
            if scale is not None:
                kw["scale"] = scale
            S.op("act", lambda h: h.activation(out=out, in_=in_, func=func, **kw), reads, writes)

        def tt(en, out, in0, in1, op, reads, writes):
            S.op(en, lambda h: h.tensor_tensor(out=out, in0=in0, in1=in1, op=op), reads, writes)

        def ts(en, out, in0, s1, s2, op0, op1, reads, writes):
            if op1 is None:
                S.op(en, lambda h: h.tensor_scalar(out=out, in0=in0, scalar1=s1, scalar2=None, op0=op0), reads, writes)
            else:
                S.op(en, lambda h: h.tensor_scalar(out=out, in0=in0, scalar1=s1, scalar2=s2, op0=op0, op1=op1), reads, writes)

        def stt(out, in0, scalar, in1, op0, op1, reads, writes):
            S.op("dve", lambda h: h.scalar_tensor_tensor(out=out, in0=in0, scalar=scalar, in1=in1, op0=op0, op1=op1), reads, writes)

        def cp(en, out, in_, reads, writes):
            if en == "act":
                S.op("act", lambda h: h.activation(out=out, in_=in_, func=AF.Copy), reads, writes)
            else:
                S.op(en, lambda h: h.tensor_copy(out=out, in_=in_), reads, writes)

        def ttr(out, in0, in1, accum, reads, writes):
            S.op("act", lambda h: h.activation(out=out, in_=in0, func=AF.Square, accum_out=accum), reads, writes)

        def rstd_from_ss(ss_ap, out_ap, n, width, reads_buf, out_buf):
            ts("dve", out_ap, ss_ap, 1.0 / n, EPS, ALU.mult, ALU.add, [reads_buf], [out_buf])
            act(out_ap, out_ap, AF.Sqrt, [out_buf], [out_buf])
            S.op("dve", lambda h: h.reciprocal(out=out_ap, in_=out_ap), [out_buf], [out_buf])

        cF = lambda a, n: cst[:, a:a + n]
        identF = cF(K_ID, 128)
        triT = cF(K_TRI, 128)
        triG = cF(K_TRIG, 128)
        CST = B["cst"]

        wstate = {"next": 0}

        def wload(name, kc0, nkc, c0, ncols):
            i = wstate["next"]
            wstate["next"] = (i + 1) % NW
            t = wpool[i]
            bw = B["wp%d" % i]
            src = wbf[name].rearrange("(kc p) n -> p kc n", p=128)[:, kc0:kc0 + nkc, c0:c0 + ncols]
            dst = t[:, 0:nkc * ncols].rearrange("p (kc n) -> p kc n", n=ncols)
            S.dma("sp", dst, src, [wbuf[name]], [bw], "wp%d" % i)
            return dst, bw

        S.dma("sp", cst[:], consts_d, [], [CST], "cst")
        S.op("pool", lambda h: h.memset(onesF[:], 1.0), [], [B["onesF"]])
        S.op("pool", lambda h: h.memset(negc[:], -1.0 / 16.0), [], [B["negc"]])
        S.op("pool", lambda h: h.memset(Sg[:], 0.0), [], [B["Sg"]])
        S.op("pool", lambda h: h.memset(Sgb[:], 0.0), [], [B["Sgb"]])
        S.op("pool", lambda h: h.memset(Hs[:], 0.0), [], [B["Hs"]])
        S.op("pool", lambda h: h.memset(Hsb[:], 0.0), [], [B["Hsb"]])
        S.op("pool", lambda h: h.memset(halo_x[:], 0.0), [], [B["halo_x"]])
        S.op("pool", lambda h: h.memset(halo_f[:], 0.0), [], [B["halo_f"]])
        cp("dve", identB[:], identF, [CST], [B["identB"]])
        act(A_bc[:], cF(K_ALOG, 32), AF.Exp, [CST], [B["A_bc"]])
        ts("dve", A_bc[:], A_bc[:], -1.0, None, ALU.mult, None, [B["A_bc"]], [B["A_bc"]])

        cvt_engs = ["dve", "act", "pool"]
        cvt_in = [xs[i][:].rearrange("p a b -> p (a b)") for i in range(2)]
        cvt_in_b = [B["xs0"], B["xs1"]]
        cvt_out = [Xtm, Xdt]
        cvt_out_b = [B["Xtm"], B["Xdt"]]
        ci = 0
        for name, wd in wsrc.items():
            K, N = wd.shape
            for r0 in range(0, K, 128):
                for c0 in range(0, N, 2048):
                    ncol = min(2048, N - c0)
                    j = ci % 2
                    S.dma("sp", cvt_in[j][:, 0:ncol], wd[r0:r0 + 128, c0:c0 + ncol], [], [cvt_in_b[j]], "cvi%d" % j)
                    cp(cvt_engs[ci % 3], cvt_out[j][:, 0:ncol], cvt_in[j][:, 0:ncol], [cvt_in_b[j]], [cvt_out_b[j]])
                    S.dma("pool", wbf[name][r0:r0 + 128, c0:c0 + ncol], cvt_out[j][:, 0:ncol], [cvt_out_b[j]], [], "cvo%d" % j)
                    ci += 1
        cv_done = [Tok(S.dma_sems["cvo%d" % j][0], S.dma_sems["cvo%d" % j][1]) for j in range(2)]
        for name in wsrc:
            wbuf[name].w = cv_done

        def norm_transpose(xsrc, xbuf, gain_col0, ti, mask_col=None):
            ttr(junk[:], xsrc, xsrc, st4[:, 0:1], [xbuf], [B["junk"], B["st4"]])
            rstd_from_ss(st4[:, 0:1], rs4[:, 0:1], D, 1, B["st4"], B["rs4"])
            if mask_col is not None:
                tt("dve", rs4[:, 0:1], rs4[:, 0:1], mask_col, ALU.mult, [B["rs4"], CST], [B["rs4"]])
            act(tmpA[:], xsrc, AF.Copy, [xbuf, B["rs4"]], [B["tmpA"]], scale=rs4[:, 0:1])
            for hb in range(2):
                pi = psum_f()
                pt, pb = psF[pi]
                for c in range(4):
                    cc = hb * 4 + c
                    tr(pt[:, c * 128:(c + 1) * 128], tmpA[:, cc * 128:(cc + 1) * 128], identF, [B["tmpA"], CST], [pb], c == 3)
                g = cst[:, gain_col0 + hb * 4:gain_col0 + hb * 4 + 4].unsqueeze(2).to_broadcast([128, 4, 128])
                tt("dve", hT[:, hb * 4:hb * 4 + 4, ti * 128:(ti + 1) * 128], pt[:].rearrange("p (c t) -> p c t", c=4), g, ALU.mult,
                   [pb, CST], [B["hT"]])
                free_f(pi)

        def tm_group(wname, c0, ncols, evac):
            wt, bw = wload(wname, 0, 8, c0, ncols)
            for ti in range(NT):
                pi = psum_f()
                pt, pb = psF[pi]
                for k in range(8):
                    mm(pt[:, 0:ncols], hT[:, k, ti * 128:(ti + 1) * 128], wt[:, k, :], k == 0, k == 7, [B["hT"], bw], [pb], k == 7)
                evac(ti, pt, pb)
                free_f(pi)

        def fm_group(wname, c0, nch, evac, src=None, srcbuf=None, nk=8):
            src = hT if src is None else src
            srcbuf = B["hT"] if srcbuf is None else srcbuf
            wt, bw = wload(wname, 0, nk, c0, nch * 128)
            for j in range(nch):
                pi = psum_f()
                pt, pb = psF[pi]
                for k in range(nk):
                    mm(pt[:, 0:TB], wt[:, k, j * 128:(j + 1) * 128], src[:, k, :], k == 0, k == nk - 1, [srcbuf, bw], [pb], k == nk - 1)
                evac(j, pt, pb)
                free_f(pi)

        conv_i = {"i": 0}
        conv_deferred = []

        def conv_chunk(pt, pb, TBn, ntap, halo, halo_buf, cidx, wcol0, bcol, func, outap, outbuf, post=None):
            j = conv_i["i"] % 3
            ja = conv_i["i"] % 2
            conv_i["i"] += 1
            hl = ntap - 1
            pr, pbuf = pre[j], B["pre%d" % j]
            ac, abuf = acc[ja], B["acc%d" % ja]
            wlast = cst[:, wcol0 + cidx * ntap + hl:wcol0 + cidx * ntap + hl + 1]
            cp("pool", pr[:, 0:hl], halo[:, cidx, :], [halo_buf], [pbuf])
            cp("act", pr[:, hl:hl + TB], pt[:, 0:TB], [pb], [pbuf])
            act(ac[:], pt[:, 0:TB], AF.Copy, [pb, CST], [abuf], scale=wlast)
            cp("pool", halo[:, cidx, :], pr[:, TB:TB + hl], [pbuf], [halo_buf])

            def stage2():
                for k in range(hl):
                    wk = cst[:, wcol0 + cidx * ntap + k:wcol0 + cidx * ntap + k + 1]
                    stt(ac[:], pr[:, k:k + TB], wk, ac[:], ALU.mult, ALU.add, [pbuf, CST, abuf], [abuf])
                act(outap, ac[:], func, [abuf, CST], [outbuf], bias=cst[:, bcol + cidx:bcol + cidx + 1])
                if post is not None:
                    post()
            if conv_deferred:
                conv_deferred.pop(0)()
            conv_deferred.append(stage2)

        def conv_flush():
            while conv_deferred:
                conv_deferred.pop(0)()

        def do_block(bi, full):
            xsb = xs[bi % 2]
            xbuf = B["xs%d" % (bi % 2)]
            t0 = bi * NT
            for ti in range(NT):
                S.dma("sp", xsb[:, ti, :], xw[(t0 + ti) * 128:(t0 + ti + 1) * 128, :], [], [xbuf], "xs%d" % (bi % 2))
            for ti in range(NT):
                norm_transpose(xsb[:, ti, :], xbuf, K_MIX, ti)


            for (dc0, sc0, n) in ((0, C_ALR, 16), (16, C_DT, 32)):
                S.dma("sp", wsm[:, :, dc0:dc0 + n], wbf["in"].rearrange("(kc p) n -> p kc n", p=128)[:, :, sc0:sc0 + n],
                      [wbuf["in"]], [B["wsm"]], "wsm")
            for ti in range(NT):
                pi = psum_f()
                pt, pb = psF[pi]
                for (dc0, n) in ((0, 16), (16, 32)):
                    for k in range(8):
                        mm(pt[:, dc0:dc0 + n], hT[:, k, ti * 128:(ti + 1) * 128], wsm[:, k, dc0:dc0 + n], k == 0, k == 7,
                           [B["hT"], B["wsm"]], [pb], k == 7)
                cp("act", sm_a_t[ti][:], pt[:, 0:48], [pb], [B["sm_a%d" % ti]])
                free_f(pi)

            def gates_chain(ti):
                tg = t0 + ti
                sm_a, SMA = sm_a_t[ti], B["sm_a%d" % ti]
                sm_b, SMB = sm_b_t[ti], B["sm_b%d" % ti]
                alrT, ALRT = alrT_t[ti], B["alrT%d" % ti]
                acs_sb, ACS = acs_t[ti], B["acs%d" % ti]
                l1, L1 = l1_t[ti], l1_b[ti]
                tt("dve", sm_b[:], sm_a[:, 16:48], cF(K_DTB, 32), ALU.add, [SMA, CST], [SMB])
                pi = psum_f()
                pt, pb = psF[pi]
                tr(pt[0:16, 0:128], sm_a[:, 0:16], identF, [SMA, CST], [pb], True)
                cp("act", alrT[:], pt[0:16, 0:128], [pb], [ALRT])
                free_f(pi)
                act(sm_b[:], sm_b[:], AF.Exp, [SMB], [SMB])
                yield
                pi = psum_f()
                pt, pb = psF[pi]
                mm(pt[:, 0:512], alrT[:], cst[0:16, K_WG:K_WG + 512], True, False, [ALRT, CST], [pb], False)
                mm(pt[:, 0:512], onesF[0:1, :], cst[0:1, K_BG:K_BG + 512], False, True, [B["onesF"], CST], [pb], True)
                act(l1[:], pt[:, 0:512], AF.Exp, [pb], [L1], scale=-1.0)
                free_f(pi)
                act(dtv[:, ti, :], sm_b[:], AF.Ln, [SMB], [B["dtv"]], bias=1.0)
                yield
                act(l1[:], l1[:], AF.Ln, [L1], [L1], bias=1.0)
                ts("dve", dtv[:, ti, :], dtv[:, ti, :], cst[:, K_MASK + tg:K_MASK + tg + 1], None, ALU.mult, None, [B["dtv"], CST], [B["dtv"]])
                tt("dve", dav[:, ti, :], dtv[:, ti, :], A_bc[:], ALU.mult, [B["dtv"], B["A_bc"]], [B["dav"]])
                yield
                pi = psum_f()
                pt, pb = psF[pi]
                mm(pt[:, 0:512], triG, l1[:], True, True, [CST, L1], [pb], True)
                cp("act", b_sb[:, ti, :], pt[:, 0:512], [pb], [B["b_sb"]])
                free_f(pi)
                yield
                pi = psum_f()
                pt, pb = psF[pi]
                for h in range(4):
                    mm(pt[:, h:h + 1], l1[:, h * 128:(h + 1) * 128], negc[:, 0:1], True, True, [L1, B["negc"]], [pb], h == 3)
                mm(pt[:, 32:64], triT, dav[:, ti, :], True, True, [CST, B["dav"]], [pb], False)
                mm(pt[:, 64:96], onesF[:], dav[:, ti, :], True, True, [B["onesF"], B["dav"]], [pb], True)
                act(ebl[:, ti, :], pt[:, 0:4], AF.Exp, [pb], [B["ebl"]])
                cp("act", acs_sb[:], pt[:, 32:64], [pb], [ACS])
                act(eacs[:, ti, :], pt[:, 32:64], AF.Exp, [pb], [B["eacs"]])
                act(cdbc[:, ti, :], pt[:, 64:96], AF.Exp, [pb], [B["cdbc"]])
                ts("dve", nacs[:, ti, :], acs_sb[:], -1.0, None, ALU.mult, None, [ACS], [B["nacs"]])
                tt("dve", sm_b[:], pt[:, 64:96], acs_sb[:], ALU.subtract, [pb, ACS], [SMB])
                free_f(pi)
                yield
                act(sm_b[:], sm_b[:], AF.Exp, [SMB], [SMB])
                tt("dve", dtd[:, ti, :], dtv[:, ti, :], sm_b[:], ALU.mult, [B["dtv"], SMB], [B["dtd"]])

            fillers = []
            for hf in range(2):
                fillers.append(lambda hf=hf: tm_group("in", C_V + hf * 512, 512,
                               lambda ti, pt, pb: cp("act", v_sb[:, ti, hf * 512:(hf + 1) * 512], pt[:, 0:512], [pb], [B["v_sb"]])))
            nxt = 6 if full else 5
            for g in range(nxt):
                def ev_x(j, pt, pb, g=g):
                    c = g * 4 + j
                    conv_chunk(pt, pb, TB, 4, halo_x, B["halo_x"], c, K_WC, K_BC, AF.Silu, xbcT[:, c, :], B["xbcT"])
                fillers.append(lambda g=g, ev_x=ev_x: fm_group("in", C_XBC + g * 512, 4, ev_x))
            if full:
                for hf in range(2):
                    fillers.append(lambda hf=hf: tm_group("in", C_G + hf * 512, 512,
                                   lambda ti, pt, pb: act(sg[:, ti, hf * 512:(hf + 1) * 512], pt[:, 0:512], AF.Silu, [pb], [B["sg"]])))
                for qd in range(4):
                    fillers.append(lambda qd=qd: tm_group("in", C_Z + qd * 512, 512,
                                   lambda ti, pt, pb: act(sz[:, ti, qd * 512:(qd + 1) * 512], pt[:, 0:512], AF.Silu, [pb], [B["sz"]])))
                for g in range(4):
                    def ev_g(j, pt, pb, g=g):
                        c = g * 4 + j
                        act(sgab[:, c, :], pt[:, 0:TB], AF.Sigmoid, [pb], [B["sgab"]])
                    fillers.append(lambda g=g, ev_g=ev_g: fm_group("in", C_GA + g * 512, 4, ev_g))

            chains = [gates_chain(ti) for ti in range(NT)]
            while chains:
                for gch in list(chains):
                    try:
                        next(gch)
                    except StopIteration:
                        chains.remove(gch)
                if fillers:
                    fillers.pop(0)()
            def ev_q(ti, pt, pb):
                act(tmpA[:, 0:512], b_sb[:, ti, :], AF.Exp, [B["b_sb"]], [B["tmpA"]])
                stt(qt[:, ti, :], pt[:, 0:512], 128.0 ** -0.5, tmpA[:, 0:512], ALU.mult, ALU.mult, [pb, B["tmpA"]], [B["qt"]])

            def ev_k(ti, pt, pb):
                act(tmpB[:, 0:512], b_sb[:, ti, :], AF.Exp, [B["b_sb"]], [B["tmpB"]], scale=-1.0)
                tt("dve", kt[:, ti, :], pt[:, 0:512], tmpB[:, 0:512], ALU.mult, [pb, B["tmpB"]], [B["kt"]])

            tm_group("in", C_K, 512, ev_k)
            if full:
                tm_group("in", C_Q, 512, ev_q)
            for f in fillers:
                f()
            conv_flush()

            def gla_gen(ti):
                tsl = slice(ti * 128, (ti + 1) * 128)
                bi2 = psum_b()
                ptb, pbb = psB[bi2]
                for h in range(4):
                    tr(ptb[:, h * 128:(h + 1) * 128], kt[:, ti, h * 128:(h + 1) * 128], identB[:], [B["kt"], B["identB"]], [pbb], h == 3)
                cp("act", kT[:], ptb[:, 0:512].rearrange("p (h t) -> p h t", h=4), [pbb], [B["kT"]])
                free_b(bi2)
                yield
                if full:
                    bi2 = psum_b()
                    ptb, pbb = psB[bi2]
                    for h in range(4):
                        tr(ptb[:, h * 128:(h + 1) * 128], qt[:, ti, h * 128:(h + 1) * 128], identB[:], [B["qt"], B["identB"]], [pbb], h == 3)
                    cp("act", qT[:], ptb[:, 0:512].rearrange("p (h t) -> p h t", h=4), [pbb], [B["qT"]])
                    free_b(bi2)
                    yield
                    pi = psum_f()
                    pt, pb = psF[pi]
                    for h in range(4):
                        mm(pt[:, h * 128:(h + 1) * 128], kT[:, h, :], qT[:, h, :], True, True, [B["kT"], B["qT"]], [pb], h == 3)
                    tt("dve", attm[:], pt[:].rearrange("p (h t) -> p h t", h=4), triT.unsqueeze(1).to_broadcast([128, 4, 128]), ALU.mult,
                       [pb, CST], [B["attm"]])
                    free_f(pi)
                    yield
                    pos = [psum_f(), psum_f()]
                    for h in range(4):
                        pt, pb = psF[pos[h // 2]]
                        o_ap = pt[:, (h % 2) * 256:(h % 2) * 256 + 256]
                        mm(o_ap, attm[:, h, :], v_sb[:, ti, h * 256:(h + 1) * 256], True, False, [B["attm"], B["v_sb"]], [pb], False)
                        mm(o_ap, qT[:, h, :], Sgb[:, h, :], False, True, [B["qT"], B["Sgb"]], [pb], True)
                    for b2 in range(2):
                        pt, pb = psF[pos[b2]]
                        cp("act", tmpA[:, b2 * 512:(b2 + 1) * 512], pt[:, 0:512], [pb], [B["tmpA"]])
                        free_f(pos[b2])
                    yield
                    for h in range(4):
                        o_ap = tmpA[:, h * 256:(h + 1) * 256]
                        ttr(junk[:, 0:256], o_ap, o_ap, st4[:, h:h + 1], [B["tmpA"]], [B["junk"], B["st4"]])
                    rstd_from_ss(st4[:, 0:4], rs4[:, 0:4], 256, 4, B["st4"], B["rs4"])
                    yield
                    for h in range(4):
                        o_ap = tmpA[:, h * 256:(h + 1) * 256]
                        stt(tmpB[:, h * 256:(h + 1) * 256], o_ap, rs4[:, h:h + 1], sg[:, ti, h * 256:(h + 1) * 256], ALU.mult, ALU.mult,
                            [B["tmpA"], B["rs4"], B["sg"]], [B["tmpB"]])
                    yield
                    for hb in range(2):
                        pi = psum_f()
                        pt, pb = psF[pi]
                        for c in range(4):
                            cc = hb * 4 + c
                            tr(pt[:, c * 128:(c + 1) * 128], tmpB[:, cc * 128:(cc + 1) * 128], identF, [B["tmpB"], CST], [pb], c == 3)
                        g = cst[:, K_HN + hb * 4:K_HN + hb * 4 + 4].unsqueeze(2).to_broadcast([128, 4, 128])
                        tt("dve", oaT[:, hb * 4:hb * 4 + 4, tsl], pt[:].rearrange("p (c t) -> p c t", c=4), g, ALU.mult, [pb, CST], [B["oaT"]])
                        free_f(pi)
                    yield
                pds = [psum_f(), psum_f()]
                for h in range(4):
                    pt, pb = psF[pds[h // 2]]
                    mm(pt[:, (h % 2) * 256:(h % 2) * 256 + 256], kt[:, ti, h * 128:(h + 1) * 128], v_sb[:, ti, h * 256:(h + 1) * 256], True, True,
                       [B["kt"], B["v_sb"]], [pb], h % 2 == 1)
                for b2 in range(2):
                    pt, pb = psF[pds[b2]]
                    sl = Sg[:, 2 * b2:2 * b2 + 2, :]
                    tt("dve", sl, pt[:].rearrange("p (h v) -> p h v", h=2), sl, ALU.add, [pb, B["Sg"]], [B["Sg"]])
                    tt("dve", sl, sl, ebl[:, ti, 2 * b2:2 * b2 + 2].unsqueeze(2).to_broadcast([128, 2, 256]), ALU.mult, [B["Sg"], B["ebl"]], [B["Sg"]])
                    free_f(pds[b2])
                cp("pool", Sgb[:], Sg[:], [B["Sg"]], [B["Sgb"]])

            def ssd_gen(ti):
                tsl = slice(ti * 128, (ti + 1) * 128)
                for hb in range(2):
                    bi2 = psum_b()
                    ptb, pbb = psB[bi2]
                    for c in range(8):
                        cc = hb * 8 + c
                        tr(ptb[:, c * 128:(c + 1) * 128], xbcT[:, cc, tsl], identB[:], [B["xbcT"], B["identB"]], [pbb], c == 7)
                    cp("act", Xtm[:, hb * 1024:(hb + 1) * 1024], ptb[:, 0:1024], [pbb], [B["Xtm"]])
                    free_b(bi2)
                    yield
                bi2 = psum_b()
                ptb, pbb = psB[bi2]
                for c in range(4):
                    tr(ptb[:, c * 128:(c + 1) * 128], xbcT[:, 16 + c, tsl], identB[:], [B["xbcT"], B["identB"]], [pbb], c == 3)
                cp("act", Btm[:], ptb[:, 0:512], [pbb], [B["Btm"]])
                free_b(bi2)
                X3 = Xtm[:].rearrange("p (h d) -> p h d", d=64)
                tt("pool", Xd[:].rearrange("p (h d) -> p h d", d=64), X3, dtd[:, ti, :].unsqueeze(2).to_broadcast([128, 32, 64]), ALU.mult,
                   [B["Xtm"], B["dtd"]], [B["Xd"]])
                yield
                if full:
                    tt("pool", Xdt[:].rearrange("p (h d) -> p h d", d=64), X3, dtv[:, ti, :].unsqueeze(2).to_broadcast([128, 32, 64]), ALU.mult,
                       [B["Xtm"], B["dtv"]], [B["Xdt"]])
                    pi = psum_f()
                    pt, pb = psF[pi]
                    for g in range(4):
                        mm(pt[:, g * 128:(g + 1) * 128], xbcT[:, 16 + g, tsl], xbcT[:, 20 + g, tsl], True, True, [B["xbcT"]], [pb], g == 3)
                    tt("dve", scm[:], pt[:].rearrange("p (g t) -> p g t", g=4), triT.unsqueeze(1).to_broadcast([128, 4, 128]), ALU.mult,
                       [pb, CST], [B["scm"]])
                    free_f(pi)
                    yield

                    def stage_a(g, hh2):
                        j = g % 2
                        jz = (2 * g + hh2) % 2
                        h0 = 8 * g + 4 * hh2
                        tt("pool", Zg[jz][:], triT.unsqueeze(1).to_broadcast([128, 4, 128]),
                           dav[:, ti, h0:h0 + 4].unsqueeze(2).to_broadcast([128, 4, 128]), ALU.mult, [CST, B["dav"]], [B["Zg%d" % jz]])
                        pi = psum_f()
                        pt, pb = psF[pi]
                        mm(pt[:, 0:512], onesF[:], Zg[jz][:].rearrange("p h t -> p (h t)"), True, True,
                           [B["onesF"], B["Zg%d" % jz]], [pb], True)
                        for h4 in range(4):
                            hd = h0 + h4
                            act(Eg[jz][:, h4, :], pt[:, h4 * 128:(h4 + 1) * 128], AF.Relu, [pb, B["acs%d" % ti]], [B["Eg%d" % jz]],
                                bias=acs_t[ti][:, hd:hd + 1], scale=-1.0)
                        free_f(pi)
                        act(Eg[jz][:], Eg[jz][:], AF.Exp, [B["Eg%d" % jz]], [B["Eg%d" % jz]], scale=-1.0)
                        tt("dve", MTg[j][:, 4 * hh2:4 * hh2 + 4, :], Eg[jz][:], scm[:, g, :].unsqueeze(1).to_broadcast([128, 4, 128]), ALU.mult,
                           [B["Eg%d" % jz], B["scm"]], [B["MTg%d" % j]])

                    def stage_b(g):
                        j = g % 2
                        pyd = psum_f()
                        pt, pb = psF[pyd]
                        for hh in range(8):
                            hd = 8 * g + hh
                            mm(pt[:, hh * 64:(hh + 1) * 64], MTg[j][:, hh, :], Xdt[:, hd * 64:(hd + 1) * 64], True, True,
                               [B["MTg%d" % j], B["Xdt"]], [pb], hh == 7)
                        pyo = psum_f()
                        pt2, pb2 = psF[pyo]
                        mm(pt2[:, 0:512], xbcT[:, 20 + g, tsl], Hsb[:, g * 512:(g + 1) * 512], True, True, [B["xbcT"], B["Hsb"]], [pb2], True)
                        e_bc = eacs[:, ti, 8 * g:8 * g + 8].unsqueeze(2).to_broadcast([128, 8, 64])
                        tt("dve", tmpC[:].rearrange("p (h d) -> p h d", d=64), pt2[:].rearrange("p (h d) -> p h d", d=64), e_bc, ALU.mult,
                           [pb2, B["eacs"]], [B["tmpC"]])
                        tt("dve", tmpC[:], pt[:, 0:512], tmpC[:], ALU.add, [pb, B["tmpC"]], [B["tmpC"]])
                        free_f(pyd)
                        free_f(pyo)
                        d_bc = cst[:, K_DSK + 8 * g:K_DSK + 8 * g + 8].unsqueeze(2).to_broadcast([128, 8, 64])
                        tt("pool", tmpD[:].rearrange("p (h d) -> p h d", d=64), Xtm[:, g * 512:(g + 1) * 512].rearrange("p (h d) -> p h d", d=64),
                           d_bc, ALU.mult, [B["Xtm"], CST], [B["tmpD"]])
                        tt("dve", tmpC[:], tmpC[:], tmpD[:], ALU.add, [B["tmpC"], B["tmpD"]], [B["tmpC"]])
                        tt("dve", yz[:, g * 512:(g + 1) * 512], tmpC[:], sz[:, ti, g * 512:(g + 1) * 512], ALU.mult, [B["tmpC"], B["sz"]], [B["yz"]])
                        ttr(junk2[:, 0:512], yz[:, g * 512:(g + 1) * 512], yz[:, g * 512:(g + 1) * 512], st4s[:, g:g + 1],
                            [B["yz"]], [B["junk2"], B["st4s"]])

                    order = [("a", 0), ("a", 1), ("b", 0), ("a", 2), ("b", 1), ("a", 3), ("b", 2), ("b", 3)]
                    for kind, g in order:
                        if kind == "a":
                            stage_a(g, 0)
                            yield
                            stage_a(g, 1)
                            yield
                        else:
                            stage_b(g)
                            yield
                    rstd_from_ss(st4s[:, 0:4], rs4s[:, 0:4], 512, 4, B["st4s"], B["rs4s"])
                    tt("dve", yz[:].rearrange("p (g c) -> p g c", g=4), yz[:].rearrange("p (g c) -> p g c", g=4),
                       rs4s[:, 0:4].unsqueeze(2).to_broadcast([128, 4, 512]), ALU.mult, [B["yz"], B["rs4s"]], [B["yz"]])
                    yield
                    for hb in range(4):
                        pi = psum_f()
                        pt, pb = psF[pi]
                        for c in range(4):
                            cc = hb * 4 + c
                            tr(pt[:, c * 128:(c + 1) * 128], yz[:, cc * 128:(cc + 1) * 128], identF, [B["yz"], CST], [pb], c == 3)
                        g = cst[:, K_SSM + hb * 4:K_SSM + hb * 4 + 4].unsqueeze(2).to_broadcast([128, 4, 128])
                        tt("dve", ynT[:, hb * 4:hb * 4 + 4, tsl], pt[:].rearrange("p (c t) -> p c t", c=4), g, ALU.mult, [pb, CST], [B["ynT"]])
                        free_f(pi)
                        yield
                for g in range(4):
                    pi = psum_f()
                    pt, pb = psF[pi]
                    mm(pt[:, 0:512], Btm[:, g * 128:(g + 1) * 128], Xd[:, g * 512:(g + 1) * 512], True, True, [B["Btm"], B["Xd"]], [pb], True)
                    hsl = Hs[:, g * 512:(g + 1) * 512]
                    c_bc = cdbc[:, ti, 8 * g:8 * g + 8].unsqueeze(2).to_broadcast([128, 8, 64])
                    tt("dve", hsl.rearrange("p (h d) -> p h d", d=64), hsl.rearrange("p (h d) -> p h d", d=64), c_bc, ALU.mult,
                       [B["Hs"], B["cdbc"]], [B["Hs"]])
                    tt("dve", hsl, hsl, pt[:, 0:512], ALU.add, [B["Hs"], pb], [B["Hs"]])
                    free_f(pi)
                    yield
                cp("pool", Hsb[:], Hs[:], [B["Hs"]], [B["Hsb"]])

            for ti in range(NT):
                gens = [ssd_gen(ti), gla_gen(ti)]
                while gens:
                    for gg in list(gens):
                        try:
                            next(gg)
                        except StopIteration:
                            gens.remove(gg)

            if not full:
                return

            for nh in range(2):
                wa_t, wa_b = wload("a", 0, 8, nh * 512, 512)
                wb0_t, wb0_b = wload("b", 0, 8, nh * 512, 512)
                wb1_t, wb1_b = wload("b", 8, 8, nh * 512, 512)
                for j in range(4):
                    n = nh * 4 + j
                    pa = psum_f()
                    pta, pba = psF[pa]
                    for k in range(8):
                        mm(pta[:, 0:TB], wa_t[:, k, j * 128:(j + 1) * 128], oaT[:, k, :], k == 0, k == 7, [wa_b, B["oaT"]], [pba], k == 7)
                    pbk = psum_f()
                    ptb_, pbb_ = psF[pbk]
                    for k in range(16):
                        wt_, wb_ = (wb0_t, wb0_b) if k < 8 else (wb1_t, wb1_b)
                        mm(ptb_[:, 0:TB], wt_[:, k % 8, j * 128:(j + 1) * 128], ynT[:, k, :], k == 0, k == 15, [wb_, B["ynT"]], [pbb_], k % 8 == 7)
                    tt("dve", tmpC[:, 0:TB], pta[:, 0:TB], sgab[:, n, :], ALU.mult, [pba, B["sgab"]], [B["tmpC"]])
                    tt("dve", tmpD[:, 0:TB], ptb_[:, 0:TB], sgab[:, 8 + n, :], ALU.mult, [pbb_, B["sgab"]], [B["tmpD"]])
                    tt("pool", mT[:, n, :], tmpC[:, 0:TB], tmpD[:, 0:TB], ALU.add, [B["tmpC"], B["tmpD"]], [B["mT"]])
                    free_f(pa)
                    free_f(pbk)
            for hf in range(2):
                wt, bw = wload("out", 0, 8, hf * 512, 512)
                for ti in range(NT):
                    pi = psum_f()
                    pt, pb = psF[pi]
                    for k in range(8):
                        mm(pt[:, 0:512], mT[:, k, ti * 128:(ti + 1) * 128], wt[:, k, :], k == 0, k == 7, [B["mT"], bw], [pb], k == 7)
                    xsl = xsb[:, ti, hf * 512:(hf + 1) * 512]
                    tt("dve", xsl, xsl, pt[:, 0:512], ALU.add, [xbuf, pb], [xbuf])
                    free_f(pi)

            for ti in range(NT):
                tg = t0 + ti
                norm_transpose(xsb[:, ti, :], xbuf, K_FFN, ti, mask_col=cst[:, K_MASK + tg:K_MASK + tg + 1])
            gT = xbcT
            GT = B["xbcT"]
            fchunks = [(0, 4), (4, 4), (8, 4), (12, 4), (16, 4), (20, 2)]
            for (f0, nf) in fchunks:
                wa_t, wa_b = wload("up", 0, 8, f0 * 128, nf * 128)
                wl_t, wl_b = wload("up", 0, 8, FFN_H + f0 * 128, nf * 128)
                for j in range(nf):
                    f = f0 + j
                    pa = psum_f()
                    pta, pba = psF[pa]
                    for k in range(8):
                        mm(pta[:, 0:TB], wa_t[:, k, j * 128:(j + 1) * 128], hT[:, k, :], k == 0, k == 7, [wa_b, B["hT"]], [pba], k == 7)
                    pl = psum_f()
                    ptl, pbl = psF[pl]
                    for k in range(8):
                        mm(ptl[:, 0:TB], wl_t[:, k, j * 128:(j + 1) * 128], hT[:, k, :], k == 0, k == 7, [wl_b, B["hT"]], [pbl], k == 7)
                    def post(f=f, ptl=ptl, pbl=pbl, pl=pl):
                        tt("dve", gT[:, f, :], tmpC[:, 0:TB], ptl[:, 0:TB], ALU.mult, [B["tmpC"], pbl], [GT])
                        free_f(pl)
                    conv_chunk(pta, pba, TB, 3, halo_f, B["halo_f"], f, K_WFC, K_BFC, AF.Gelu_apprx_tanh, tmpC[:, 0:TB], B["tmpC"], post=post)
                    free_f(pa)
            conv_flush()
            kparts = [(0, 8), (8, 8), (16, 6)]
            for hf in range(2):
                pis = [psum_f() for _ in range(NT)]
                for kp, (k0, nk) in enumerate(kparts):
                    wt, bw = wload("down", k0, nk, hf * 512, 512)
                    for ti in range(NT):
                        pt, pb = psF[pis[ti]]
                        for k in range(nk):
                            mm(pt[:, 0:512], gT[:, k0 + k, ti * 128:(ti + 1) * 128], wt[:, k, :], kp == 0 and k == 0, kp == 2 and k == nk - 1,
                               [GT, bw], [pb], k == nk - 1)
                for ti in range(NT):
                    pt, pb = psF[pis[ti]]
                    xsl = xsb[:, ti, hf * 512:(hf + 1) * 512]
                    tt("dve", xsl, xsl, pt[:, 0:512], ALU.add, [xbuf, pb], [xbuf])
                    free_f(pis[ti])

            fb = bi - n_sblk
            for ti in range(NT):
                norm_transpose(xsb[:, ti, :], xbuf, K_PLE, ti)
                S.dma("sp", p_in[:], pw[(fb * NT + ti) * 128:(fb * NT + ti + 1) * 128, :], [], [B["p_in"]], "p_in")
                pi = psum_f()
                pt, pb = psF[pi]
                for c in range(2):
                    tr(pt[:, c * 128:(c + 1) * 128], p_in[:, c * 128:(c + 1) * 128], identF, [B["p_in"], CST], [pb], c == 1)
                cp("act", pT[:, :, ti * 128:(ti + 1) * 128], pt[:, 0:256].rearrange("p (c t) -> p c t", c=2), [pb], [B["pT"]])
                free_f(pi)
            wpp_t, wpp_b = wload("pp", 0, 2, 0, 1024)
            for hf in range(2):
                wt, bw = wload("pg", 0, 8, hf * 512, 512)
                for ti in range(NT):
                    pi = psum_f()
                    pt, pb = psF[pi]
                    for k in range(8):
                        mm(pt[:, 0:512], hT[:, k, ti * 128:(ti + 1) * 128], wt[:, k, :], k == 0, k == 7, [B["hT"], bw], [pb], k == 7)
                    act(tmpA[:, 0:512], pt[:, 0:512], AF.Sigmoid, [pb], [B["tmpA"]])
                    free_f(pi)
                    pi = psum_f()
                    pt, pb = psF[pi]
                    for k in range(2):
                        mm(pt[:, 0:512], pT[:, k, ti * 128:(ti + 1) * 128], wpp_t[:, k, hf * 512:(hf + 1) * 512], k == 0, k == 1,
                           [B["pT"], wpp_b], [pb], k == 1)
                    tt("dve", tmpB[:, 0:512], tmpA[:, 0:512], pt[:, 0:512], ALU.mult, [B["tmpA"], pb], [B["tmpB"]])
                    free_f(pi)
                    xsl = xsb[:, ti, hf * 512:(hf + 1) * 512]
                    tt("dve", xsl, xsl, tmpB[:, 0:512], ALU.add, [xbuf, B["tmpB"]], [xbuf])

            for ti in range(NT):
                tg = t0 + ti
                ot = tg - (NTILES - n_out_tiles)
                if ot < 0:
                    continue
                xsrc = xsb[:, ti, :]
                ttr(junk[:], xsrc, xsrc, st4[:, 0:1], [xbuf], [B["junk"], B["st4"]])
                rstd_from_ss(st4[:, 0:1], rs4[:, 0:1], D, 1, B["st4"], B["rs4"])
                ob = out_sb[0]
                obuf = B["out_sb0"]
                stt(ob[:], xsrc, rs4[:, 0:1], cF(K_FIN, 1024), ALU.mult, ALU.mult, [xbuf, B["rs4"], CST], [obuf])
                S.dma("pool", out_d[ot * 128:(ot + 1) * 128, :], ob[:], [obuf], [], "out_sb0")

        for bi in range(NBLK):
            do_block(bi, bi >= n_sblk)

        if debug is not None:
            debug_fn = build_program.debug_fn
            debug_fn(S, B, locals(), dbg_d)

        for key, ent in S.dma_sems.items():
            if key.startswith("out_sb") or key == "dbg":
                S.wait_tok("pool", Tok(ent[0], ent[1]))

        block = es.enter_context(nc.Block())
        S.emit(block)
    return nc


build_program.debug_fn = None


def make_consts(inp, mask, ntiles_cols=64):
    c = np.zeros((128, 2848), np.float32)
    c[:, K_ID:K_ID + 128] = np.eye(128, dtype=np.float32)
    tri = np.triu(np.ones((128, 128), np.float32))
    c[:, K_TRI:K_TRI + 128] = tri
    c[:, K_TRIG:K_TRIG + 128] = tri * np.float32(-1.0 / 16.0)
    c[0:16, K_WG:K_WG + 512] = inp["w_gla_gate"][0]
    c[0:1, K_BG:K_BG + 512] = inp["b_gla_gate"][0][None, :]
    c[:, K_FIN:K_FIN + 1024] = np.broadcast_to(inp["final_norm"][None, :], (128, 1024))
    colv = lambda v: np.ascontiguousarray(v.reshape(-1, 128).T)
    c[:, K_MIX:K_MIX + 8] = colv(inp["mixer_norm"][0])
    c[:, K_FFN:K_FFN + 8] = colv(inp["ffn_norm"][0])
    c[:, K_PLE:K_PLE + 8] = colv(inp["ple_norm"][0])
    c[:, K_HN:K_HN + 8] = np.tile(colv(inp["gla_norm"][0]), (1, 4))
    c[:, K_SSM:K_SSM + 16] = colv(inp["ssm_norm"][0])
    wc = inp["w_ssm_conv"][0]
    c[:, K_WC:K_WC + 96] = wc.T.reshape(24, 128, 4).transpose(1, 0, 2).reshape(128, 96)
    c[:, K_BC:K_BC + 24] = colv(inp["b_ssm_conv"][0])
    wf = inp["w_ffn_conv"][0]
    c[:, K_WFC:K_WFC + 66] = wf.T.reshape(22, 128, 3).transpose(1, 0, 2).reshape(128, 66)
    c[:, K_BFC:K_BFC + 22] = colv(inp["b_ffn_conv"][0])
    c[:, K_DTB:K_DTB + 32] = np.broadcast_to(inp["dt_bias"][0][None, :], (128, 32))
    c[:, K_ALOG:K_ALOG + 32] = np.broadcast_to(inp["a_log"][0][None, :], (128, 32))
    c[:, K_DSK:K_DSK + 32] = np.broadcast_to(inp["d_skip"][0][None, :], (128, 32))
    nt = mask.shape[0] // 128
    c[:, K_MASK:K_MASK + nt] = mask.reshape(nt, 128).T
    return c


NT_ = 2
N_FBLK = 9
N_SBLK = 23


def kernel(**inp):
    inp = {k: np.asarray(v) for k, v in inp.items()}
    x = inp["x"]
    p = inp["p"][0]
    ntiles = (N_SBLK + N_FBLK) * NT_
    WT = ntiles * 128
    FT = N_FBLK * NT_ * 128
    nc = build_program(N_SBLK, N_FBLK, OWN // 128, NT=NT_)
    shared = {
        "w_in": np.ascontiguousarray(inp["w_in"][0]),
        "w_branch_a": np.ascontiguousarray(inp["w_branch_a"][0]),
        "w_branch_b": np.ascontiguousarray(inp["w_branch_b"][0]),
        "w_out": np.ascontiguousarray(inp["w_out"][0]),
        "w_ffn_up": np.ascontiguousarray(inp["w_ffn_up"][0]),
        "w_ffn_down": np.ascontiguousarray(inp["w_ffn_down"][0]),
        "w_ple_gate": np.ascontiguousarray(inp["w_ple_gate"][0]),
        "w_ple_proj": np.ascontiguousarray(inp["w_ple_proj"][0]),
    }
    in_maps = []
    for core in range(NCORES):
        b, j = core // 4, core % 4
        end = (j + 1) * OWN
        start = end - WT
        xwin = np.zeros((WT, D), np.float32)
        mask = np.zeros((WT,), np.float32)
        s0 = max(start, 0)
        xwin[s0 - start:] = x[b, s0:end]
        mask[s0 - start:] = 1.0
        pwin = np.zeros((FT, 256), np.float32)
        ps = end - FT
        ps0 = max(ps, 0)
        pwin[ps0 - ps:] = p[b, ps0:end]
        m = dict(shared)
        m["xw"] = xwin
        m["pw"] = pwin
        m["consts"] = make_consts(inp, mask)
        in_maps.append(m)
    res = run_bass_kernel_spmd(nc, in_maps, core_ids=list(range(NCORES)))
    out = np.zeros((BATCH, SEQ, D), np.float32)
    for core in range(NCORES):
        b, j = core // 4, core % 4
        out[b, j * OWN:(j + 1) * OWN] = np.asarray(res.results[core]["out"]).reshape(OWN, D)
    return out
```

```python
import numpy as np
from contextlib import ExitStack
import concourse.bass as bass
import concourse.mybir as mybir
from concourse.bass_utils import run_bass_kernel_spmd

F32 = mybir.dt.float32
BF16 = mybir.dt.bfloat16
AF = mybir.ActivationFunctionType
ALU = mybir.AluOpType
AX = mybir.AxisListType

D = 1024
SEQ = 8192
BATCH = 2
NCORES = 8
OWN = 2048
EPS = 1e-6
C_Q, C_K, C_V, C_G, C_ALR, C_Z, C_XBC, C_DT, C_GA, C_GB = 0, 512, 1024, 2048, 3072, 3088, 5136, 8208, 8240, 9264
IN_W = 10288
FFN_H = 2816

K_ID, K_TRI, K_TRIG, K_WG, K_BG, K_FIN = 0, 128, 256, 384, 896, 1408
K_MIX, K_FFN, K_PLE, K_HN, K_SSM = 2432, 2440, 2448, 2456, 2464
K_WC, K_BC, K_WFC, K_BFC = 2480, 2576, 2600, 2666
K_DTB, K_ALOG, K_DSK, K_MASK = 2688, 2720, 2752, 2784


class Tok:
    __slots__ = ("sem", "val")

    def __init__(self, sem, val):
        self.sem = sem
        self.val = val


class Buf:
    def __init__(self, name):
        self.name = name
        self.w = None
        self.r = []


class Eng:
    def __init__(self, name, sem):
        self.name = name
        self.sem = sem
        self.count = 0
        self.known = {}
        self.prog = []


class Sched:
    def __init__(self, nc, es):
        self.nc = nc
        self.es = es
        self.eng = {}
        for n in ("pe", "act", "dve", "pool", "sp"):
            self.eng[n] = Eng(n, es.enter_context(nc.semaphore("sem_" + n)))
        self.nsem = 0
        self.dma_sems = {}

    def _need(self, e, tok, waits, skip_sem=None):
        if tok is None:
            return
        if isinstance(tok, list):
            for t in tok:
                self._need(e, t, waits, skip_sem)
            return
        if skip_sem is not None and tok.sem is skip_sem:
            return
        if tok.sem is e.sem and e.name == "pe":
            return
        k = id(tok.sem)
        if e.known.get(k, 0) < tok.val:
            e.known[k] = tok.val
            waits[k] = tok

    def _deps(self, e, reads, writes, skip_sem=None):
        waits = {}
        for b in reads:
            self._need(e, b.w, waits)
        for b in writes:
            self._need(e, b.w, waits, skip_sem)
            for t in b.r:
                self._need(e, t, waits)
        for t in waits.values():
            sem, val = t.sem, t.val
            e.prog.append(lambda h, sem=sem, val=val: h.wait_ge(sem, val))

    def _commit(self, tok, reads, writes):
        for b in reads:
            b.r.append(tok)
            if len(b.r) > 12:
                best = {}
                for t in b.r:
                    k = id(t.sem)
                    if k not in best or best[k].val < t.val:
                        best[k] = t
                b.r = list(best.values())
        for b in writes:
            b.w = tok
            b.r = []

    def op(self, en, fn, reads=(), writes=(), inc=True):
        e = self.eng[en]
        self._deps(e, reads, writes)
        if inc:
            e.count += 1
            sem = e.sem
            e.prog.append(lambda h, fn=fn, sem=sem: fn(h).then_inc(sem, 1))
            tok = Tok(e.sem, e.count)
        else:
            e.prog.append(lambda h, fn=fn: fn(h))
            tok = Tok(e.sem, e.count + 1)
        self._commit(tok, reads, writes)

    def dma(self, q, out, in_, reads, writes, key, **kw):
        e = self.eng[q]
        if key not in self.dma_sems:
            self.dma_sems[key] = [self.es.enter_context(self.nc.semaphore("dma_" + key)), 0]
        ent = self.dma_sems[key]
        self._deps(e, reads, writes, skip_sem=ent[0])
        ent[1] += 16
        sem = ent[0]
        e.prog.append(lambda h, out=out, in_=in_, sem=sem, kw=kw: h.dma_start(out=out, in_=in_, **kw).then_inc(sem, 16))
        tok = Tok(sem, ent[1])
        self._commit(tok, reads, writes)
        return tok

    def wait_tok(self, en, tok):
        e = self.eng[en]
        waits = {}
        self._need(e, tok, waits)
        for t in waits.values():
            sem, val = t.sem, t.val
            e.prog.append(lambda h, sem=sem, val=val: h.wait_ge(sem, val))

    def emit(self, block):
        def run(prog):
            def f(h):
                for p in prog:
                    p(h)
            return f
        block.sync(run(self.eng["sp"].prog))
        block.tensor(run(self.eng["pe"].prog))
        block.scalar(run(self.eng["act"].prog))
        block.vector(run(self.eng["dve"].prog))
        block.gpsimd(run(self.eng["pool"].prog))


def build_program(n_sblk, n_fblk, n_out_tiles, NT=2, debug=None, ncores=NCORES):
    TB = NT * 128
    NBLK = n_fblk
    NTILES = NBLK * NT
    WT = NTILES * 128
    FT = n_fblk * TB
    OT = n_out_tiles * 128
    assert NTILES <= 32 and n_sblk <= n_fblk
    XW = 3136

    nc = bass.Bass("TRN2", target_bir_lowering=False)

    def din(name, shape, dt=F32):
        return nc.dram_tensor(name, list(shape), dt, kind="ExternalInput").ap()

    xw = din("xw", [WT, D])
    xh = din("xh", [128, D])
    pw = din("pw", [FT, 256])
    consts_d = din("consts", [128, 2848])
    w_in = din("w_in", [D, IN_W])
    w_a = din("w_branch_a", [1024, D])
    w_b = din("w_branch_b", [2048, D])
    w_out = din("w_out", [D, D])
    w_up = din("w_ffn_up", [D, 2 * FFN_H])
    w_down = din("w_ffn_down", [FFN_H, D])
    w_pg = din("w_ple_gate", [D, D])
    w_pp = din("w_ple_proj", [256, D])
    out_d = nc.dram_tensor("out", [OT, D], F32, kind="ExternalOutput").ap()
    dbg_d = None
    if debug is not None:
        dbg_d = nc.dram_tensor("dbg", list(debug), F32, kind="ExternalOutput").ap()

    def dscr(name, shape):
        return nc.dram_tensor(name, list(shape), BF16, kind="Internal").ap()

    xch_in = nc.dram_tensor("xch_in", [128, XW], F32).ap()
    xch_out = nc.dram_tensor("xch_out", [ncores * 128, XW], F32).ap()
    wsrc = {"in": w_in, "a": w_a, "b": w_b, "out": w_out, "up": w_up, "down": w_down, "pg": w_pg, "pp": w_pp}
    wbf = {k: dscr("wbf_" + k, v.shape) for k, v in wsrc.items()}

    es = ExitStack()
    with es:
        S = Sched(nc, es)
        bufs = {}

        def sb(name, shape, dt=F32):
            t = es.enter_context(nc.sbuf_tensor(name, list(shape), dt))
            bufs[name] = Buf(name)
            return t

        cst = sb("cst", [128, 2848])
        identB = sb("identB", [128, 128], BF16)
        onesF = sb("onesF", [128, 128])
        negc = sb("negc", [128, 1])
        A_bc = sb("A_bc", [128, 32])
        xs = [sb("xs%d" % i, [128, NT, D]) for i in range(2)]
        hT = sb("hT", [128, 8, TB], BF16)
        b_sb = sb("b_sb", [128, NT, 512])
        ebl = sb("ebl", [128, NT, 4])
        qt = sb("qt", [128, NT, 512], BF16)
        kt = sb("kt", [128, NT, 512], BF16)
        v_sb = sb("v_sb", [128, NT, 1024], BF16)
        sg = sb("sg", [128, NT, 1024], BF16)
        sz = sb("sz", [128, NT, 2048], BF16)
        xbcT = sb("xbcT", [128, 24, TB], BF16)
        sgab = sb("sgab", [128, 16, TB], BF16)
        oaT = sb("oaT", [128, 8, TB], BF16)
        ynT = sb("ynT", [128, 16, TB], BF16)
        mT = sb("mT", [128, 8, TB], BF16)
        pT = sb("pT", [128, 2, TB], BF16)
        p_in = sb("p_in", [128, 256])
        dtv = sb("dtv", [128, NT, 32])
        dav = sb("dav", [128, NT, 32])
        nacs = sb("nacs", [128, NT, 32])
        eacs = sb("eacs", [128, NT, 32])
        dtd = sb("dtd", [128, NT, 32])
        cdbc = sb("cdbc", [128, NT, 32])
        acs_t = [sb("acs%d" % i, [128, 32]) for i in range(NT)]
        sm_a_t = [sb("sm_a%d" % i, [128, 48]) for i in range(NT)]
        sm_b_t = [sb("sm_b%d" % i, [128, 32]) for i in range(NT)]
        alrT_t = [sb("alrT%d" % i, [16, 128]) for i in range(NT)]
        tmpA = sb("tmpA", [128, 1024])
        tmpB = sb("tmpB", [128, 1024])
        tmpC = sb("tmpC", [128, 512])
        tmpD = sb("tmpD", [128, 512])
        assert NT == 2
        l1_t = [tmpC, tmpD]
        l1_b = [bufs["tmpC"], bufs["tmpD"]]
        junk = sb("junk", [128, 1024], BF16)
        st4 = sb("st4", [128, 8])
        rs4 = sb("rs4", [128, 8])
        st4s = sb("st4s", [128, 4])
        rs4s = sb("rs4s", [128, 4])
        junk2 = junk
        bufs["junk2"] = bufs["junk"]
        qT = sb("qT", [128, 4, 128], BF16)
        kT = sb("kT", [128, 4, 128], BF16)
        attm = sb("attm", [128, 4, 128], BF16)
        Sg = sb("Sg", [128, 4, 256])
        Sgb = sb("Sgb", [128, 4, 256], BF16)
        Xtm = sb("Xtm", [128, 2048], BF16)
        Xdt = sb("Xdt", [128, 2048], BF16)
        Xd = sb("Xd", [128, 2048], BF16)
        Btm = sb("Btm", [128, 512], BF16)
        scm = sb("scm", [128, 4, 128], BF16)
        Zg = [sb("Zg%d" % i, [128, 4, 128]) for i in range(2)]
        Eg = [sb("Eg%d" % i, [128, 4, 128]) for i in range(2)]
        MTg = [sb("MTg%d" % i, [128, 8, 128], BF16) for i in range(3)]
        yz = sb("yz", [128, 2048])
        Hs = sb("Hs", [128, 2048])
        Hsb = sb("Hsb", [128, 2048], BF16)
        pre = [sb("pre%d" % i, [128, TB + 3]) for i in range(3)]
        acc = [sb("acc%d" % i, [128, TB]) for i in range(2)]
        halo_x = sb("halo_x", [128, 24, 3])
        halo_f = sb("halo_f", [128, 22, 2])
        halo0 = sb("halo0", [128, 24, 3])
        PDg = sb("PDg", [128, 4])
        PDs = sb("PDs", [128, 32])
        dst_ = sb("dst_", [128, 36])
        dgm = sb("dgm", [128, 36])
        out_sb = [tmpB] * 2
        bufs["out_sb0"] = bufs["tmpB"]
        NW = 4
        wpool = [sb("wp%d" % i, [128, 4096], BF16) for i in range(NW)]
        wsm = sb("wsm", [128, 8, 48], BF16)

        B = bufs
        psF = []
        for i in range(6):
            t = es.enter_context(nc.psum_tensor("psF%d" % i, [128, 512], F32))
            psF.append((t, Buf("psF%d" % i)))
        psB = []
        for i in range(2):
            t = es.enter_context(nc.psum_tensor("psB%d" % i, [128, 1024], BF16))
            psB.append((t, Buf("psB%d" % i)))
        freeF = list(range(6))
        freeB = list(range(2))

        def psum_f():
            i = freeF.pop(0)
            return i

        def free_f(i):
            freeF.append(i)

        def psum_b():
            return freeB.pop(0)

        def free_b(i):
            freeB.append(i)

        wbuf = {k: Buf("wbf_" + k) for k in wsrc}

        def mm(out, lhsT, rhs, start, stop, reads, writes, inc):
            S.op("pe", lambda h: h.matmul(out, lhsT, rhs, start=start, stop=stop), reads, writes, inc)

        def tr(out, in_, ident, reads, writes, inc):
            S.op("pe", lambda h: h.transpose(out, in_, ident), reads, writes, inc)

        def act(out, in_, func, reads, writes, bias=None, scale=None):
            kw = {}
            if bias is not None:
                kw["bias"] = bias
            if scale is not None:
                kw["scale"] = scale
            S.op("act", lambda h: h.activation(out=out, in_=in_, func=func, **kw), reads, writes)

        def tt(en, out, in0, in1, op, reads, writes):
            S.op(en, lambda h: h.tensor_tensor(out=out, in0=in0, in1=in1, op=op), reads, writes)

        def ts(en, out, in0, s1, s2, op0, op1, reads, writes):
            if op1 is None:
                S.op(en, lambda h: h.tensor_scalar(out=out, in0=in0, scalar1=s1, scalar2=None, op0=op0), reads, writes)
            else:
                S.op(en, lambda h: h.tensor_scalar(out=out, in0=in0, scalar1=s1, scalar2=s2, op0=op0, op1=op1), reads, writes)

        def stt(out, in0, scalar, in1, op0, op1, reads, writes):
            S.op("dve", lambda h: h.scalar_tensor_tensor(out=out, in0=in0, scalar=scalar, in1=in1, op0=op0, op1=op1), reads, writes)

        def cp(en, out, in_, reads, writes):
            if en == "act":
                S.op("act", lambda h: h.activation(out=out, in_=in_, func=AF.Copy), reads, writes)
            else:
                S.op(en, lambda h: h.tensor_copy(out=out, in_=in_), reads, writes)

        def ttr(out, in0, in1, accum, reads, writes):
            S.op("act", lambda h: h.activation(out=out, in_=in0, func=AF.Square, accum_out=accum), reads, writes)

        def rstd_from_ss(ss_ap, out_ap, n, width, reads_buf, out_buf):
            ts("dve", out_ap, ss_ap, 1.0 / n, EPS, ALU.mult, ALU.add, [reads_buf], [out_buf])
            act(out_ap, out_ap, AF.Ln, [out_buf], [out_buf])
            act(out_ap, out_ap, AF.Exp, [out_buf], [out_buf], scale=-0.5)

        cF = lambda a, n: cst[:, a:a + n]
        identF = cF(K_ID, 128)
        triT = cF(K_TRI, 128)
        triG = cF(K_TRIG, 128)
        CST = B["cst"]

        wstate = {"next": 0}

        def wload(name, kc0, nkc, c0, ncols):
            i = wstate["next"]
            wstate["next"] = (i + 1) % NW
            t = wpool[i]
            bw = B["wp%d" % i]
            src = wbf[name].rearrange("(kc p) n -> p kc n", p=128)[:, kc0:kc0 + nkc, c0:c0 + ncols]
            dst = t[:, 0:nkc * ncols].rearrange("p (kc n) -> p kc n", n=ncols)
            S.dma("sp", dst, src, [wbuf[name]], [bw], "wp%d" % i)
            return dst, bw

        S.dma("sp", cst[:], consts_d, [], [CST], "cst")
        S.op("pool", lambda h: h.memset(onesF[:], 1.0), [], [B["onesF"]])
        S.op("pool", lambda h: h.memset(negc[:], -1.0 / 16.0), [], [B["negc"]])
        S.op("pool", lambda h: h.memset(Sg[:], 0.0), [], [B["Sg"]])
        S.op("pool", lambda h: h.memset(Sgb[:], 0.0), [], [B["Sgb"]])
        S.op("pool", lambda h: h.memset(Hs[:], 0.0), [], [B["Hs"]])
        S.op("pool", lambda h: h.memset(Hsb[:], 0.0), [], [B["Hsb"]])
        S.op("pool", lambda h: h.memset(halo_x[:], 0.0), [], [B["halo_x"]])
        S.op("pool", lambda h: h.memset(halo_f[:], 0.0), [], [B["halo_f"]])
        cp("dve", identB[:], identF, [CST], [B["identB"]])
        act(A_bc[:], cF(K_ALOG, 32), AF.Exp, [CST], [B["A_bc"]])
        ts("dve", A_bc[:], A_bc[:], -1.0, None, ALU.mult, None, [B["A_bc"]], [B["A_bc"]])

        cvt_engs = ["dve", "act"]
        cvt_ci = {"i": 0}

        def cvt_gen(names, cin, cin_b, cout, cout_b, width, tag):
            for name in names:
                wd = wsrc[name]
                K, N = wd.shape
                for r0 in range(0, K, 128):
                    for c0 in range(0, N, width):
                        ncol = min(width, N - c0)
                        ci = cvt_ci["i"]
                        cvt_ci["i"] += 1
                        j = ci % 2
                        S.dma("sp", cin[j][:, 0:ncol], wd[r0:r0 + 128, c0:c0 + ncol], [], [cin_b[j]], "cvi%s%d" % (tag, j))
                        cp(cvt_engs[ci % 2], cout[j][:, 0:ncol], cin[j][:, 0:ncol], [cin_b[j]], [cout_b[j]])
                        S.dma("pool", wbf[name][r0:r0 + 128, c0:c0 + ncol], cout[j][:, 0:ncol], [cout_b[j]], [], "cvo%s%d" % (tag, j))
                        yield
                wbuf[name].w = [Tok(S.dma_sems["cvo%s%d" % (tag, j)][0], S.dma_sems["cvo%s%d" % (tag, j)][1]) for j in range(2)
                                if ("cvo%s%d" % (tag, j)) in S.dma_sems]

        for _ in cvt_gen(["in"], [xs[i][:].rearrange("p a b -> p (a b)") for i in range(2)], [B["xs0"], B["xs1"]],
                         [Xtm, Xdt], [B["Xtm"], B["Xdt"]], 2048, "a"):
            pass
        YZ = [Buf("yz_h0"), Buf("yz_h1")]
        SZ = [Buf("sz_h0"), Buf("sz_h1")]
        bg = cvt_gen(["a", "b", "out", "up", "down", "pg", "pp"], [yz[:, 0:1024], yz[:, 1024:2048]], YZ,
                     [sz[:, 0, 0:1024], sz[:, 1, 0:1024]], SZ, 1024, "b")
        bg_state = {"done": False}

        def bg_pump(n=1):
            if bg_state["done"]:
                return
            for _ in range(n):
                try:
                    next(bg)
                except StopIteration:
                    bg_state["done"] = True
                    B["yz"].w = [t for t in (YZ[0].w, YZ[1].w) if t is not None]
                    B["yz"].r = YZ[0].r + YZ[1].r
                    B["sz"].w = [t for t in (SZ[0].w, SZ[1].w) if t is not None]
                    B["sz"].r = SZ[0].r + SZ[1].r
                    return

        def norm_transpose(xsrc, xbuf, gain_col0, ti, mask_col=None):
            ttr(junk[:], xsrc, xsrc, st4[:, 0:1], [xbuf], [B["junk"], B["st4"]])
            rstd_from_ss(st4[:, 0:1], rs4[:, 0:1], D, 1, B["st4"], B["rs4"])
            if mask_col is not None:
                tt("dve", rs4[:, 0:1], rs4[:, 0:1], mask_col, ALU.mult, [B["rs4"], CST], [B["rs4"]])
            act(tmpA[:], xsrc, AF.Copy, [xbuf, B["rs4"]], [B["tmpA"]], scale=rs4[:, 0:1])
            for hb in range(2):
                pi = psum_f()
                pt, pb = psF[pi]
                for c in range(4):
                    cc = hb * 4 + c
                    tr(pt[:, c * 128:(c + 1) * 128], tmpA[:, cc * 128:(cc + 1) * 128], identF, [B["tmpA"], CST], [pb], c == 3)
                g = cst[:, gain_col0 + hb * 4:gain_col0 + hb * 4 + 4].unsqueeze(2).to_broadcast([128, 4, 128])
                tt("dve", hT[:, hb * 4:hb * 4 + 4, ti * 128:(ti + 1) * 128], pt[:].rearrange("p (c t) -> p c t", c=4), g, ALU.mult,
                   [pb, CST], [B["hT"]])
                free_f(pi)

        def tm_group(wname, c0, ncols, evac):
            wt, bw = wload(wname, 0, 8, c0, ncols)
            for ti in range(NT):
                pi = psum_f()
                pt, pb = psF[pi]
                for k in range(8):
                    mm(pt[:, 0:ncols], hT[:, k, ti * 128:(ti + 1) * 128], wt[:, k, :], k == 0, k == 7, [B["hT"], bw], [pb], k == 7)
                evac(ti, pt, pb)
                free_f(pi)

        def fm_group(wname, c0, nch, evac, src=None, srcbuf=None, nk=8):
            src = hT if src is None else src
            srcbuf = B["hT"] if srcbuf is None else srcbuf
            wt, bw = wload(wname, 0, nk, c0, nch * 128)
            for j in range(nch):
                pi = psum_f()
                pt, pb = psF[pi]
                for k in range(nk):
                    mm(pt[:, 0:TB], wt[:, k, j * 128:(j + 1) * 128], src[:, k, :], k == 0, k == nk - 1, [srcbuf, bw], [pb], k == nk - 1)
                evac(j, pt, pb)
                free_f(pi)

        conv_i = {"i": 0}
        conv_deferred = []

        def conv_chunk(pt, pb, TBn, ntap, halo, halo_buf, cidx, wcol0, bcol, func, outap, outbuf, post=None):
            j = conv_i["i"] % 3
            ja = conv_i["i"] % 2
            conv_i["i"] += 1
            hl = ntap - 1
            pr, pbuf = pre[j], B["pre%d" % j]
            ac, abuf = acc[ja], B["acc%d" % ja]
            wlast = cst[:, wcol0 + cidx * ntap + hl:wcol0 + cidx * ntap + hl + 1]
            cp("pool", pr[:, 0:hl], halo[:, cidx, :], [halo_buf], [pbuf])
            cp("act", pr[:, hl:hl + TB], pt[:, 0:TB], [pb], [pbuf])
            act(ac[:], pt[:, 0:TB], AF.Copy, [pb, CST], [abuf], scale=wlast)
            cp("pool", halo[:, cidx, :], pr[:, TB:TB + hl], [pbuf], [halo_buf])

            def stage2():
                for k in range(hl):
                    wk = cst[:, wcol0 + cidx * ntap + k:wcol0 + cidx * ntap + k + 1]
                    stt(ac[:], pr[:, k:k + TB], wk, ac[:], ALU.mult, ALU.add, [pbuf, CST, abuf], [abuf])
                act(outap, ac[:], func, [abuf, CST], [outbuf], bias=cst[:, bcol + cidx:bcol + cidx + 1])
                if post is not None:
                    post()
            if conv_deferred:
                conv_deferred.pop(0)()
            conv_deferred.append(stage2)

        def conv_flush():
            while conv_deferred:
                conv_deferred.pop(0)()

        blk_ctr = {"n": 0}

        def do_block(bi, full, mid_hook=None):
            par = blk_ctr["n"] % 2
            blk_ctr["n"] += 1
            xsb = xs[par]
            xbuf = B["xs%d" % par]
            t0 = bi * NT
            halo_only = full and all(t0 + ti < NTILES - n_out_tiles for ti in range(NT))
            c0h = TB - 128 if halo_only else 0
            for ti in range(NT):
                S.dma("sp", xsb[:, ti, :], xw[(t0 + ti) * 128:(t0 + ti + 1) * 128, :], [], [xbuf], "xs%d" % par)
            for ti in range(NT):
                norm_transpose(xsb[:, ti, :], xbuf, K_MIX, ti)


            for (dc0, sc0, n) in ((0, C_ALR, 16), (16, C_DT, 32)):
                S.dma("sp", wsm[:, :, dc0:dc0 + n], wbf["in"].rearrange("(kc p) n -> p kc n", p=128)[:, :, sc0:sc0 + n],
                      [wbuf["in"]], [B["wsm"]], "wsm")
            for ti in range(NT):
                pi = psum_f()
                pt, pb = psF[pi]
                for (dc0, n) in ((0, 16), (16, 32)):
                    for k in range(8):
                        mm(pt[:, dc0:dc0 + n], hT[:, k, ti * 128:(ti + 1) * 128], wsm[:, k, dc0:dc0 + n], k == 0, k == 7,
                           [B["hT"], B["wsm"]], [pb], k == 7)
                cp("act", sm_a_t[ti][:], pt[:, 0:48], [pb], [B["sm_a%d" % ti]])
                free_f(pi)

            def gates_chain(ti):
                tg = t0 + ti
                sm_a, SMA = sm_a_t[ti], B["sm_a%d" % ti]
                sm_b, SMB = sm_b_t[ti], B["sm_b%d" % ti]
                alrT, ALRT = alrT_t[ti], B["alrT%d" % ti]
                acs_sb, ACS = acs_t[ti], B["acs%d" % ti]
                l1, L1 = l1_t[ti], l1_b[ti]
                tt("dve", sm_b[:], sm_a[:, 16:48], cF(K_DTB, 32), ALU.add, [SMA, CST], [SMB])
                pi = psum_f()
                pt, pb = psF[pi]
                tr(pt[0:16, 0:128], sm_a[:, 0:16], identF, [SMA, CST], [pb], True)
                cp("act", alrT[:], pt[0:16, 0:128], [pb], [ALRT])
                free_f(pi)
                act(sm_b[:], sm_b[:], AF.Exp, [SMB], [SMB])
                yield
                pi = psum_f()
                pt, pb = psF[pi]
                mm(pt[:, 0:512], alrT[:], cst[0:16, K_WG:K_WG + 512], True, False, [ALRT, CST], [pb], False)
                mm(pt[:, 0:512], onesF[0:1, :], cst[0:1, K_BG:K_BG + 512], False, True, [B["onesF"], CST], [pb], True)
                act(l1[:], pt[:, 0:512], AF.Exp, [pb], [L1], scale=-1.0)
                free_f(pi)
                act(dtv[:, ti, :], sm_b[:], AF.Ln, [SMB], [B["dtv"]], bias=1.0)
                yield
                act(l1[:], l1[:], AF.Ln, [L1], [L1], bias=1.0)
                ts("dve", dtv[:, ti, :], dtv[:, ti, :], cst[:, K_MASK + tg:K_MASK + tg + 1], None, ALU.mult, None, [B["dtv"], CST], [B["dtv"]])
                tt("dve", dav[:, ti, :], dtv[:, ti, :], A_bc[:], ALU.mult, [B["dtv"], B["A_bc"]], [B["dav"]])
                yield
                pi = psum_f()
                pt, pb = psF[pi]
                mm(pt[:, 0:512], triG, l1[:], True, True, [CST, L1], [pb], True)
                cp("act", b_sb[:, ti, :], pt[:, 0:512], [pb], [B["b_sb"]])
                free_f(pi)
                yield
                pi = psum_f()
                pt, pb = psF[pi]
                for h in range(4):
                    mm(pt[:, h:h + 1], l1[:, h * 128:(h + 1) * 128], negc[:, 0:1], True, True, [L1, B["negc"]], [pb], h == 3)
                mm(pt[:, 32:64], triT, dav[:, ti, :], True, True, [CST, B["dav"]], [pb], False)
                mm(pt[:, 64:96], onesF[:], dav[:, ti, :], True, True, [B["onesF"], B["dav"]], [pb], True)
                act(ebl[:, ti, :], pt[:, 0:4], AF.Exp, [pb], [B["ebl"]])
                cp("act", acs_sb[:], pt[:, 32:64], [pb], [ACS])
                act(eacs[:, ti, :], pt[:, 32:64], AF.Exp, [pb], [B["eacs"]])
                act(cdbc[:, ti, :], pt[:, 64:96], AF.Exp, [pb], [B["cdbc"]])
                ts("dve", nacs[:, ti, :], acs_sb[:], -1.0, None, ALU.mult, None, [ACS], [B["nacs"]])
                tt("dve", sm_b[:], pt[:, 64:96], acs_sb[:], ALU.subtract, [pb, ACS], [SMB])
                free_f(pi)
                yield
                act(sm_b[:], sm_b[:], AF.Exp, [SMB], [SMB])
                tt("dve", dtd[:, ti, :], dtv[:, ti, :], sm_b[:], ALU.mult, [B["dtv"], SMB], [B["dtd"]])

            fillers = []
            for hf in range(2):
                fillers.append(lambda hf=hf: tm_group("in", C_V + hf * 512, 512,
                               lambda ti, pt, pb: cp("act", v_sb[:, ti, hf * 512:(hf + 1) * 512], pt[:, 0:512], [pb], [B["v_sb"]])))
            nxt = 6 if full else 5
            for g in range(nxt):
                def ev_x(j, pt, pb, g=g):
                    c = g * 4 + j
                    conv_chunk(pt, pb, TB, 4, halo_x, B["halo_x"], c, K_WC, K_BC, AF.Silu, xbcT[:, c, :], B["xbcT"])
                fillers.append(lambda g=g, ev_x=ev_x: fm_group("in", C_XBC + g * 512, 4, ev_x))
            if full:
                for hf in range(2):
                    fillers.append(lambda hf=hf: tm_group("in", C_G + hf * 512, 512,
                                   lambda ti, pt, pb: act(sg[:, ti, hf * 512:(hf + 1) * 512], pt[:, 0:512], AF.Silu, [pb], [B["sg"]])))
                for qd in range(4):
                    fillers.append(lambda qd=qd: tm_group("in", C_Z + qd * 512, 512,
                                   lambda ti, pt, pb: act(sz[:, ti, qd * 512:(qd + 1) * 512], pt[:, 0:512], AF.Silu, [pb], [B["sz"]])))
                for g in range(4):
                    def ev_g(j, pt, pb, g=g):
                        c = g * 4 + j
                        act(sgab[:, c, :], pt[:, 0:TB], AF.Sigmoid, [pb], [B["sgab"]])
                    fillers.append(lambda g=g, ev_g=ev_g: fm_group("in", C_GA + g * 512, 4, ev_g))

            chains = [gates_chain(ti) for ti in range(NT)]
            while chains:
                for gch in list(chains):
                    try:
                        next(gch)
                    except StopIteration:
                        chains.remove(gch)
                if fillers:
                    fillers.pop(0)()
                if not full:
                    bg_pump(1)
            def ev_q(ti, pt, pb):
                act(tmpA[:, 0:512], b_sb[:, ti, :], AF.Exp, [B["b_sb"]], [B["tmpA"]])
                stt(qt[:, ti, :], pt[:, 0:512], 128.0 ** -0.5, tmpA[:, 0:512], ALU.mult, ALU.mult, [pb, B["tmpA"]], [B["qt"]])

            def ev_k(ti, pt, pb):
                act(tmpB[:, 0:512], b_sb[:, ti, :], AF.Exp, [B["b_sb"]], [B["tmpB"]], scale=-1.0)
                tt("dve", kt[:, ti, :], pt[:, 0:512], tmpB[:, 0:512], ALU.mult, [pb, B["tmpB"]], [B["kt"]])

            tm_group("in", C_K, 512, ev_k)
            if full:
                tm_group("in", C_Q, 512, ev_q)
            for f in fillers:
                f()
                if not full:
                    bg_pump(1)
            conv_flush()
            if mid_hook is not None:
                mid_hook()

            def gla_gen(ti, full=full):
                tsl = slice(ti * 128, (ti + 1) * 128)
                bi2 = psum_b()
                ptb, pbb = psB[bi2]
                for h in range(4):
                    tr(ptb[:, h * 128:(h + 1) * 128], kt[:, ti, h * 128:(h + 1) * 128], identB[:], [B["kt"], B["identB"]], [pbb], h == 3)
                cp("act", kT[:], ptb[:, 0:512].rearrange("p (h t) -> p h t", h=4), [pbb], [B["kT"]])
                free_b(bi2)
                yield
                if full:
                    bi2 = psum_b()
                    ptb, pbb = psB[bi2]
                    for h in range(4):
                        tr(ptb[:, h * 128:(h + 1) * 128], qt[:, ti, h * 128:(h + 1) * 128], identB[:], [B["qt"], B["identB"]], [pbb], h == 3)
                    cp("act", qT[:], ptb[:, 0:512].rearrange("p (h t) -> p h t", h=4), [pbb], [B["qT"]])
                    free_b(bi2)
                    yield
                    pi = psum_f()
                    pt, pb = psF[pi]
                    for h in range(4):
                        mm(pt[:, h * 128:(h + 1) * 128], kT[:, h, :], qT[:, h, :], True, True, [B["kT"], B["qT"]], [pb], h == 3)
                    tt("dve", attm[:], pt[:].rearrange("p (h t) -> p h t", h=4), triT.unsqueeze(1).to_broadcast([128, 4, 128]), ALU.mult,
                       [pb, CST], [B["attm"]])
                    free_f(pi)
                    yield
                    pos = [psum_f(), psum_f()]
                    for h in range(4):
                        pt, pb = psF[pos[h // 2]]
                        o_ap = pt[:, (h % 2) * 256:(h % 2) * 256 + 256]
                        mm(o_ap, attm[:, h, :], v_sb[:, ti, h * 256:(h + 1) * 256], True, False, [B["attm"], B["v_sb"]], [pb], False)
                        mm(o_ap, qT[:, h, :], Sgb[:, h, :], False, True, [B["qT"], B["Sgb"]], [pb], True)
                    for b2 in range(2):
                        pt, pb = psF[pos[b2]]
                        cp("act", tmpA[:, b2 * 512:(b2 + 1) * 512], pt[:, 0:512], [pb], [B["tmpA"]])
                        free_f(pos[b2])
                    yield
                    for h in range(4):
                        o_ap = tmpA[:, h * 256:(h + 1) * 256]
                        ttr(junk[:, 0:256], o_ap, o_ap, st4[:, h:h + 1], [B["tmpA"]], [B["junk"], B["st4"]])
                    rstd_from_ss(st4[:, 0:4], rs4[:, 0:4], 256, 4, B["st4"], B["rs4"])
                    yield
                    for h in range(4):
                        o_ap = tmpA[:, h * 256:(h + 1) * 256]
                        stt(tmpB[:, h * 256:(h + 1) * 256], o_ap, rs4[:, h:h + 1], sg[:, ti, h * 256:(h + 1) * 256], ALU.mult, ALU.mult,
                            [B["tmpA"], B["rs4"], B["sg"]], [B["tmpB"]])
                    yield
                    for hb in range(2):
                        pi = psum_f()
                        pt, pb = psF[pi]
                        for c in range(4):
                            cc = hb * 4 + c
                            tr(pt[:, c * 128:(c + 1) * 128], tmpB[:, cc * 128:(cc + 1) * 128], identF, [B["tmpB"], CST], [pb], c == 3)
                        g = cst[:, K_HN + hb * 4:K_HN + hb * 4 + 4].unsqueeze(2).to_broadcast([128, 4, 128])
                        tt("dve", oaT[:, hb * 4:hb * 4 + 4, tsl], pt[:].rearrange("p (c t) -> p c t", c=4), g, ALU.mult, [pb, CST], [B["oaT"]])
                        free_f(pi)
                    yield
                pds = [psum_f(), psum_f()]
                for h in range(4):
                    pt, pb = psF[pds[h // 2]]
                    mm(pt[:, (h % 2) * 256:(h % 2) * 256 + 256], kt[:, ti, h * 128:(h + 1) * 128], v_sb[:, ti, h * 256:(h + 1) * 256], True, True,
                       [B["kt"], B["v_sb"]], [pb], h % 2 == 1)
                for b2 in range(2):
                    pt, pb = psF[pds[b2]]
                    sl = Sg[:, 2 * b2:2 * b2 + 2, :]
                    tt("dve", sl, pt[:].rearrange("p (h v) -> p h v", h=2), sl, ALU.add, [pb, B["Sg"]], [B["Sg"]])
                    tt("dve", sl, sl, ebl[:, ti, 2 * b2:2 * b2 + 2].unsqueeze(2).to_broadcast([128, 2, 256]), ALU.mult, [B["Sg"], B["ebl"]], [B["Sg"]])
                    free_f(pds[b2])
                cp("pool", Sgb[:], Sg[:], [B["Sg"]], [B["Sgb"]])
                if not full:
                    tt("dve", PDg[:], PDg[:], ebl[:, ti, :], ALU.mult, [B["PDg"], B["ebl"]], [B["PDg"]])

            def ssd_gen(ti, full=full):
                tsl = slice(ti * 128, (ti + 1) * 128)
                for hb in range(2):
                    bi2 = psum_b()
                    ptb, pbb = psB[bi2]
                    for c in range(8):
                        cc = hb * 8 + c
                        tr(ptb[:, c * 128:(c + 1) * 128], xbcT[:, cc, tsl], identB[:], [B["xbcT"], B["identB"]], [pbb], c == 7)
                    cp("act", Xtm[:, hb * 1024:(hb + 1) * 1024], ptb[:, 0:1024], [pbb], [B["Xtm"]])
                    free_b(bi2)
                    yield
                bi2 = psum_b()
                ptb, pbb = psB[bi2]
                for c in range(4):
                    tr(ptb[:, c * 128:(c + 1) * 128], xbcT[:, 16 + c, tsl], identB[:], [B["xbcT"], B["identB"]], [pbb], c == 3)
                cp("act", Btm[:], ptb[:, 0:512], [pbb], [B["Btm"]])
                free_b(bi2)
                X3 = Xtm[:].rearrange("p (h d) -> p h d", d=64)
                tt("pool", Xd[:].rearrange("p (h d) -> p h d", d=64), X3, dtd[:, ti, :].unsqueeze(2).to_broadcast([128, 32, 64]), ALU.mult,
                   [B["Xtm"], B["dtd"]], [B["Xd"]])
                yield
                if full:
                    tt("pool", Xdt[:].rearrange("p (h d) -> p h d", d=64), X3, dtv[:, ti, :].unsqueeze(2).to_broadcast([128, 32, 64]), ALU.mult,
                       [B["Xtm"], B["dtv"]], [B["Xdt"]])
                    pi = psum_f()
                    pt, pb = psF[pi]
                    for g in range(4):
                        mm(pt[:, g * 128:(g + 1) * 128], xbcT[:, 16 + g, tsl], xbcT[:, 20 + g, tsl], True, True, [B["xbcT"]], [pb], g == 3)
                    tt("dve", scm[:], pt[:].rearrange("p (g t) -> p g t", g=4), triT.unsqueeze(1).to_broadcast([128, 4, 128]), ALU.mult,
                       [pb, CST], [B["scm"]])
                    free_f(pi)
                    yield

                    def stage_a(g, hh2):
                        j = g % 3
                        jz = (2 * g + hh2) % 2
                        h0 = 8 * g + 4 * hh2
                        tt("pool", Zg[jz][:], triT.unsqueeze(1).to_broadcast([128, 4, 128]),
                           dav[:, ti, h0:h0 + 4].unsqueeze(2).to_broadcast([128, 4, 128]), ALU.mult, [CST, B["dav"]], [B["Zg%d" % jz]])
                        pi = psum_f()
                        pt, pb = psF[pi]
                        mm(pt[:, 0:512], onesF[:], Zg[jz][:].rearrange("p h t -> p (h t)"), True, True,
                           [B["onesF"], B["Zg%d" % jz]], [pb], True)
                        for h4 in range(4):
                            hd = h0 + h4
                            act(Eg[jz][:, h4, :], pt[:, h4 * 128:(h4 + 1) * 128], AF.Relu, [pb, B["acs%d" % ti]], [B["Eg%d" % jz]],
                                bias=acs_t[ti][:, hd:hd + 1], scale=-1.0)
                        free_f(pi)
                        act(Eg[jz][:], Eg[jz][:], AF.Exp, [B["Eg%d" % jz]], [B["Eg%d" % jz]], scale=-1.0)
                        tt("dve", MTg[j][:, 4 * hh2:4 * hh2 + 4, :], Eg[jz][:], scm[:, g, :].unsqueeze(1).to_broadcast([128, 4, 128]), ALU.mult,
                           [B["Eg%d" % jz], B["scm"]], [B["MTg%d" % j]])

                    def stage_b(g):
                        j = g % 3
                        pyd = psum_f()
                        pt, pb = psF[pyd]
                        for hh in range(8):
                            hd = 8 * g + hh
                            mm(pt[:, hh * 64:(hh + 1) * 64], MTg[j][:, hh, :], Xdt[:, hd * 64:(hd + 1) * 64], True, True,
                               [B["MTg%d" % j], B["Xdt"]], [pb], hh == 7)
                        pyo = psum_f()
                        pt2, pb2 = psF[pyo]
                        mm(pt2[:, 0:512], xbcT[:, 20 + g, tsl], Hsb[:, g * 512:(g + 1) * 512], True, True, [B["xbcT"], B["Hsb"]], [pb2], True)
                        e_bc = eacs[:, ti, 8 * g:8 * g + 8].unsqueeze(2).to_broadcast([128, 8, 64])
                        tt("dve", tmpC[:].rearrange("p (h d) -> p h d", d=64), pt2[:].rearrange("p (h d) -> p h d", d=64), e_bc, ALU.mult,
                           [pb2, B["eacs"]], [B["tmpC"]])
                        tt("dve", tmpC[:], pt[:, 0:512], tmpC[:], ALU.add, [pb, B["tmpC"]], [B["tmpC"]])
                        free_f(pyd)
                        free_f(pyo)
                        d_bc = cst[:, K_DSK + 8 * g:K_DSK + 8 * g + 8].unsqueeze(2).to_broadcast([128, 8, 64])
                        tt("pool", tmpD[:].rearrange("p (h d) -> p h d", d=64), Xtm[:, g * 512:(g + 1) * 512].rearrange("p (h d) -> p h d", d=64),
                           d_bc, ALU.mult, [B["Xtm"], CST], [B["tmpD"]])
                        tt("dve", tmpC[:], tmpC[:], tmpD[:], ALU.add, [B["tmpC"], B["tmpD"]], [B["tmpC"]])
                        tt("dve", yz[:, g * 512:(g + 1) * 512], tmpC[:], sz[:, ti, g * 512:(g + 1) * 512], ALU.mult, [B["tmpC"], B["sz"]], [B["yz"]])
                        ttr(junk2[:, 0:512], yz[:, g * 512:(g + 1) * 512], yz[:, g * 512:(g + 1) * 512], st4s[:, g:g + 1],
                            [B["yz"]], [B["junk2"], B["st4s"]])

                    order = [("a", 0), ("a", 1), ("a", 2), ("b", 0), ("a", 3), ("b", 1), ("b", 2), ("b", 3)]
                    for kind, g in order:
                        if kind == "a":
                            stage_a(g, 0)
                            yield
                            stage_a(g, 1)
                            yield
                        else:
                            stage_b(g)
                            yield
                    rstd_from_ss(st4s[:, 0:4], rs4s[:, 0:4], 512, 4, B["st4s"], B["rs4s"])
                    tt("dve", yz[:].rearrange("p (g c) -> p g c", g=4), yz[:].rearrange("p (g c) -> p g c", g=4),
                       rs4s[:, 0:4].unsqueeze(2).to_broadcast([128, 4, 512]), ALU.mult, [B["yz"], B["rs4s"]], [B["yz"]])
                    yield
                    for hb in range(4):
                        pi = psum_f()
                        pt, pb = psF[pi]
                        for c in range(4):
                            cc = hb * 4 + c
                            tr(pt[:, c * 128:(c + 1) * 128], yz[:, cc * 128:(cc + 1) * 128], identF, [B["yz"], CST], [pb], c == 3)
                        g = cst[:, K_SSM + hb * 4:K_SSM + hb * 4 + 4].unsqueeze(2).to_broadcast([128, 4, 128])
                        tt("dve", ynT[:, hb * 4:hb * 4 + 4, tsl], pt[:].rearrange("p (c t) -> p c t", c=4), g, ALU.mult, [pb, CST], [B["ynT"]])
                        free_f(pi)
                        yield
                for g in range(4):
                    pi = psum_f()
                    pt, pb = psF[pi]
                    mm(pt[:, 0:512], Btm[:, g * 128:(g + 1) * 128], Xd[:, g * 512:(g + 1) * 512], True, True, [B["Btm"], B["Xd"]], [pb], True)
                    hsl = Hs[:, g * 512:(g + 1) * 512]
                    c_bc = cdbc[:, ti, 8 * g:8 * g + 8].unsqueeze(2).to_broadcast([128, 8, 64])
                    tt("dve", hsl.rearrange("p (h d) -> p h d", d=64), hsl.rearrange("p (h d) -> p h d", d=64), c_bc, ALU.mult,
                       [B["Hs"], B["cdbc"]], [B["Hs"]])
                    tt("dve", hsl, hsl, pt[:, 0:512], ALU.add, [B["Hs"], pb], [B["Hs"]])
                    free_f(pi)
                    yield
                cp("pool", Hsb[:], Hs[:], [B["Hs"]], [B["Hsb"]])
                if not full:
                    tt("dve", PDs[:], PDs[:], cdbc[:, ti, :], ALU.mult, [B["PDs"], B["cdbc"]], [B["PDs"]])

            for ti in range(NT):
                ft = full and not (halo_only and ti < NT - 1)
                gens = [[ssd_gen(ti, ft), 0.0, 1.0], [gla_gen(ti, ft), 0.5, 3.0]]
                while gens:
                    gens.sort(key=lambda e: e[1])
                    ent = gens[0]
                    try:
                        next(ent[0])
                        ent[1] += ent[2]
                    except StopIteration:
                        gens.remove(ent)
                    if not full:
                        bg_pump(1)

            if not full:
                return

            for nh in range(2):
                wa_t, wa_b = wload("a", 0, 8, nh * 512, 512)
                wb0_t, wb0_b = wload("b", 0, 8, nh * 512, 512)
                wb1_t, wb1_b = wload("b", 8, 8, nh * 512, 512)
                for j in range(4):
                    n = nh * 4 + j
                    pa = psum_f()
                    pta, pba = psF[pa]
                    for k in range(8):
                        mm(pta[:, c0h:TB], wa_t[:, k, j * 128:(j + 1) * 128], oaT[:, k, c0h:TB], k == 0, k == 7, [wa_b, B["oaT"]], [pba], k == 7)
                    pbk = psum_f()
                    ptb_, pbb_ = psF[pbk]
                    for k in range(16):
                        wt_, wb_ = (wb0_t, wb0_b) if k < 8 else (wb1_t, wb1_b)
                        mm(ptb_[:, c0h:TB], wt_[:, k % 8, j * 128:(j + 1) * 128], ynT[:, k, c0h:TB], k == 0, k == 15, [wb_, B["ynT"]], [pbb_], k % 8 == 7)
                    tt("dve", tmpC[:, c0h:TB], pta[:, c0h:TB], sgab[:, n, c0h:TB], ALU.mult, [pba, B["sgab"]], [B["tmpC"]])
                    tt("dve", tmpD[:, c0h:TB], ptb_[:, c0h:TB], sgab[:, 8 + n, c0h:TB], ALU.mult, [pbb_, B["sgab"]], [B["tmpD"]])
                    tt("pool", mT[:, n, c0h:TB], tmpC[:, c0h:TB], tmpD[:, c0h:TB], ALU.add, [B["tmpC"], B["tmpD"]], [B["mT"]])
                    free_f(pa)
                    free_f(pbk)
            for hf in range(2):
                wt, bw = wload("out", 0, 8, hf * 512, 512)
                for ti in range(NT - 1 if halo_only else 0, NT):
                    pi = psum_f()
                    pt, pb = psF[pi]
                    for k in range(8):
                        mm(pt[:, 0:512], mT[:, k, ti * 128:(ti + 1) * 128], wt[:, k, :], k == 0, k == 7, [B["mT"], bw], [pb], k == 7)
                    xsl = xsb[:, ti, hf * 512:(hf + 1) * 512]
                    tt("dve", xsl, xsl, pt[:, 0:512], ALU.add, [xbuf, pb], [xbuf])
                    free_f(pi)

            for ti in range(NT - 1 if halo_only else 0, NT):
                tg = t0 + ti
                norm_transpose(xsb[:, ti, :], xbuf, K_FFN, ti, mask_col=cst[:, K_MASK + tg:K_MASK + tg + 1])
            gT = xbcT
            GT = B["xbcT"]
            fchunks = [(0, 4), (4, 4), (8, 4), (12, 4), (16, 4), (20, 2)]
            if halo_only:
                for (f0, nf) in fchunks:
                    wa_t, wa_b = wload("up", 0, 8, f0 * 128, nf * 128)
                    for j in range(nf):
                        f = f0 + j
                        pa = psum_f()
                        pta, pba = psF[pa]
                        for k in range(8):
                            mm(pta[:, 0:128], wa_t[:, k, j * 128:(j + 1) * 128], hT[:, k, TB - 128:TB], k == 0, k == 7, [wa_b, B["hT"]], [pba], k == 7)
                        cp("act", halo_f[:, f, :], pta[:, 126:128], [pba], [B["halo_f"]])
                        free_f(pa)
                return
            for (f0, nf) in fchunks:
                wa_t, wa_b = wload("up", 0, 8, f0 * 128, nf * 128)
                wl_t, wl_b = wload("up", 0, 8, FFN_H + f0 * 128, nf * 128)
                for j in range(nf):
                    f = f0 + j
                    pa = psum_f()
                    pta, pba = psF[pa]
                    for k in range(8):
                        mm(pta[:, 0:TB], wa_t[:, k, j * 128:(j + 1) * 128], hT[:, k, :], k == 0, k == 7, [wa_b, B["hT"]], [pba], k == 7)
                    pl = psum_f()
                    ptl, pbl = psF[pl]
                    for k in range(8):
                        mm(ptl[:, 0:TB], wl_t[:, k, j * 128:(j + 1) * 128], hT[:, k, :], k == 0, k == 7, [wl_b, B["hT"]], [pbl], k == 7)
                    def post(f=f, ptl=ptl, pbl=pbl, pl=pl):
                        tt("dve", gT[:, f, :], tmpC[:, 0:TB], ptl[:, 0:TB], ALU.mult, [B["tmpC"], pbl], [GT])
                        free_f(pl)
                    conv_chunk(pta, pba, TB, 3, halo_f, B["halo_f"], f, K_WFC, K_BFC, AF.Gelu_apprx_tanh, tmpC[:, 0:TB], B["tmpC"], post=post)
                    free_f(pa)
            conv_flush()
            kparts = [(0, 8), (8, 8), (16, 6)]
            for hf in range(2):
                pis = [psum_f() for _ in range(NT)]
                for kp, (k0, nk) in enumerate(kparts):
                    wt, bw = wload("down", k0, nk, hf * 512, 512)
                    for ti in range(NT):
                        pt, pb = psF[pis[ti]]
                        for k in range(nk):
                            mm(pt[:, 0:512], gT[:, k0 + k, ti * 128:(ti + 1) * 128], wt[:, k, :], kp == 0 and k == 0, kp == 2 and k == nk - 1,
                               [GT, bw], [pb], k == nk - 1)
                for ti in range(NT):
                    pt, pb = psF[pis[ti]]
                    xsl = xsb[:, ti, hf * 512:(hf + 1) * 512]
                    tt("dve", xsl, xsl, pt[:, 0:512], ALU.add, [xbuf, pb], [xbuf])
                    free_f(pis[ti])

            fb = bi
            for ti in range(NT):
                norm_transpose(xsb[:, ti, :], xbuf, K_PLE, ti)
                S.dma("sp", p_in[:], pw[(fb * NT + ti) * 128:(fb * NT + ti + 1) * 128, :], [], [B["p_in"]], "p_in")
                pi = psum_f()
                pt, pb = psF[pi]
                for c in range(2):
                    tr(pt[:, c * 128:(c + 1) * 128], p_in[:, c * 128:(c + 1) * 128], identF, [B["p_in"], CST], [pb], c == 1)
                cp("act", pT[:, :, ti * 128:(ti + 1) * 128], pt[:, 0:256].rearrange("p (c t) -> p c t", c=2), [pb], [B["pT"]])
                free_f(pi)
            wpp_t, wpp_b = wload("pp", 0, 2, 0, 1024)
            for hf in range(2):
                wt, bw = wload("pg", 0, 8, hf * 512, 512)
                for ti in range(NT):
                    pi = psum_f()
                    pt, pb = psF[pi]
                    for k in range(8):
                        mm(pt[:, 0:512], hT[:, k, ti * 128:(ti + 1) * 128], wt[:, k, :], k == 0, k == 7, [B["hT"], bw], [pb], k == 7)
                    act(tmpA[:, 0:512], pt[:, 0:512], AF.Sigmoid, [pb], [B["tmpA"]])
                    free_f(pi)
                    pi = psum_f()
                    pt, pb = psF[pi]
                    for k in range(2):
                        mm(pt[:, 0:512], pT[:, k, ti * 128:(ti + 1) * 128], wpp_t[:, k, hf * 512:(hf + 1) * 512], k == 0, k == 1,
                           [B["pT"], wpp_b], [pb], k == 1)
                    tt("dve", tmpB[:, 0:512], tmpA[:, 0:512], pt[:, 0:512], ALU.mult, [B["tmpA"], pb], [B["tmpB"]])
                    free_f(pi)
                    xsl = xsb[:, ti, hf * 512:(hf + 1) * 512]
                    tt("dve", xsl, xsl, tmpB[:, 0:512], ALU.add, [xbuf, B["tmpB"]], [xbuf])

            for ti in range(NT):
                tg = t0 + ti
                ot = tg - (NTILES - n_out_tiles)
                if ot < 0:
                    continue
                xsrc = xsb[:, ti, :]
                ttr(junk[:], xsrc, xsrc, st4[:, 0:1], [xbuf], [B["junk"], B["st4"]])
                rstd_from_ss(st4[:, 0:1], rs4[:, 0:1], D, 1, B["st4"], B["rs4"])
                ob = out_sb[0]
                obuf = B["out_sb0"]
                stt(ob[:], xsrc, rs4[:, 0:1], cF(K_FIN, 1024), ALU.mult, ALU.mult, [xbuf, B["rs4"], CST], [obuf])
                S.dma("pool", out_d[ot * 128:(ot + 1) * 128, :], ob[:], [obuf], [], "out_sb0")

        S.dma("sp", xs[0][:, 0, :], xh, [], [B["xs0"]], "xs0")
        norm_transpose(xs[0][:, 0, :], B["xs0"], K_MIX, 0)
        for g in range(6):
            wt, bw = wload("in", 0, 8, C_XBC + g * 512, 512)
            for j in range(4):
                c = g * 4 + j
                pi = psum_f()
                pt, pb = psF[pi]
                for k in range(8):
                    mm(pt[:, 0:128], wt[:, k, j * 128:(j + 1) * 128], hT[:, k, 0:128], k == 0, k == 7, [B["hT"], bw], [pb], k == 7)
                cp("act", halo0[:, c, :], pt[:, 125:128], [pb], [B["halo0"]])
                free_f(pi)

        S.op("pool", lambda h: h.memset(PDg[:], 1.0), [], [B["PDg"]])
        S.op("pool", lambda h: h.memset(PDs[:], 1.0), [], [B["PDs"]])
        cp("pool", halo_x[:], halo0[:], [B["halo0"]], [B["halo_x"]])
        for bi in range(n_sblk):
            do_block(bi, False)

        while not bg_state["done"]:
            bg_pump(1)
        XIN, XOUT = Buf("xch_in"), Buf("xch_out")
        S.dma("sp", xch_in[:, 0:1024], Sg[:].rearrange("p h v -> p (h v)"), [B["Sg"]], [XIN], "xch")
        S.dma("sp", xch_in[:, 1024:3072], Hs[:], [B["Hs"]], [XIN], "xch")
        S.dma("sp", xch_in[:, 3072:3076], PDg[:], [B["PDg"]], [XIN], "xch")
        S.dma("sp", xch_in[:, 3076:3108], PDs[:], [B["PDs"]], [XIN], "xch")
        S.op("pool", lambda h: h.memset(dgm[:], 0.0), [], [B["dgm"]])
        S.dma("sp", xch_in[:, 3108:3136], dgm[:, 0:28], [B["dgm"]], [XIN], "xch")
        cc_sem = es.enter_context(nc.semaphore("cc_sem"))
        pe_ = S.eng["pool"]
        S._deps(pe_, [XIN], [XOUT])
        rg = [list(range(ncores))]
        pe_.prog.append(lambda h: h.collective_compute("AllGather", ALU.bypass, replica_groups=rg,
                                                       ins=[xch_in], outs=[xch_out]).then_inc(cc_sem))
        cc_tok = Tok(cc_sem, 1)
        S._commit(cc_tok, [XIN], [XOUT])

        def combine():
            S.op("pool", lambda h: h.memset(Sg[:], 0.0), [B["PDg"]], [B["Sg"]])
            S.op("pool", lambda h: h.memset(Hs[:], 0.0), [B["PDs"]], [B["Hs"]])
            for i in range(ncores):
                m_ap = cst[:, K_MASK + 32 + i:K_MASK + 33 + i]
                r0 = i * 128
                S.dma("sp", yz[:, 0:2048], xch_out[r0:r0 + 128, 1024:3072], [XOUT], [B["yz"]], "cmbH")
                S.dma("sp", tmpA[:, 0:1024], xch_out[r0:r0 + 128, 0:1024], [XOUT], [B["tmpA"]], "cmbS")
                S.dma("sp", dst_[:], xch_out[r0:r0 + 128, 3072:3108], [XOUT], [B["dst_"]], "dst_")
                ts("dve", dgm[:], dst_[:], -1.0, m_ap, ALU.add, ALU.mult, [B["dst_"], CST], [B["dgm"]])
                ts("dve", dgm[:], dgm[:], 1.0, None, ALU.add, None, [B["dgm"]], [B["dgm"]])
                tt("dve", Sg[:], Sg[:], dgm[:, 0:4].unsqueeze(2).to_broadcast([128, 4, 256]), ALU.mult, [B["Sg"], B["dgm"]], [B["Sg"]])
                stt(Sg[:].rearrange("p h v -> p (h v)"), tmpA[:, 0:1024], m_ap, Sg[:].rearrange("p h v -> p (h v)"), ALU.mult, ALU.add,
                    [B["tmpA"], CST, B["Sg"]], [B["Sg"]])
                H3 = Hs[:].rearrange("p (h d) -> p h d", d=64)
                tt("dve", H3, H3, dgm[:, 4:36].unsqueeze(2).to_broadcast([128, 32, 64]), ALU.mult, [B["Hs"], B["dgm"]], [B["Hs"]])
                stt(Hs[:], yz[:, 0:2048], m_ap, Hs[:], ALU.mult, ALU.add, [B["yz"], CST, B["Hs"]], [B["Hs"]])
            cp("pool", Sgb[:], Sg[:], [B["Sg"]], [B["Sgb"]])
            cp("pool", Hsb[:], Hs[:], [B["Hs"]], [B["Hsb"]])
        cp("pool", halo_x[:], halo0[:], [B["halo0"]], [B["halo_x"]])

        for bi in range(n_fblk):
            do_block(bi, True, mid_hook=combine if bi == 0 else None)

        if debug is not None:
            debug_fn = build_program.debug_fn
            debug_fn(S, B, locals(), dbg_d)

        for key, ent in S.dma_sems.items():
            if key.startswith("out_sb") or key == "dbg":
                S.wait_tok("pool", Tok(ent[0], ent[1]))

        block = es.enter_context(nc.Block())
        S.emit(block)
    return nc


build_program.debug_fn = None


def make_consts(inp, mask, ntiles_cols=64):
    c = np.zeros((128, 2848), np.float32)
    c[:, K_ID:K_ID + 128] = np.eye(128, dtype=np.float32)
    tri = np.triu(np.ones((128, 128), np.float32))
    c[:, K_TRI:K_TRI + 128] = tri
    c[:, K_TRIG:K_TRIG + 128] = tri * np.float32(-1.0 / 16.0)
    c[0:16, K_WG:K_WG + 512] = inp["w_gla_gate"][0]
    c[0:1, K_BG:K_BG + 512] = inp["b_gla_gate"][0][None, :]
    c[:, K_FIN:K_FIN + 1024] = np.broadcast_to(inp["final_norm"][None, :], (128, 1024))
    colv = lambda v: np.ascontiguousarray(v.reshape(-1, 128).T)
    c[:, K_MIX:K_MIX + 8] = colv(inp["mixer_norm"][0])
    c[:, K_FFN:K_FFN + 8] = colv(inp["ffn_norm"][0])
    c[:, K_PLE:K_PLE + 8] = colv(inp["ple_norm"][0])
    c[:, K_HN:K_HN + 8] = np.tile(colv(inp["gla_norm"][0]), (1, 4))
    c[:, K_SSM:K_SSM + 16] = colv(inp["ssm_norm"][0])
    wc = inp["w_ssm_conv"][0]
    c[:, K_WC:K_WC + 96] = wc.T.reshape(24, 128, 4).transpose(1, 0, 2).reshape(128, 96)
    c[:, K_BC:K_BC + 24] = colv(inp["b_ssm_conv"][0])
    wf = inp["w_ffn_conv"][0]
    c[:, K_WFC:K_WFC + 66] = wf.T.reshape(22, 128, 3).transpose(1, 0, 2).reshape(128, 66)
    c[:, K_BFC:K_BFC + 22] = colv(inp["b_ffn_conv"][0])
    c[:, K_DTB:K_DTB + 32] = np.broadcast_to(inp["dt_bias"][0][None, :], (128, 32))
    c[:, K_ALOG:K_ALOG + 32] = np.broadcast_to(inp["a_log"][0][None, :], (128, 32))
    c[:, K_DSK:K_DSK + 32] = np.broadcast_to(inp["d_skip"][0][None, :], (128, 32))
    nt = mask.shape[0] // 128
    c[:, K_MASK:K_MASK + nt] = mask.reshape(nt, 128).T
    return c


NT_ = 2
N_FBLK = 9
N_SBLK = 8


def core_inputs(inp, b, j, ncores_mask, core_idx, n_fblk=N_FBLK, NT=NT_):
    x = inp["x"]
    p = inp["p"][0]
    WT = n_fblk * NT * 128
    end = (j + 1) * OWN
    start = end - WT
    xwin = np.zeros((WT, D), np.float32)
    mask = np.zeros((WT,), np.float32)
    s0 = max(start, 0)
    xwin[s0 - start:] = x[b, s0:end]
    mask[s0 - start:] = 1.0
    xhalo = np.zeros((128, D), np.float32)
    if start >= 128:
        xhalo[:] = x[b, start - 128:start]
    pwin = np.zeros((WT, 256), np.float32)
    pwin[s0 - start:] = p[b, s0:end]
    c = make_consts(inp, mask)
    c[:, K_MASK + 32:K_MASK + 32 + len(ncores_mask)] = np.asarray(ncores_mask, np.float32)[None, :]
    return {"xw": xwin, "xh": xhalo, "pw": pwin, "consts": c}


def kernel(**inp):
    inp = {k: np.asarray(v) for k, v in inp.items()}
    nc = build_program(N_SBLK, N_FBLK, OWN // 128, NT=NT_)
    shared = {
        "w_in": np.ascontiguousarray(inp["w_in"][0]),
        "w_branch_a": np.ascontiguousarray(inp["w_branch_a"][0]),
        "w_branch_b": np.ascontiguousarray(inp["w_branch_b"][0]),
        "w_out": np.ascontiguousarray(inp["w_out"][0]),
        "w_ffn_up": np.ascontiguousarray(inp["w_ffn_up"][0]),
        "w_ffn_down": np.ascontiguousarray(inp["w_ffn_down"][0]),
        "w_ple_gate": np.ascontiguousarray(inp["w_ple_gate"][0]),
        "w_ple_proj": np.ascontiguousarray(inp["w_ple_proj"][0]),
    }
    in_maps = []
    for core in range(NCORES):
        b, j = core // 4, core % 4
        cm = [1.0 if (i // 4 == b and i % 4 < j) else 0.0 for i in range(NCORES)]
        m = dict(shared)
        m.update(core_inputs(inp, b, j, cm, core))
        in_maps.append(m)
    res = run_bass_kernel_spmd(nc, in_maps, core_ids=list(range(NCORES)))
    out = np.zeros((BATCH, SEQ, D), np.float32)
    for core in range(NCORES):
        b, j = core // 4, core % 4
        out[b, j * OWN:(j + 1) * OWN] = np.asarray(res.results[core]["out"]).reshape(OWN, D)
    return out
```
